# Optimizing a Trainium2 kernel written in Bass

```python
import math
import jax
import jax.numpy as jnp
from jax import lax
import numpy as np

D_MODEL = 1024
BATCH = 8
SEQ = 4096
DEPTH = 4
DEC_BATCH = 8
DEC_SEQ = 8192
PAST_LEN = 128

N_MIXERS = 2
N_SSD_LAYERS = (DEPTH + 1) // 2
N_ATTN_LAYERS = DEPTH // 2
EPS = 1e-6
D_FF = 2816
D_INNER = 2 * D_MODEL
SSD_HEAD_DIM = 64
SSD_HEADS = D_INNER // SSD_HEAD_DIM
N_GROUPS = 8
D_STATE = 128
CHUNK = 128
D_CONV = 5
CONV_PAD = D_CONV // 2
GN = N_GROUPS * D_STATE
CONV_DIM = D_INNER + 2 * GN
D_IN_PROJ = D_INNER + CONV_DIM + 2 * SSD_HEADS
HEAD_DIM = 64
N_HEADS = D_MODEL // HEAD_DIM
N_KV_HEADS = 4
GQA_GROUP = N_HEADS // N_KV_HEADS
Q_DIM = N_HEADS * HEAD_DIM
KV_DIM = N_KV_HEADS * HEAD_DIM
QKV_DIM = Q_DIM + 2 * KV_DIM
WINDOW = 128
BLOCK = 128
N_BUCKETS = 32
MAX_DISTANCE = 128

kernel_name = "hybrid_bissd_swa_macaron_encoder"


def rmsnorm(x, g):
    xf = x.astype(jnp.float32)
    y = xf * lax.rsqrt(jnp.mean(xf * xf, axis=-1, keepdims=True) + EPS)
    return (y * g.astype(jnp.float32)).astype(x.dtype)


def swiglu(h, w_gate, w_up, w_down):
    return (jax.nn.silu(h @ w_gate) * (h @ w_up)) @ w_down


def t5_bucket(rel):
    half = N_BUCKETS // 2
    max_exact = half // 2
    ret = jnp.where(rel > 0, half, 0)
    n = jnp.abs(rel)
    large = max_exact + (jnp.log(jnp.maximum(n, 1).astype(jnp.float32) / max_exact)
                         / math.log(MAX_DISTANCE / max_exact) * (half - max_exact)).astype(jnp.int32)
    large = jnp.minimum(large, half - 1)
    return ret + jnp.where(n < max_exact, n, large)


def ssd_chunked_scan(x, dt, A, B, C):
    bsz, s, h, p = x.shape
    g, n = B.shape[-2:]
    r = h // g
    c = s // CHUNK
    xd = (x * dt[..., None]).reshape(bsz, c, CHUNK, g, r, p)
    a = jnp.moveaxis((dt * A).reshape(bsz, c, CHUNK, g, r), 2, -1)
    a_cs = jnp.cumsum(a, axis=-1)
    tril = jnp.tril(jnp.ones((CHUNK, CHUNK), dtype=bool))
    seg = jnp.exp(jnp.where(tril, a_cs[..., :, None] - a_cs[..., None, :], -jnp.inf))
    Bc = B.reshape(bsz, c, CHUNK, g, n)
    Cc = C.reshape(bsz, c, CHUNK, g, n)
    cb = jnp.einsum('bclgn,bcsgn->bcgls', Cc, Bc)
    y_diag = jnp.einsum('bcgls,bcgrls,bcsgrp->bclgrp', cb, seg, xd)
    decay_to_end = jnp.exp(a_cs[..., -1:] - a_cs)
    chunk_states = jnp.einsum('bclgn,bcgrl,bclgrp->bcgrpn', Bc, decay_to_end, xd)
    chunk_decay = jnp.exp(a_cs[..., -1])

    def step(state, inp):
        st, dec = inp
        return state * dec[..., None, None] + st, state

    init = jnp.zeros((bsz, g, r, p, n), jnp.float32)
    _, prev = lax.scan(step, init, (jnp.moveaxis(chunk_states, 1, 0), jnp.moveaxis(chunk_decay, 1, 0)))
    prev = jnp.moveaxis(prev, 0, 1)
    y_off = jnp.einsum('bclgn,bcgrpn,bcgrl->bclgrp', Cc, prev, jnp.exp(a_cs))
    return (y_diag + y_off).reshape(bsz, s, h, p)


def ssd_mixer(h, w_in, conv_w, conv_b, dt_bias, A_log, D, norm_g, w_out):
    bsz, s, _ = h.shape
    proj = h @ w_in
    z = proj[..., :D_INNER]
    xbc = proj[..., D_INNER:D_INNER + CONV_DIM]
    dt_raw = proj[..., D_INNER + CONV_DIM:].astype(jnp.float32)
    xbc = lax.conv_general_dilated(xbc, conv_w[:, None, :], window_strides=(1,),
                                   padding=[(CONV_PAD, CONV_PAD)],
                                   dimension_numbers=('NWC', 'WIO', 'NWC'),
                                   feature_group_count=CONV_DIM)
    xbc = jax.nn.silu(xbc + conv_b).astype(jnp.float32)
    xs = xbc[..., :D_INNER].reshape(bsz, s, SSD_HEADS, SSD_HEAD_DIM)
    Bm = xbc[..., D_INNER:D_INNER + GN].reshape(bsz, s, N_GROUPS, D_STATE)
    Cm = xbc[..., D_INNER + GN:].reshape(bsz, s, N_GROUPS, D_STATE)
    dt = jax.nn.softplus(dt_raw.reshape(bsz, s, 2, SSD_HEADS) + dt_bias.astype(jnp.float32))
    A = -jnp.exp(A_log.astype(jnp.float32))
    flip = lambda t: jnp.flip(t, axis=1)
    y_fwd = ssd_chunked_scan(xs, dt[:, :, 0], A[0], Bm, Cm)
    y_bwd = flip(ssd_chunked_scan(flip(xs), flip(dt[:, :, 1]), A[1], flip(Bm), flip(Cm)))
    y = y_fwd + y_bwd + xs * D.astype(jnp.float32)[:, None]
    y = y.reshape(bsz, s, D_INNER) * jax.nn.silu(z.astype(jnp.float32))
    yg = y.reshape(bsz, s, N_GROUPS, D_INNER // N_GROUPS)
    yg = yg * lax.rsqrt(jnp.mean(yg * yg, axis=-1, keepdims=True) + EPS)
    y = (yg.reshape(bsz, s, D_INNER) * norm_g.astype(jnp.float32)).astype(h.dtype)
    return y @ w_out


def window_attention(h, w_qkv, sink, w_out, rel_bias):
    bsz, s, _ = h.shape
    nb = s // BLOCK
    qkv = h @ w_qkv
    q = qkv[..., :Q_DIM].reshape(bsz, nb, BLOCK, N_KV_HEADS, GQA_GROUP, HEAD_DIM)
    k = qkv[..., Q_DIM:Q_DIM + KV_DIM].reshape(bsz, s, N_KV_HEADS, HEAD_DIM)
    v = qkv[..., Q_DIM + KV_DIM:].reshape(bsz, s, N_KV_HEADS, HEAD_DIM)

    def band(t):
        tp = jnp.pad(t, ((0, 0), (BLOCK, BLOCK), (0, 0), (0, 0)))
        tp = tp.reshape(bsz, nb + 2, BLOCK, N_KV_HEADS, HEAD_DIM)
        return jnp.concatenate([tp[:, :-2], tp[:, 1:-1], tp[:, 2:]], axis=2)

    kb, vb = band(k), band(v)
    scores = jnp.einsum('bnqkgd,bnskd->bnkgqs', q, kb,
                        preferred_element_type=jnp.float32) * (HEAD_DIM ** -0.5)
    qi = jnp.arange(BLOCK)[:, None]
    kj = jnp.arange(3 * BLOCK)[None, :]
    rel = kj - BLOCK - qi
    bias = rel_bias[t5_bucket(rel)].astype(jnp.float32)
    bias = jnp.transpose(bias, (2, 0, 1)).reshape(N_KV_HEADS, GQA_GROUP, BLOCK, 3 * BLOCK)
    kpos = (jnp.arange(nb)[:, None] - 1) * BLOCK + kj
    mask = (jnp.abs(rel) <= WINDOW)[None] & ((kpos >= 0) & (kpos < s))[:, None, :]
    scores = jnp.where(mask[None, :, None, None], scores + bias, -jnp.inf)
    sink_b = sink.astype(jnp.float32).reshape(N_KV_HEADS, GQA_GROUP, 1, 1)
    m = jnp.maximum(jnp.max(scores, axis=-1, keepdims=True), sink_b)
    e = jnp.exp(scores - m)
    probs = e / (jnp.sum(e, axis=-1, keepdims=True) + jnp.exp(sink_b - m))
    o = jnp.einsum('bnkgqs,bnskd->bnqkgd', probs.astype(vb.dtype), vb).reshape(bsz, s, Q_DIM)
    return o @ w_out


def encoder(x, norm_g, ffn_w_gate, ffn_w_up, ffn_w_down, ssd_w_in, ssd_conv_w, ssd_conv_b,
            ssd_dt_bias, ssd_A_log, ssd_D, ssd_norm_g, ssd_w_out, attn_w_qkv, attn_sink,
            attn_w_out, rel_bias):
    for i in range(DEPTH):
        ng = norm_g[i]
        x = x + 0.5 * rmsnorm(swiglu(rmsnorm(x, ng[0]), ffn_w_gate[i, 0], ffn_w_up[i, 0],
                                     ffn_w_down[i, 0]), ng[1])
        h = rmsnorm(x, ng[2])
        j = i // N_MIXERS
        if i % N_MIXERS == 0:
            mix = ssd_mixer(h, ssd_w_in[j], ssd_conv_w[j], ssd_conv_b[j], ssd_dt_bias[j],
                            ssd_A_log[j], ssd_D[j], ssd_norm_g[j], ssd_w_out[j])
        else:
            mix = window_attention(h, attn_w_qkv[j], attn_sink[j], attn_w_out[j], rel_bias)
        x = x + rmsnorm(mix, ng[3])
        x = x + 0.5 * rmsnorm(swiglu(rmsnorm(x, ng[4]), ffn_w_gate[i, 1], ffn_w_up[i, 1],
                                     ffn_w_down[i, 1]), ng[5])
    return x


def setup_inputs(seed: int = 0) -> dict:
    key = jax.random.key(seed)
    ks = jax.random.split(key, 20)
    f32 = jnp.float32

    def nrm(k, shape, fan):
        return jax.random.normal(k, shape, f32) * (fan ** -0.5)

    dt0 = jnp.exp(jax.random.uniform(ks[9], (N_SSD_LAYERS, 2, SSD_HEADS), f32,
                                     math.log(1e-3), math.log(1e-1)))
    return {
        "x_prompt": jax.random.normal(ks[0], (BATCH, SEQ, D_MODEL), f32),
        "x_sample": jax.random.normal(ks[1], (DEC_BATCH, DEC_SEQ, D_MODEL), f32),
        "norm_g": 1.0 + 0.05 * jax.random.normal(ks[2], (DEPTH, 6, D_MODEL), f32),
        "ffn_w_gate": nrm(ks[3], (DEPTH, 2, D_MODEL, D_FF), D_MODEL),
        "ffn_w_up": nrm(ks[4], (DEPTH, 2, D_MODEL, D_FF), D_MODEL),
        "ffn_w_down": nrm(ks[5], (DEPTH, 2, D_FF, D_MODEL), D_FF),
        "ssd_w_in": nrm(ks[6], (N_SSD_LAYERS, D_MODEL, D_IN_PROJ), D_MODEL),
        "ssd_conv_w": nrm(ks[7], (N_SSD_LAYERS, D_CONV, CONV_DIM), D_CONV),
        "ssd_conv_b": 0.02 * jax.random.normal(ks[8], (N_SSD_LAYERS, CONV_DIM), f32),
        "ssd_dt_bias": dt0 + jnp.log(-jnp.expm1(-dt0)),
        "ssd_A_log": jnp.log(jax.random.uniform(ks[10], (N_SSD_LAYERS, 2, SSD_HEADS), f32, 1.0, 16.0)),
        "ssd_D": 1.0 + 0.1 * jax.random.normal(ks[11], (N_SSD_LAYERS, SSD_HEADS), f32),
        "ssd_norm_g": 1.0 + 0.05 * jax.random.normal(ks[12], (N_SSD_LAYERS, D_INNER), f32),
        "ssd_w_out": nrm(ks[13], (N_SSD_LAYERS, D_INNER, D_MODEL), D_INNER),
        "attn_w_qkv": nrm(ks[14], (N_ATTN_LAYERS, D_MODEL, QKV_DIM), D_MODEL),
        "attn_sink": 0.5 * jax.random.normal(ks[15], (N_ATTN_LAYERS, N_HEADS), f32),
        "attn_w_out": nrm(ks[16], (N_ATTN_LAYERS, Q_DIM, D_MODEL), Q_DIM),
        "rel_bias": 0.5 * jax.random.normal(ks[17], (N_BUCKETS, N_HEADS), f32),
    }


def reference(x_prompt, x_sample, norm_g, ffn_w_gate, ffn_w_up, ffn_w_down, ssd_w_in, ssd_conv_w,
              ssd_conv_b, ssd_dt_bias, ssd_A_log, ssd_D, ssd_norm_g, ssd_w_out, attn_w_qkv,
              attn_sink, attn_w_out, rel_bias):
    y_prompt = encoder(x_prompt, norm_g, ffn_w_gate, ffn_w_up, ffn_w_down, ssd_w_in, ssd_conv_w,
                       ssd_conv_b, ssd_dt_bias, ssd_A_log, ssd_D, ssd_norm_g, ssd_w_out,
                       attn_w_qkv, attn_sink, attn_w_out, rel_bias)
    y_sample = encoder(x_sample, norm_g, ffn_w_gate, ffn_w_up, ffn_w_down, ssd_w_in, ssd_conv_w,
                       ssd_conv_b, ssd_dt_bias, ssd_A_log, ssd_D, ssd_norm_g, ssd_w_out,
                       attn_w_qkv, attn_sink, attn_w_out, rel_bias)
    return (y_prompt, y_sample)
```

```python
import contextlib
import numpy as np
import ml_dtypes
import concourse.bass as bass
import concourse.mybir as mybir
from concourse.bass_utils import run_bass_kernel_spmd

F32 = mybir.dt.float32
BF16 = mybir.dt.bfloat16
AF = mybir.ActivationFunctionType
ALU = mybir.AluOpType
AX = mybir.AxisListType

D = 1024
DFF = 2816
NFC = DFF // 128
EPS = 1e-6
DEPTH = 4
D_INNER = 2048
SSD_HEADS = 32
N_GROUPS = 8
D_STATE = 128
GN = 1024
CONV_DIM = 4096
D_IN_PROJ = 6208
N_HEADS = 16
N_KV = 4
HD = 64
QKV_DIM = 1536
N_BUCKETS = 32
NEG = -30000.0

SAME_ENGINE_SYNC = True


class Res:
    __slots__ = ("name", "last_w", "reads", "dsem", "dcnt")

    def __init__(self, name):
        self.name = name
        self.last_w = None
        self.reads = {}
        self.dsem = None
        self.dcnt = 0


class Eng:
    def __init__(self, name, kind, handle):
        self.name = name
        self.kind = kind
        self.h = handle
        self.sem = None
        self.tick = 0
        self.ops = []
        self.seen = {}


class Sched:
    def __init__(self, nc, stack):
        self.nc = nc
        self.stack = stack
        self.sems = {}
        self.nsem = 0
        self.pe = self._eng("pe", "pe", nc.tensor)
        self.act = self._eng("act", "act", nc.scalar)
        self.dve = self._eng("dve", "dve", nc.vector)
        self.pool = self._eng("pool", "pool", nc.gpsimd)
        self.sp = self._eng("sp", "sp", nc.sync)
        self.engines = [self.pe, self.act, self.dve, self.pool, self.sp]
        self.dma_slots = []
        self.bar_done = {}
        self.free_dsems = []

    def _newsem(self, name):
        s = self.stack.enter_context(self.nc.semaphore(name))
        self.nsem += 1
        sid = self.nsem
        self.sems[sid] = s
        return sid

    def _eng(self, name, kind, handle):
        e = Eng(name, kind, handle)
        e.sem = self._newsem("sem_" + name)
        return e

    def _deps(self, reads, writes):
        deps = {}

        def add(ev):
            if ev is None:
                return
            s, v = ev
            if deps.get(s, 0) < v:
                deps[s] = v
        for r in reads:
            add(r.last_w)
        for w in writes:
            add(w.last_w)
            for ev in w.reads.items():
                add(ev)
        return deps

    def _waits(self, eng, deps):
        waits = []
        for s, v in deps.items():
            if s == eng.sem and (eng.kind in ("pe", "sp") or not SAME_ENGINE_SYNC):
                continue
            if eng.seen.get(s, 0) >= v:
                continue
            eng.seen[s] = v
            waits.append((s, v))
        return waits

    def _commit(self, ev, reads, writes):
        for r in reads:
            if r.reads.get(ev[0], 0) < ev[1]:
                r.reads[ev[0]] = ev[1]
        for w in writes:
            w.last_w = ev
            w.reads = {}

    def op(self, eng, fn, reads=(), writes=()):
        deps = self._deps(reads, writes)
        waits = self._waits(eng, deps)
        eng.tick += 1
        ev = (eng.sem, eng.tick)
        eng.ops.append((waits, fn, eng.sem, 1))
        self._commit(ev, reads, writes)

    def dma(self, eng, fn, slot, reads=(), writes=()):
        if slot.dsem is None:
            if self.free_dsems:
                slot.dsem, slot.dcnt = self.free_dsems.pop()
            else:
                slot.dsem = self._newsem("d%d" % self.nsem)
            self.dma_slots.append(slot)
        deps = self._deps(reads, writes)
        waits = self._waits(eng, deps)
        slot.dcnt += 16
        ev = (slot.dsem, slot.dcnt)
        eng.ops.append((waits, fn, slot.dsem, 16))
        self._commit(ev, reads, writes)

    def barrier(self):
        evs = {}
        for e in self.engines:
            if e.kind != "sp" and e.tick > 0:
                evs[e.sem] = e.tick
        for sl in self.dma_slots:
            if sl.dcnt > self.bar_done.get(sl.dsem, 0):
                evs[sl.dsem] = sl.dcnt
                self.bar_done[sl.dsem] = sl.dcnt
        for e in self.engines:
            waits = []
            for s, v in evs.items():
                if s == e.sem and e.kind in ("pe", "sp"):
                    continue
                if e.seen.get(s, 0) >= v:
                    continue
                e.seen[s] = v
                waits.append((s, v))
            if waits:
                e.ops.append((waits, None, None, 0))
        for sl in self.dma_slots:
            self.free_dsems.append((sl.dsem, sl.dcnt))
            sl.dsem = None
        self.dma_slots = []

    def finish(self, out_res):
        deps = {}
        for r in out_res:
            if r.last_w is not None:
                s, v = r.last_w
                deps[s] = max(deps.get(s, 0), v)
        self.final_waits = list(deps.items())

    def emit(self):
        nc = self.nc
        sems = self.sems
        final_waits = getattr(self, "final_waits", [])

        def replay(e, h):
            for waits, fn, isem, inc in e.ops:
                for s, v in waits:
                    h.wait_ge(sems[s], v)
                if fn is not None:
                    fn(h).then_inc(sems[isem], inc)

        with nc.Block() as block:
            @block.tensor
            def _(h):
                replay(self.pe, h)

            @block.scalar
            def _(h):
                replay(self.act, h)

            @block.vector
            def _(h):
                replay(self.dve, h)

            @block.gpsimd
            def _(h):
                replay(self.pool, h)

            @block.sync
            def _(h):
                replay(self.sp, h)
                for s, v in final_waits:
                    h.wait_ge(sems[s], v)


class Ctx:
    pass


def sb(cx, stack, name, shape, dtype):
    t = stack.enter_context(cx.nc.sbuf_tensor(name, list(shape), dtype))
    return t


def ps(cx, stack, name, shape, dtype):
    t = stack.enter_context(cx.nc.psum_tensor(name, list(shape), dtype))
    return t


def emit_rstd(cx, ss_ap, v_ap, rstd_ap, r_ss, r_v, r_rstd, n):
    S = cx.S
    S.op(S.pool, lambda h: h.tensor_scalar(v_ap, ss_ap, 1.0 / n, EPS, ALU.mult, ALU.add),
         reads=[r_ss], writes=[r_v])
    S.op(S.pool, lambda h: h.tensor_tensor(rstd_ap, v_ap, cx.mhalf[:, 0:1], ALU.pow),
         reads=[r_v], writes=[r_rstd])


def load_bcast_row(cx, eng, dst_tile, dst_res, src_row_ap):
    S = cx.S
    S.dma(eng, lambda h: h.dma_start(out=dst_tile[:, :], in_=src_row_ap.partition_broadcast(128)),
          slot=dst_res, writes=[dst_res])


def ffn_phase(cx, wg, wu, wd, g_pre, g_post, src, dst, src_res, dst_res, ntok, tag):
    S, nc = cx.S, cx.nc
    T = cx.ffn_T
    NS = T // 128
    ntiles = ntok // T
    assert ntok % T == 0
    with contextlib.ExitStack() as st:
        Wg = sb(cx, st, "Wg" + tag, [128, 8, DFF], BF16)
        Wu = sb(cx, st, "Wu" + tag, [128, 8, DFF], BF16)
        Wd = sb(cx, st, "Wd" + tag, [128, NFC, D], BF16)
        gpre = sb(cx, st, "gpre" + tag, [128, D], F32)
        gpost = sb(cx, st, "gpost" + tag, [128, D], F32)
        xa = [sb(cx, st, f"xa{i}" + tag, [128, D], F32) for i in range(2)]
        xb = [sb(cx, st, f"xb{i}" + tag, [128, D], F32) for i in range(2)]
        hb = [sb(cx, st, f"hb{i}" + tag, [128, D], BF16) for i in range(4)]
        hT = sb(cx, st, "hT" + tag, [128, 8, T], BF16)
        actT = sb(cx, st, "actT" + tag, [128, NFC, T], BF16)
        junk = sb(cx, st, "junk" + tag, [128, D], BF16)
        sg = [sb(cx, st, f"sg{i}" + tag, [128, T], BF16) for i in range(2)]
        tt = [sb(cx, st, f"tt{i}" + tag, [128, 512], F32) for i in range(2)]
        small = sb(cx, st, "small" + tag, [128, 64], F32)
        psm = ps(cx, st, "psm" + tag, [128, 7, 512], F32)
        pst2 = [ps(cx, st, "pst" + tag, [128, D], BF16)] * 2

        r_Wg = [Res(f"Wg{k}" + tag) for k in range(8)]
        r_Wu = [Res(f"Wu{k}" + tag) for k in range(8)]
        r_Wd = [Res(f"Wd{k}" + tag) for k in range(NFC)]
        r_gpre, r_gpost = Res("gpre" + tag), Res("gpost" + tag)
        r_xa = [Res(f"xa{i}" + tag) for i in range(2)]
        r_xb = [Res(f"xb{i}" + tag) for i in range(2)]
        r_hb = [Res(f"hb{i}" + tag) for i in range(4)]
        r_hT = [Res(f"hT{s}" + tag) for s in range(NS)]
        r_act = [Res(f"act{f}" + tag) for f in range(NFC)]
        r_junk = Res("junk" + tag)
        r_sg = [Res(f"sg{i}" + tag) for i in range(2)]
        r_tt = [Res(f"tt{i}" + tag) for i in range(2)]
        r_bank = [Res(f"bank{i}" + tag) for i in range(7)]
        r_pst2 = [Res("pst" + tag)] * 2
        r_sm = [Res(f"sm{i}" + tag) for i in range(64)]

        for k in range(8):
            S.dma(S.pool, (lambda k: lambda h: h.dma_start(out=Wg[:, k, :], in_=wg[k * 128:(k + 1) * 128, :]))(k),
                  slot=r_Wg[k], writes=[r_Wg[k]])
            S.dma(S.pool, (lambda k: lambda h: h.dma_start(out=Wu[:, k, :], in_=wu[k * 128:(k + 1) * 128, :]))(k),
                  slot=r_Wu[k], writes=[r_Wu[k]])
        for f in range(NFC):
            S.dma(S.pool, (lambda f: lambda h: h.dma_start(out=Wd[:, f, :], in_=wd[f * 128:(f + 1) * 128, :]))(f),
                  slot=r_Wd[f], writes=[r_Wd[f]])
        load_bcast_row(cx, S.sp, gpre, r_gpre, g_pre)
        load_bcast_row(cx, S.sp, gpost, r_gpost, g_post)
        S.op(S.pool, lambda h: h.tensor_scalar(gpost[:, :], gpost[:, :], 0.5, None, ALU.mult),
             reads=[r_gpost], writes=[r_gpost])

        cnt = {"sub": 0, "gu": 0, "dn": 0, "ep": 0}

        def prologue_pre(t, only=None):
            for s in range(NS):
                if only is not None and s != only:
                    continue
                i = cnt["sub"] % 2
                cnt["sub"] += 1
                hi_ = s % 4
                gt = t * NS + s
                rows = slice(gt * 128, (gt + 1) * 128)
                S.dma(S.sp, (lambda i, rows: lambda h: h.dma_start(out=xa[i][:, :], in_=src[rows, :]))(i, rows),
                      slot=r_xa[i], reads=[src_res[gt]], writes=[r_xa[i]])
                S.op(S.act, (lambda i: lambda h: h.activation(junk[:, :], xa[i][:, :], AF.Square,
                                                              accum_out=small[:, i:i + 1]))(i),
                     reads=[r_xa[i]], writes=[r_junk, r_sm[i]])
                emit_rstd(cx, small[:, i:i + 1], small[:, 2 + i:3 + i], small[:, 4 + i:5 + i],
                          r_sm[i], r_sm[2 + i], r_sm[4 + i], D)
                S.op(S.dve, (lambda i, hi_: lambda h: h.scalar_tensor_tensor(
                    hb[hi_][:, :], xa[i][:, :], small[:, 4 + i:5 + i], gpre[:, :], ALU.mult, ALU.mult))(i, hi_),
                    reads=[r_xa[i], r_sm[4 + i], r_gpre], writes=[r_hb[hi_]])

        def prologue_post(t):
            for s in range(NS):
                hi_ = s % 4
                pst, r_pst = pst2[s % 2], r_pst2[s % 2]
                for k in range(8):
                    S.op(S.pe, (lambda hi_, k, pst: lambda h: h.transpose(
                        pst[:, k * 128:(k + 1) * 128], hb[hi_][:, k * 128:(k + 1) * 128], cx.ident[:, :]))(hi_, k, pst),
                        reads=[r_hb[hi_]], writes=[r_pst])
                S.op(S.act, (lambda s, pst: lambda h: h.activation(
                    hT[:, :, s * 128:(s + 1) * 128], pst[:, :].rearrange("p (k c) -> p k c", k=8), AF.Copy))(s, pst),
                    reads=[r_pst], writes=[r_hT[s]])

        def main(t, hook=None):
            for f in range(NFC):
                if hook is not None and f in (3, 7, 11, 15):
                    hook((f - 3) // 4)
                q = cnt["gu"] % 2
                cnt["gu"] += 1
                bg, bu = 2 * q, 2 * q + 1
                for (W, rW, b) in ((Wg, r_Wg, bg), (Wu, r_Wu, bu)):
                    for k in range(8):
                        S.op(S.pe, (lambda W, b, k, f: lambda h: h.matmul(
                            psm[:, b, 0:T], W[:, k, f * 128:(f + 1) * 128], hT[:, k, :],
                            start=(k == 0), stop=(k == 7)))(W, b, k, f),
                            reads=[rW[k]] + r_hT, writes=[r_bank[b]])
                S.op(S.act, (lambda q, bg: lambda h: h.activation(sg[q][:, :], psm[:, bg, 0:T], AF.Silu))(q, bg),
                     reads=[r_bank[bg]], writes=[r_sg[q]])
                S.op(S.dve, (lambda q, bu, f: lambda h: h.tensor_tensor(
                    actT[:, f, :], sg[q][:, :], psm[:, bu, 0:T], ALU.mult))(q, bu, f),
                    reads=[r_sg[q], r_bank[bu]], writes=[r_act[f]])

        def down(t):
            for s in range(NS):
                gt = t * NS + s
                rows = slice(gt * 128, (gt + 1) * 128)
                j = cnt["ep"] % 2
                cnt["ep"] += 1
                S.dma(S.sp, (lambda j, rows: lambda h: h.dma_start(out=xb[j][:, :], in_=src[rows, :]))(j, rows),
                      slot=r_xb[j], reads=[src_res[gt]], writes=[r_xb[j]])
                banks = []
                for half in range(2):
                    b = 4 + cnt["dn"] % 3
                    cnt["dn"] += 1
                    banks.append(b)
                    for f in range(NFC):
                        S.op(S.pe, (lambda b, f, s, half: lambda h: h.matmul(
                            psm[:, b, :], actT[:, f, s * 128:(s + 1) * 128], Wd[:, f, half * 512:(half + 1) * 512],
                            start=(f == 0), stop=(f == NFC - 1)))(b, f, s, half),
                            reads=[r_act[f], r_Wd[f]], writes=[r_bank[b]])
                    c = 8 + 2 * j + half
                    S.op(S.act, (lambda b, c: lambda h: h.activation(junk[:, 0:512], psm[:, b, :], AF.Square,
                                                                     accum_out=small[:, c:c + 1]))(b, c),
                         reads=[r_bank[b]], writes=[r_junk, r_sm[c]])
                c0 = 8 + 2 * j
                S.op(S.pool, (lambda c0, j: lambda h: h.tensor_tensor(
                    small[:, 16 + j:17 + j], small[:, c0:c0 + 1], small[:, c0 + 1:c0 + 2], ALU.add))(c0, j),
                    reads=[r_sm[c0], r_sm[c0 + 1]], writes=[r_sm[16 + j]])
                emit_rstd(cx, small[:, 16 + j:17 + j], small[:, 12 + j:13 + j], small[:, 14 + j:15 + j],
                          r_sm[16 + j], r_sm[12 + j], r_sm[14 + j], D)
                for half in range(2):
                    b = banks[half]
                    u = half
                    S.op(S.dve, (lambda b, u, j, half: lambda h: h.scalar_tensor_tensor(
                        tt[u][:, :], psm[:, b, :], small[:, 14 + j:15 + j], gpost[:, half * 512:(half + 1) * 512],
                        ALU.mult, ALU.mult))(b, u, j, half),
                        reads=[r_bank[b], r_sm[14 + j], r_gpost], writes=[r_tt[u]])
                    S.op(S.pool, (lambda u, j, half: lambda h: h.tensor_tensor(
                        xb[j][:, half * 512:(half + 1) * 512], tt[u][:, :], xb[j][:, half * 512:(half + 1) * 512],
                        ALU.add))(u, j, half),
                        reads=[r_tt[u], r_xb[j]], writes=[r_xb[j]])
                S.dma(S.sp, (lambda j, rows: lambda h: h.dma_start(out=dst[rows, :], in_=xb[j][:, :]))(j, rows),
                      slot=r_xb[j], reads=[r_xb[j]], writes=[dst_res[gt]])

        prologue_pre(0)
        prologue_post(0)
        for t in range(ntiles):
            if t + 1 < ntiles:
                main(t, hook=(lambda s, t=t: prologue_pre(t + 1, only=s)))
                prologue_post(t + 1)
            else:
                main(t)
            down(t)
        S.barrier()


def build_program(seqs, plan=None, ffn_T=512):
    ntok = sum(seqs)
    nc = bass.Bass("TRN2", target_bir_lowering=False)
    cx = Ctx()
    cx.nc = nc
    cx.ffn_T = ffn_T
    cx.seqs = seqs
    cx.scr = {}
    tens = {}

    def dram(name, shape, kind="ExternalInput", dtype=F32):
        if name not in tens:
            tens[name] = nc.dram_tensor(name, list(shape), dtype, kind=kind).ap()
        return tens[name]
    cx.dram = dram
    if plan is None:
        plan = []
        for i in range(DEPTH):
            plan.append(("ffn", i, 0))
            plan.append(("ssd", i // 2) if i % 2 == 0 else ("attn", i // 2))
            plan[-1] = plan[-1] + (i,)
            plan.append(("ffn", i, 1))
    x_in = dram("x", [ntok, D])
    y_out = dram("y", [ntok, D], "ExternalOutput")
    norm_g = dram("norm_g", [DEPTH, 6, D])
    c_ident = dram("c_ident", [128, 128])
    xres = dram("xres", [ntok, D], "Internal")
    nt128 = ntok // 128
    r_x = [Res(f"xin{t}") for t in range(nt128)]
    r_xres = [Res(f"xres{t}") for t in range(nt128)]
    r_y = [Res(f"y{t}") for t in range(nt128)]
    seq_list = []
    b0 = 0
    for sl in seqs:
        seq_list.append((b0, sl))
        b0 += sl

    with contextlib.ExitStack() as top:
        S = Sched(nc, top)
        cx.S = S
        cx.ident = sb(cx, top, "ident", [128, 128], BF16)
        cx.mhalf = sb(cx, top, "mhalf", [128, 1], F32)
        r_ident, r_mhalf = Res("ident"), Res("mhalf")
        S.dma(S.pool, lambda h: h.dma_start(out=cx.ident[:, :], in_=c_ident[:, :]), slot=r_ident, writes=[r_ident])
        S.op(S.pool, lambda h: h.memset(cx.mhalf[:, :], -0.5), writes=[r_mhalf])
        S.barrier()

        for pi, p in enumerate(plan):
            src, src_res = (x_in, r_x) if pi == 0 else (xres, r_xres)
            dst, dst_res = (y_out, r_y) if pi == len(plan) - 1 else (xres, r_xres)
            tag = f"_p{pi}"
            if p[0] == "ffn":
                _, i, w = p
                wg = dram("ffn_w_gate", [DEPTH, 2, D, DFF])
                wu = dram("ffn_w_up", [DEPTH, 2, D, DFF])
                wd = dram("ffn_w_down", [DEPTH, 2, DFF, D])
                ffn_phase(cx, wg[i, w], wu[i, w], wd[i, w], norm_g[i, 4 * w], norm_g[i, 4 * w + 1],
                          src, dst, src_res, dst_res, ntok, tag)
            elif p[0] == "attn":
                _, j, i = p
                attn_phase(cx, dram("attn_w_qkv", [2, D, QKV_DIM])[j], dram("attn_w_out", [2, D, D])[j],
                           dram("attn_sink", [2, N_HEADS])[j], norm_g[i, 2], norm_g[i, 3],
                           src, dst, src_res, dst_res, seq_list, tag)
            elif p[0] == "ssd":
                _, j, i = p
                ssd_phase(cx, j, norm_g[i, 2], norm_g[i, 3], src, dst, src_res, dst_res, seq_list, tag)
        S.finish(r_y)
        S.emit()
    return nc


def _t5_bucket_np(rel):
    half, max_exact = 16, 8
    ret = np.where(rel > 0, half, 0)
    n = np.abs(rel)
    large = max_exact + (np.log(np.maximum(n, 1).astype(np.float32) / np.float32(max_exact))
                         / np.float32(np.log(128 / max_exact)) * np.float32(half - max_exact)).astype(np.int32)
    large = np.minimum(large, half - 1)
    return ret + np.where(n < max_exact, n, large)


def consts():
    i = np.arange(512)
    rel = i - 255
    inwin = np.abs(rel) <= 128
    bucket = _t5_bucket_np(rel)
    oh = np.zeros((33, 512), np.float32)
    oh[bucket[inwin], i[inwin]] = 1.0
    oh[32, ~inwin] = 1.0
    u = np.arange(128)[:, None]
    l = np.arange(128)[None, :]
    tri_f = (u <= l).astype(np.float32)
    tri_b = (u >= l).astype(np.float32)
    nm_f = np.where(l >= u, 0.0, NEG).astype(np.float32)
    nm_b = np.where(l <= u, 0.0, NEG).astype(np.float32)
    sel2 = np.zeros((128, 64, 128), np.float32)
    for k in range(128):
        sel2[k, k % 64, :] = 1.0
    return {"c_ident": np.eye(128, dtype=np.float32),
            "c_anti": np.ascontiguousarray(np.eye(128, dtype=np.float32)[::-1]),
            "c_onehot": oh,
            "c_tri": np.stack([tri_f, tri_b, nm_f, nm_b]),
            "c_sel2": sel2.reshape(128, 64 * 128).astype(ml_dtypes.bfloat16)}


def bias_setup(cx, rel_bias, c_onehot, c_anti, tvec, tag=""):
    S, nc = cx.S, cx.nc
    with contextlib.ExitStack() as st:
        rb = sb(cx, st, "rb33" + tag, [33, 16], F32)
        oh = sb(cx, st, "oh33" + tag, [33, 512], F32)
        anti = sb(cx, st, "anti" + tag, [128, 128], F32)
        tv = sb(cx, st, "tv" + tag, [128, 4, 16], F32)
        hk = sb(cx, st, "hankel" + tag, [128, 384 * 16], F32)
        pp = ps(cx, st, "ps_bias" + tag, [128, 8, 512], F32)
        r_rb, r_oh, r_anti, r_tv, r_hk = Res("rb"), Res("oh"), Res("anti" + tag), Res("tv" + tag), Res("hk")
        r_tvec = Res("tvec")
        r_b = [Res(f"bb{i}") for i in range(8)]
        S.op(S.pool, lambda h: h.memset(rb[:, :], NEG), writes=[r_rb])
        S.dma(S.sp, lambda h: h.dma_start(out=rb[0:32, :], in_=rel_bias[:, :]), slot=r_rb, writes=[r_rb])
        S.dma(S.sp, lambda h: h.dma_start(out=oh[:, :], in_=c_onehot[:, :]), slot=r_oh, writes=[r_oh])
        S.dma(S.sp, lambda h: h.dma_start(out=anti[:, :], in_=c_anti[:, :]), slot=r_anti, writes=[r_anti])
        for c in range(4):
            S.op(S.pe, (lambda c: lambda h: h.matmul(pp[:, 0, c * 16:(c + 1) * 16], oh[:, c * 128:(c + 1) * 128],
                                                     rb[:, :], start=True, stop=True))(c),
                 reads=[r_rb, r_oh], writes=[r_b[0]])
        S.op(S.dve, lambda h: h.tensor_copy(tv[:, :, :], pp[:, 0, 0:64].rearrange("p (c h) -> p c h", c=4)),
             reads=[r_b[0]], writes=[r_tv])
        S.dma(S.sp, lambda h: h.dma_start(out=tvec.rearrange("(c p) h -> p c h", p=128), in_=tv[:, :, :]),
              slot=r_tv, reads=[r_tv], writes=[r_tvec])
        hank_src = bass.AP(tvec.tensor, 0, [[16, 128], [1, 384 * 16]])
        S.dma(S.sp, lambda h: h.dma_start(out=hk[:, :], in_=hank_src), slot=r_hk, reads=[r_tvec], writes=[r_hk])
        for m in range(12):
            b = 1 + m % 4
            S.op(S.pe, (lambda m, b: lambda h: h.matmul(pp[:, b, :], anti[:, :], hk[:, m * 512:(m + 1) * 512],
                                                        start=True, stop=True))(m, b),
                 reads=[r_anti, r_hk], writes=[r_b[b]])
            S.op(S.dve, (lambda m, b: lambda h: h.tensor_copy(
                cx.biasH[:, :, m * 32:(m + 1) * 32], pp[:, b, :].rearrange("p (j h) -> p h j", h=16)))(m, b),
                reads=[r_b[b]], writes=[cx.r_biasH])
        S.barrier()


def attn_phase(cx, wqkv, wout, sink, g_pre, g_post, src, dst, src_res, dst_res, seq_list, tag):
    S, nc = cx.S, cx.nc
    with contextlib.ExitStack() as st:
        cx.biasH = sb(cx, st, "biasH" + tag, [128, 16, 384], F32)
        cx.r_biasH = Res("biasH" + tag)
        bias_setup(cx, cx.dram("rel_bias", [N_BUCKETS, N_HEADS]), cx.dram("c_onehot", [33, 512]),
                   cx.dram("c_anti", [128, 128]), cx.dram("tvec", [512, 16], "Internal"), tag)
        Wqkv = sb(cx, st, "Wqkv" + tag, [128, 8, 1280 + 512], BF16)
        Wo = sb(cx, st, "Wo" + tag, [128, 8, D], BF16)
        gpre = sb(cx, st, "agpre" + tag, [128, D], F32)
        gpost = sb(cx, st, "agpost" + tag, [128, D], F32)
        sinkb = sb(cx, st, "sinkb" + tag, [128, 16], F32)
        xa = [sb(cx, st, f"axa{i}" + tag, [128, D], F32) for i in range(2)]
        xb = [sb(cx, st, f"axb{i}" + tag, [128, D], F32) for i in range(2)]
        hb = [sb(cx, st, f"ahb{i}" + tag, [128, D], BF16) for i in range(2)]
        hT = [sb(cx, st, f"ahT{i}" + tag, [128, 8, 128], BF16) for i in range(2)]
        qT = [sb(cx, st, f"qT{i}" + tag, [128, 8, 128], BF16) for i in range(3)]
        klo = sb(cx, st, "klo" + tag, [128, 4, 4, 128], BF16)
        khi = sb(cx, st, "khi" + tag, [128, 4, 4, 128], BF16)
        vv = sb(cx, st, "vv" + tag, [128, 4, 256], BF16)
        Ssb = sb(cx, st, "Ssb" + tag, [128, 16, 384], F32)
        Pb2 = [sb(cx, st, f"Pb{i}" + tag, [128, 16, 384], BF16) for i in range(2)]
        PT = sb(cx, st, "PT" + tag, [128, 16, 3, 128], BF16)
        osb = sb(cx, st, "osb" + tag, [128, D], BF16)
        oT = sb(cx, st, "oT" + tag, [128, 8, 128], BF16)
        junk = sb(cx, st, "ajunk" + tag, [128, D], BF16)
        tt = [sb(cx, st, f"att{i}" + tag, [128, 512], F32) for i in range(2)]
        small = sb(cx, st, "asmall" + tag, [128, 192], F32)
        pp = ps(cx, st, "aps" + tag, [128, 8, 512], F32)

        r_W, r_Wo = Res("Wqkv" + tag), Res("Wo" + tag)
        r_gpre, r_gpost, r_sink = Res("agpre" + tag), Res("agpost" + tag), Res("sinkb" + tag)
        r_xa = [Res(f"axa{i}" + tag) for i in range(2)]
        r_xb = [Res(f"axb{i}" + tag) for i in range(2)]
        r_hb = [Res(f"ahb{i}" + tag) for i in range(2)]
        r_hT = [Res(f"ahT{i}" + tag) for i in range(2)]
        r_qT = [Res(f"qT{i}" + tag) for i in range(3)]
        r_k = [Res(f"kslot{i}" + tag) for i in range(4)]
        r_v = [Res(f"vslot{i}" + tag) for i in range(4)]
        r_S, r_PT, r_osb, r_oT, r_junk = (Res("Ssb" + tag), Res("PT" + tag),
                                          Res("osb" + tag), Res("oT" + tag), Res("ajunk" + tag))
        r_P2 = [[Res(f"Pb{i}_{hp}" + tag) for hp in range(8)] for i in range(2)]
        r_PTh = [Res(f"PTh{i}" + tag) for i in range(8)]
        r_Sh = [Res(f"Sh{i}" + tag) for i in range(16)]
        r_tt = [Res(f"att{i}" + tag) for i in range(2)]
        r_b = [Res(f"abank{i}" + tag) for i in range(8)]
        r_sm = {}

        def rsm(name):
            if name not in r_sm:
                r_sm[name] = Res("asm_" + name + tag)
            return r_sm[name]
        C_SS, C_V, C_RSTD = 0, 2, 4
        C_M, C_NEGM, C_ROW, C_ES, C_DEN, C_RDEN = 16, 32, 48, 64, 80, 96
        C_SS2, C_SSS, C_V2, C_RSTD2 = 112, 116, 118, 120

        wq3 = wqkv.rearrange("(k p) f -> p k f", p=128)
        for k in range(8):
            S.dma(S.pool, (lambda k: lambda h: h.dma_start(out=Wqkv[:, k, 0:1024], in_=wqkv[k * 128:(k + 1) * 128, 0:1024]))(k),
                  slot=r_W, writes=[r_W])
            S.dma(S.pool, (lambda k: lambda h: h.dma_start(out=Wqkv[:, k, 1024:1280], in_=wqkv[k * 128:(k + 1) * 128, 1280:1536]))(k),
                  slot=r_W, writes=[r_W])
            for rep in range(2):
                S.dma(S.pool, (lambda k, rep: lambda h: h.dma_start(
                    out=Wqkv[:, k, 1280:1792].rearrange("p (a r d) -> p a r d", a=4, r=2)[:, :, rep, :],
                    in_=wqkv[k * 128:(k + 1) * 128, 1024:1280].rearrange("p (a d) -> p a d", a=4)))(k, rep),
                    slot=r_W, writes=[r_W])
            S.dma(S.pool, (lambda k: lambda h: h.dma_start(out=Wo[:, k, :], in_=wout[k * 128:(k + 1) * 128, :]))(k),
                  slot=r_Wo, writes=[r_Wo])
        load_bcast_row(cx, S.sp, gpre, r_gpre, g_pre)
        load_bcast_row(cx, S.sp, gpost, r_gpost, g_post)
        load_bcast_row(cx, S.sp, sinkb, r_sink, sink)
        S.op(S.pool, lambda h: h.memset(klo[:, :, :, :], 0.0), writes=r_k)
        S.op(S.pool, lambda h: h.memset(khi[:, :, :, :], 0.0), writes=r_k)

        cnt = {"a": 0, "ep": 0}

        def a_pre(base, n):
            i = n % 2
            gt = base // 128 + n
            rows = slice(gt * 128, (gt + 1) * 128)
            S.dma(S.sp, lambda h: h.dma_start(out=xa[i][:, :], in_=src[rows, :]),
                  slot=r_xa[i], reads=[src_res[gt]], writes=[r_xa[i]])
            S.op(S.act, lambda h: h.activation(junk[:, :], xa[i][:, :], AF.Square,
                                               accum_out=small[:, C_SS + i:C_SS + i + 1]),
                 reads=[r_xa[i]], writes=[r_junk, rsm(f"ss{i}")])
            emit_rstd(cx, small[:, C_SS + i:C_SS + i + 1], small[:, C_V + i:C_V + i + 1],
                      small[:, C_RSTD + i:C_RSTD + i + 1], rsm(f"ss{i}"), rsm(f"v{i}"), rsm(f"rstd{i}"), D)
            S.op(S.dve, lambda h: h.scalar_tensor_tensor(
                hb[i][:, :], xa[i][:, :], small[:, C_RSTD + i:C_RSTD + i + 1], gpre[:, :], ALU.mult, ALU.mult),
                reads=[r_xa[i], rsm(f"rstd{i}"), r_gpre], writes=[r_hb[i]])

        def a_main(base, n):
            i = n % 2
            sl = n % 4
            pst = pp[:, 0, :].bitcast(BF16)
            for k in range(8):
                S.op(S.pe, (lambda k: lambda h: h.transpose(
                    pst[:, k * 128:(k + 1) * 128], hb[i][:, k * 128:(k + 1) * 128], cx.ident[:, :]))(k),
                    reads=[r_hb[i]], writes=[r_b[0]])
            S.op(S.dve, lambda h: h.tensor_copy(hT[i][:, :, :], pst.rearrange("p (k c) -> p k c", k=8)),
                 reads=[r_b[0]], writes=[r_hT[i]])
            for c in range(8):
                b = 1 + c // 4
                for k in range(8):
                    S.op(S.pe, (lambda c, k, b: lambda h: h.matmul(
                        pp[:, b, (c % 4) * 128:(c % 4 + 1) * 128], Wqkv[:, k, c * 128:(c + 1) * 128], hT[i][:, k, :],
                        start=(k == 0), stop=(k == 7)))(c, k, b),
                        reads=[r_W, r_hT[i]], writes=[r_b[b]])
            qi = n % 3
            for b in (1, 2):
                S.op(S.dve, (lambda b: lambda h: h.tensor_copy(
                    qT[qi][:, (b - 1) * 4:b * 4, :], pp[:, b, :].rearrange("p (c t) -> p c t", c=4)))(b),
                    reads=[r_b[b]], writes=[r_qT[qi]])
            for a in range(4):
                for k in range(8):
                    S.op(S.pe, (lambda a, k: lambda h: h.matmul(
                        pp[:, 3, a * 128:(a + 1) * 128], Wqkv[:, k, 1280 + a * 128:1280 + (a + 1) * 128], hT[i][:, k, :],
                        start=(k == 0), stop=(k == 7)))(a, k),
                        reads=[r_W, r_hT[i]], writes=[r_b[3]])
            S.op(S.dve, lambda h: h.tensor_copy(klo[0:64, sl, :, :], pp[0:64, 3, :].rearrange("p (a t) -> p a t", a=4)),
                 reads=[r_b[3]], writes=[r_k[sl]])
            S.op(S.dve, lambda h: h.tensor_copy(khi[64:128, sl, :, :], pp[64:128, 3, :].rearrange("p (a t) -> p a t", a=4)),
                 reads=[r_b[3]], writes=[r_k[sl]])
            for k in range(8):
                S.op(S.pe, (lambda k: lambda h: h.matmul(
                    pp[:, 4, 0:256], hT[i][:, k, :], Wqkv[:, k, 1024:1280], start=(k == 0), stop=(k == 7)))(k),
                    reads=[r_W, r_hT[i]], writes=[r_b[4]])
            S.op(S.dve, lambda h: h.tensor_copy(vv[:, sl, :], pp[:, 4, 0:256]),
                 reads=[r_b[4]], writes=[r_v[sl]])

        def keys_of(n, nb):
            jlist = [j for j in range(3) if 0 <= n - 1 + j < nb]
            return jlist, jlist[0] * 128, (jlist[-1] + 1) * 128

        def pv_part(prev, hp):
            n, nb, par = prev
            Pb, r_P = Pb2[par], r_P2[par]
            jlist, _, _ = keys_of(n, nb)
            b = 3 if hp % 2 == 0 else 0
            pb = pp[:, b, :].bitcast(BF16)
            for u in range(2):
                hd = hp * 2 + u
                for j in jlist:
                    S.op(S.pe, (lambda pb, u, j, hd: lambda h: h.transpose(
                        pb[:, (u * 3 + j) * 128:(u * 3 + j + 1) * 128], Pb[:, hd, j * 128:(j + 1) * 128],
                        cx.ident[:, :]))(pb, u, j, hd),
                        reads=[r_P[hp]], writes=[r_b[b]])
            j0, j1 = jlist[0], jlist[-1] + 1
            S.op(S.act, lambda h: h.activation(
                PT[:, hp * 2:hp * 2 + 2, j0:j1, :],
                pb[:, 0:768].rearrange("p (u j t) -> p u j t", u=2, j=3)[:, :, j0:j1, :], AF.Copy),
                reads=[r_b[b]], writes=[r_PTh[hp]])
            for u in range(2):
                hd = hp * 2 + u
                kvh = hd // 4
                bo = 1 + hd // 8
                for idx, j in enumerate(jlist):
                    sl = (n - 1 + j) % 4
                    S.op(S.pe, (lambda hd, bo, j, sl, kvh, idx: lambda h: h.matmul(
                        pp[:, bo, (hd % 8) * 64:(hd % 8 + 1) * 64], PT[:, hd, j, :], vv[:, sl, kvh * 64:(kvh + 1) * 64],
                        start=(idx == 0), stop=(idx == len(jlist) - 1)))(hd, bo, j, sl, kvh, idx),
                        reads=[r_PTh[hp], r_v[sl]], writes=[r_b[bo]])

        def b_scores(n, nb, prev):
            qi = n % 3
            jlist, c0, c1 = keys_of(n, nb)
            for hd in range(16):
                kvh, half, c = hd // 4, hd % 2, hd // 2
                b = 6 + hd % 2
                kt = klo if half == 0 else khi
                for j in jlist:
                    sl = (n - 1 + j) % 4
                    S.op(S.pe, (lambda b, c, j, sl, kvh, kt: lambda h: h.matmul(
                        pp[:, b, j * 128:(j + 1) * 128], qT[qi][:, c, :], kt[:, sl, kvh, :], start=True, stop=True))(
                        b, c, j, sl, kvh, kt),
                        reads=[r_qT[qi], r_k[sl]], writes=[r_b[b]])
                S.op(S.dve, (lambda b, hd: lambda h: h.scalar_tensor_tensor(
                    Ssb[:, hd, c0:c1], pp[:, b, c0:c1], 0.125, cx.biasH[:, hd, c0:c1], ALU.mult, ALU.add))(b, hd),
                    reads=[r_b[b], cx.r_biasH], writes=[r_Sh[hd]])
                S.op(S.dve, (lambda hd: lambda h: h.tensor_reduce(small[:, C_M + hd:C_M + hd + 1], Ssb[:, hd, c0:c1], AX.X, ALU.max))(hd),
                     reads=[r_Sh[hd]], writes=[rsm(f"m{hd}")])
                if prev is not None and hd % 2 == 1:
                    pv_part(prev, hd // 2)

        def b_soft(n, nb, par):
            Pb, r_P = Pb2[par], r_P2[par]
            C_RDEN_ = C_RDEN + 48 * par
            jlist, c0, c1 = keys_of(n, nb)
            sm = lambda c: small[:, c:c + 16]
            S.op(S.dve, lambda h: h.tensor_tensor(sm(C_M), sm(C_M), sinkb[:, :], ALU.max),
                 reads=[rsm(f"m{hd}") for hd in range(16)] + [r_sink], writes=[rsm("m")])
            S.op(S.dve, lambda h: h.tensor_scalar(sm(C_NEGM), sm(C_M), -1.0, None, ALU.mult),
                 reads=[rsm("m")], writes=[rsm("negm")])
            for hd in range(16):
                S.op(S.act, (lambda hd: lambda h: h.activation(
                    Pb[:, hd, c0:c1], Ssb[:, hd, c0:c1], AF.Exp, bias=small[:, C_NEGM + hd:C_NEGM + hd + 1],
                    accum_out=small[:, C_ROW + hd:C_ROW + hd + 1]))(hd),
                    reads=[r_Sh[hd], rsm("negm")], writes=[r_P[hd // 2], rsm(f"row{hd}")])
            S.op(S.dve, lambda h: h.tensor_tensor(sm(C_ES), sinkb[:, :], sm(C_NEGM), ALU.add),
                 reads=[r_sink, rsm("negm")], writes=[rsm("es")])
            S.op(S.act, lambda h: h.activation(sm(C_ES), sm(C_ES), AF.Exp), reads=[rsm("es")], writes=[rsm("es")])

        def b_tail(base, prev):
            n, nb, par = prev
            gt = base // 128 + n
            rows = slice(gt * 128, (gt + 1) * 128)
            C_RDEN_ = C_RDEN + 48 * par
            sm = lambda c: small[:, c:c + 16]
            S.op(S.dve, lambda h: h.tensor_tensor(sm(C_DEN), sm(C_ROW), sm(C_ES), ALU.add),
                 reads=[rsm("es")] + [rsm(f"row{hd}") for hd in range(16)], writes=[rsm("den")])
            S.op(S.dve, lambda h: h.reciprocal(sm(C_RDEN_), sm(C_DEN)), reads=[rsm("den")], writes=[rsm(f"rden{par}")])
            for b in (1, 2):
                S.op(S.dve, (lambda b: lambda h: h.tensor_tensor(
                    osb[:, (b - 1) * 512:b * 512].rearrange("p (a d) -> p a d", a=8),
                    pp[:, b, :].rearrange("p (a d) -> p a d", a=8),
                    small[:, C_RDEN_ + (b - 1) * 8:C_RDEN_ + b * 8].unsqueeze(2).broadcast_to([128, 8, 64]),
                    ALU.mult))(b),
                    reads=[r_b[b], rsm(f"rden{par}")], writes=[r_osb])
            pst = pp[:, 0, :].bitcast(BF16)
            for k in range(8):
                S.op(S.pe, (lambda k: lambda h: h.transpose(
                    pst[:, k * 128:(k + 1) * 128], osb[:, k * 128:(k + 1) * 128], cx.ident[:, :]))(k),
                    reads=[r_osb], writes=[r_b[0]])
            S.op(S.act, lambda h: h.activation(oT[:, :, :], pst.rearrange("p (k c) -> p k c", k=8), AF.Copy),
                 reads=[r_b[0]], writes=[r_oT])
            jx = cnt["ep"] % 2
            cnt["ep"] += 1
            S.dma(S.sp, lambda h: h.dma_start(out=xb[jx][:, :], in_=src[rows, :]),
                  slot=r_xb[jx], reads=[src_res[gt]], writes=[r_xb[jx]])
            for half in range(2):
                b = 4 + half
                for k in range(8):
                    S.op(S.pe, (lambda b, k, half: lambda h: h.matmul(
                        pp[:, b, :], oT[:, k, :], Wo[:, k, half * 512:(half + 1) * 512],
                        start=(k == 0), stop=(k == 7)))(b, k, half),
                        reads=[r_oT, r_Wo], writes=[r_b[b]])
                S.op(S.act, (lambda b, half: lambda h: h.activation(
                    junk[:, 0:512], pp[:, b, :], AF.Square, accum_out=small[:, C_SS2 + half:C_SS2 + half + 1]))(b, half),
                    reads=[r_b[b]], writes=[r_junk, rsm(f"ss2{half}")])
            S.op(S.pool, lambda h: h.tensor_tensor(small[:, C_SSS:C_SSS + 1], small[:, C_SS2:C_SS2 + 1],
                                                   small[:, C_SS2 + 1:C_SS2 + 2], ALU.add),
                 reads=[rsm("ss20"), rsm("ss21")], writes=[rsm("sss")])
            emit_rstd(cx, small[:, C_SSS:C_SSS + 1], small[:, C_V2:C_V2 + 1], small[:, C_RSTD2:C_RSTD2 + 1],
                      rsm("sss"), rsm("v2"), rsm("rstd2"), D)
            for half in range(2):
                b = 4 + half
                S.op(S.dve, (lambda b, half: lambda h: h.scalar_tensor_tensor(
                    tt[half][:, :], pp[:, b, :], small[:, C_RSTD2:C_RSTD2 + 1], gpost[:, half * 512:(half + 1) * 512],
                    ALU.mult, ALU.mult))(b, half),
                    reads=[r_b[b], rsm("rstd2"), r_gpost], writes=[r_tt[half]])
                S.op(S.pool, (lambda half: lambda h: h.tensor_tensor(
                    xb[jx][:, half * 512:(half + 1) * 512], tt[half][:, :], xb[jx][:, half * 512:(half + 1) * 512],
                    ALU.add))(half),
                    reads=[r_tt[half], r_xb[jx]], writes=[r_xb[jx]])
            S.dma(S.sp, lambda h: h.dma_start(out=dst[rows, :], in_=xb[jx][:, :]),
                  slot=r_xb[jx], reads=[r_xb[jx]], writes=[dst_res[gt]])

        kk = 0
        for (base, slen) in seq_list:
            nb = slen // 128
            a_pre(base, 0)
            a_main(base, 0)
            if nb > 1:
                a_pre(base, 1)
                a_main(base, 1)
            if nb > 2:
                a_pre(base, 2)
            prev = None
            for n in range(nb):
                b_scores(n, nb, prev)
                if prev is not None:
                    b_tail(base, prev)
                b_soft(n, nb, kk % 2)
                if n + 2 < nb:
                    a_main(base, n + 2)
                if n + 3 < nb:
                    a_pre(base, n + 3)
                prev = (n, nb, kk % 2)
                kk += 1
            for hp in range(8):
                pv_part(prev, hp)
            b_tail(base, prev)
        S.barrier()


def bc3(ap2, n):
    return ap2.unsqueeze(2).broadcast_to([ap2.shape[0], ap2.shape[1], n])


def ssd_phase(cx, j, g_pre, g_post, src, dst, src_res, dst_res, seq_list, tag):
    S, nc = cx.S, cx.nc
    dram = cx.dram
    ntok = sum(s for _, s in seq_list)
    nch = ntok // 128
    w_in = dram("ssd_w_in", [2, D, D_IN_PROJ])[j]
    conv_w = dram("ssd_conv_w", [2, 5, CONV_DIM])[j]
    conv_b = dram("ssd_conv_b", [2, CONV_DIM])[j]
    dt_bias = dram("ssd_dt_bias", [2, 2, SSD_HEADS])[j]
    A_log = dram("ssd_A_log", [2, 2, SSD_HEADS])[j]
    Dp = dram("ssd_D", [2, SSD_HEADS])[j]
    ng = dram("ssd_norm_g", [2, D_INNER])[j]
    w_out = dram("ssd_w_out", [2, D_INNER, D])[j]
    c_tri = dram("c_tri", [4, 128, 128])
    c_sel = dram("c_sel2", [128, 64 * 128], dtype=BF16)
    c_ident = dram("c_ident", [128, 128])

    def scr(name, shape, dtype=F32):
        key = name
        if key not in cx.scr:
            cx.scr[key] = nc.dram_tensor(name, list(shape), dtype, kind="Internal").ap()
        return cx.scr[key]
    zs = scr("s_z", [ntok, 2048])
    dtr = scr("s_dtr", [ntok, 64])
    xtok = scr("s_xtok", [ntok, 2048], BF16)
    btok = scr("s_btok", [ntok, 1024], BF16)
    bct = scr("s_bct", [nch, 128, 16 * 128], BF16)
    yacc_d = scr("s_yacc", [ntok, 2048])
    xddb_d = scr("s_xddb", [ntok, 2048], BF16)
    scb_d = scr("s_scb", [ntok, 64])
    R = lambda n: [Res(f"{n}{t}{tag}") for t in range(nch)]
    r_zs, r_dtr, r_xtok, r_btok, r_bct, r_yacc, r_xddb, r_scb = (R("zs"), R("dtr"), R("xtok"), R("btok"), R("bct"),
                                                               R("yacc"), R("xddb"), R("scb"))

    def _sweep_f1():
        with contextlib.ExitStack() as st:
            Win = sb(cx, st, "Win" + tag, [128, 8, D_IN_PROJ], BF16)
            acc2 = [sb(cx, st, f"acc{i}" + tag, [128, 32, 128], F32) for i in range(2)]
            tmpA = sb(cx, st, "tmpA" + tag, [128, 32, 128], BF16)
            pc = [sb(cx, st, f"pc{i}" + tag, [128, 32, 132], BF16) for i in range(3)]
            xbcT = sb(cx, st, "xbcT" + tag, [128, 32, 128], BF16)
            xtk = sb(cx, st, "xtk" + tag, [128, 2048], BF16)
            btk = sb(cx, st, "btk" + tag, [128, 1024], BF16)
            zst = [sb(cx, st, f"zst{i}" + tag, [128, 1024], F32) for i in range(2)]
            dtst = sb(cx, st, "dtst" + tag, [128, 64], F32)
            xa2 = [sb(cx, st, f"sxa{i}" + tag, [128, D], F32) for i in range(2)]
            hb2 = [sb(cx, st, f"shb{i}" + tag, [128, D], BF16) for i in range(2)]
            hT = [sb(cx, st, f"shT{i}" + tag, [128, 8, 128], BF16) for i in range(2)]
            gpre = sb(cx, st, "sgpre" + tag, [128, D], F32)
            idf = sb(cx, st, "idf" + tag, [128, 128], F32)
            cw = [sb(cx, st, f"cw{i}" + tag, [128, 128], F32) for i in range(2)]
            wT = sb(cx, st, "wT" + tag, [128, 192], F32)
            wTb = sb(cx, st, "wTb" + tag, [128, 192], BF16)
            small = sb(cx, st, "f1small" + tag, [128, 8], F32)
            pp = ps(cx, st, "f1ps" + tag, [128, 8, 512], F32)
            r_Win, r_diag, r_gpre, r_cbrow, r_ones, r_idf, r_cw, r_wT = (Res("Win" + tag), Res("diag" + tag), Res("sgpre" + tag),
                                                                          Res("cbrow" + tag), Res("onesr" + tag), Res("idf" + tag),
                                                                          [Res("cw0" + tag), Res("cw1" + tag)], Res("wT" + tag))
            r_pc = [Res(f"pc{i}" + tag) for i in range(3)]
            r_xbcT, r_xtk, r_btk, r_dtst = (Res("xbcT" + tag), Res("xtk" + tag), Res("btk" + tag), Res("dtst" + tag))
            r_xa2 = [Res(f"sxa{i}" + tag) for i in range(2)]
            r_hb2 = [Res(f"shb{i}" + tag) for i in range(2)]
            r_acc2 = [Res(f"acc{i}" + tag) for i in range(2)]
            r_tmpA = Res("tmpA" + tag)
            r_zst = [Res(f"zst{i}" + tag) for i in range(2)]
            r_hT = [Res(f"shT{i}" + tag) for i in range(2)]
            r_b = [Res(f"f1b{i}" + tag) for i in range(8)]
            r_sm = [Res(f"f1sm{i}" + tag) for i in range(8)]

            for k in range(8):
                for q in range(4):
                    c0, c1 = q * 1552, (q + 1) * 1552
                    S.dma(S.pool, (lambda k, c0, c1: lambda h: h.dma_start(out=Win[:, k, c0:c1], in_=w_in[k * 128:(k + 1) * 128, c0:c1]))(k, c0, c1),
                          slot=r_Win, writes=[r_Win])
            load_bcast_row(cx, S.sp, gpre, r_gpre, g_pre)
            S.dma(S.sp, lambda h: h.dma_start(out=idf[:, :], in_=c_ident[:, :]), slot=r_idf, writes=[r_idf])
            cwv = conv_w.rearrange("k (c p) -> (k c) p", p=128)
            S.dma(S.sp, lambda h: h.dma_start(out=cw[0][:, :], in_=cwv[0:128, :]), slot=r_cw[0], writes=[r_cw[0]])
            S.dma(S.sp, lambda h: h.dma_start(out=cw[1][0:32, :], in_=cwv[128:160, :]), slot=r_cw[1], writes=[r_cw[1]])
            S.dma(S.sp, lambda h: h.dma_start(out=cw[1][32:64, :], in_=conv_b.rearrange("(c p) -> c p", p=128)), slot=r_cw[1], writes=[r_cw[1]])
            S.op(S.pe, lambda h: h.transpose(pp[:, 0, 0:128], cw[0][:, :], idf[:, :]), reads=[r_cw[0], r_idf], writes=[r_b[0]])
            S.op(S.pe, lambda h: h.transpose(pp[:, 0, 128:192], cw[1][0:64, :], idf[0:64, 0:64]), reads=[r_cw[1], r_idf], writes=[r_b[0]])
            S.op(S.dve, lambda h: h.tensor_copy(wT[:, :], pp[:, 0, 0:192]), reads=[r_b[0]], writes=[r_wT])
            S.op(S.dve, lambda h: h.tensor_copy(wTb[:, :], wT[:, :]), reads=[r_wT], writes=[r_wT])
            for i in range(3):
                S.op(S.pool, (lambda i: lambda h: h.memset(pc[i][:, :, :], 0.0))(i), writes=[r_pc[i]])

            cnt = {"p": 0, "z": 0}

            def project_pre(base, c):
                i = c % 2
                xa, hb, r_xa, r_hb = xa2[i], hb2[i], r_xa2[i], r_hb2[i]
                gt = base // 128 + c
                rows = slice(gt * 128, (gt + 1) * 128)
                S.dma(S.sp, lambda h: h.dma_start(out=xa[:, :], in_=src[rows, :]), slot=r_xa, reads=[src_res[gt]], writes=[r_xa])
                S.op(S.act, lambda h: h.activation(hb[:, :], xa[:, :], AF.Square, accum_out=small[:, 3 * i:3 * i + 1]),
                     reads=[r_xa], writes=[r_hb, r_sm[3 * i]])
                emit_rstd(cx, small[:, 3 * i:3 * i + 1], small[:, 3 * i + 1:3 * i + 2], small[:, 3 * i + 2:3 * i + 3],
                          r_sm[3 * i], r_sm[3 * i + 1], r_sm[3 * i + 2], D)
                S.op(S.dve, lambda h: h.scalar_tensor_tensor(hb[:, :], xa[:, :], small[:, 3 * i + 2:3 * i + 3], gpre[:, :], ALU.mult, ALU.mult),
                     reads=[r_xa, r_sm[3 * i + 2], r_gpre], writes=[r_hb])

            def project_main(base, c, ncs):
                i = c % 2
                hb, r_hb = hb2[i], r_hb2[i]
                gt = base // 128 + c
                rows = slice(gt * 128, (gt + 1) * 128)
                sl = c % 3
                pst = pp[:, 0, :].bitcast(BF16)
                for k in range(8):
                    S.op(S.pe, (lambda k: lambda h: h.transpose(pst[:, k * 128:(k + 1) * 128], hb[:, k * 128:(k + 1) * 128], cx.ident[:, :]))(k),
                         reads=[r_hb], writes=[r_b[0]])
                S.op(S.act, lambda h: h.activation(hT[i][:, :, :], pst.rearrange("p (k c) -> p k c", k=8), AF.Copy),
                     reads=[r_b[0]], writes=[r_hT[i]])
                for q in range(4):
                    for k in range(8):
                        S.op(S.pe, (lambda q, k: lambda h: h.matmul(pp[:, 1 + q, :], hT[i][:, k, :], Win[:, k, q * 512:(q + 1) * 512],
                                                                    start=(k == 0), stop=(k == 7)))(q, k),
                             reads=[r_hT[i], r_Win], writes=[r_b[1 + q]])
                for hf in range(2):
                    zi = cnt["z"] % 2
                    cnt["z"] += 1
                    S.op(S.act, (lambda hf, zi: lambda h: h.activation(zst[zi][:, :].rearrange("p (a b) -> p a b", a=2),
                                                                       pp[:, 1 + 2 * hf:3 + 2 * hf, :], AF.Copy))(hf, zi),
                         reads=[r_b[1 + 2 * hf], r_b[2 + 2 * hf]], writes=[r_zst[zi]])
                    S.dma(S.sp, (lambda hf, zi: lambda h: h.dma_start(out=zs[rows, hf * 1024:(hf + 1) * 1024], in_=zst[zi][:, :]))(hf, zi),
                          slot=r_zst[zi], reads=[r_zst[zi]], writes=[r_zs[gt]])
                for k in range(8):
                    S.op(S.pe, (lambda k: lambda h: h.matmul(pp[:, 5, 0:64], hT[i][:, k, :], Win[:, k, 6144:6208],
                                                             start=(k == 0), stop=(k == 7)))(k),
                         reads=[r_hT[i], r_Win], writes=[r_b[5]])
                S.op(S.act, lambda h: h.activation(dtst[:, :], pp[:, 5, 0:64], AF.Copy), reads=[r_b[5]], writes=[r_dtst])
                S.dma(S.sp, lambda h: h.dma_start(out=dtr[rows, :], in_=dtst[:, :]), slot=r_dtst, reads=[r_dtst], writes=[r_dtr[gt]])
                for c4 in range(8):
                    b = 6 + c4 % 2
                    for u in range(4):
                        cc = c4 * 4 + u
                        for k in range(8):
                            S.op(S.pe, (lambda b, u, cc, k: lambda h: h.matmul(
                                pp[:, b, u * 128:(u + 1) * 128], Win[:, k, 2048 + cc * 128:2048 + (cc + 1) * 128], hT[i][:, k, :],
                                start=(k == 0), stop=(k == 7)))(b, u, cc, k),
                                reads=[r_hT[i], r_Win], writes=[r_b[b]])
                    S.op(S.act, (lambda b, c4: lambda h: h.activation(pc[sl][:, c4 * 4:(c4 + 1) * 4, 2:130],
                                                                      pp[:, b, :].rearrange("p (u t) -> p u t", u=4), AF.Copy))(b, c4),
                         reads=[r_b[b]], writes=[r_pc[sl]])
                if c > 0:
                    pv_ = (c - 1) % 3
                    S.op(S.pool, lambda h: h.tensor_copy(pc[pv_][:, :, 130:132], pc[sl][:, :, 2:4]), reads=[r_pc[sl]], writes=[r_pc[pv_]])
                    S.op(S.pool, lambda h: h.tensor_copy(pc[sl][:, :, 0:2], pc[pv_][:, :, 128:130]), reads=[r_pc[pv_]], writes=[r_pc[sl]])
                else:
                    S.op(S.pool, lambda h: h.memset(pc[sl][:, :, 0:2], 0.0), writes=[r_pc[sl]])
                if c == ncs - 1:
                    S.op(S.pool, lambda h: h.memset(pc[sl][:, :, 130:132], 0.0), writes=[r_pc[sl]])

            wv = lambda k: wT[:, k * 32:(k + 1) * 32].unsqueeze(2).broadcast_to([128, 32, 128])

            def conv_dve(c):
                sl = c % 3
                P, rP = pc[sl], r_pc[sl]
                A, rA = acc2[c % 2], r_acc2[c % 2]
                S.op(S.dve, lambda h: h.tensor_tensor(A[:, :, :], P[:, :, 0:128], wv(0), ALU.mult), reads=[rP, r_wT], writes=[rA])
                for k in (1, 2, 3, 4):
                    S.op(S.dve, (lambda k: lambda h: h.tensor_tensor(tmpA[:, :, :], P[:, :, k:k + 128], wv(k), ALU.mult))(k),
                         reads=[rP, r_wT], writes=[r_tmpA])
                    S.op(S.dve, lambda h: h.tensor_tensor(A[:, :, :], A[:, :, :], tmpA[:, :, :], ALU.add), reads=[rA, r_tmpA], writes=[rA])

            def conv_post(base, c):
                gt = base // 128 + c
                rows = slice(gt * 128, (gt + 1) * 128)
                A, rA = acc2[c % 2], r_acc2[c % 2]
                for cc in range(32):
                    S.op(S.act, (lambda cc: lambda h: h.activation(xbcT[:, cc, :], A[:, cc, :], AF.Silu, bias=wT[:, 160 + cc:161 + cc]))(cc),
                         reads=[rA, r_wT], writes=[r_xbcT])
                for hf in range(2):
                    pst = pp[:, 3 + hf, :].bitcast(BF16)
                    for u in range(8):
                        cc = hf * 8 + u
                        S.op(S.pe, (lambda pst, u, cc: lambda h: h.transpose(pst[:, u * 128:(u + 1) * 128], xbcT[:, cc, :], cx.ident[:, :]))(pst, u, cc),
                             reads=[r_xbcT], writes=[r_b[3 + hf]])
                    S.op(S.act, (lambda pst, hf: lambda h: h.activation(xtk[:, hf * 1024:(hf + 1) * 1024], pst, AF.Copy))(pst, hf),
                         reads=[r_b[3 + hf]], writes=[r_xtk])
                pst = pp[:, 5, :].bitcast(BF16)
                for u in range(8):
                    S.op(S.pe, (lambda u: lambda h: h.transpose(pst[:, u * 128:(u + 1) * 128], xbcT[:, 16 + u, :], cx.ident[:, :]))(u),
                         reads=[r_xbcT], writes=[r_b[5]])
                S.op(S.act, lambda h: h.activation(btk[:, :], pst, AF.Copy), reads=[r_b[5]], writes=[r_btk])
                S.dma(S.sp, lambda h: h.dma_start(out=xtok[rows, :], in_=xtk[:, :]), slot=r_xtk, reads=[r_xtk], writes=[r_xtok[gt]])
                S.dma(S.sp, lambda h: h.dma_start(out=btok[rows, :], in_=btk[:, :]), slot=r_btk, reads=[r_btk], writes=[r_btok[gt]])
                S.dma(S.sp, lambda h: h.dma_start(out=bct[gt], in_=xbcT[:, 16:32, :].rearrange("p a t -> p (a t)")),
                      slot=r_xbcT, reads=[r_xbcT], writes=[r_bct[gt]])

            for (base, slen) in seq_list:
                ncs = slen // 128
                project_pre(base, 0)
                project_main(base, 0, ncs)
                if ncs > 1:
                    project_pre(base, 1)
                for c in range(1, ncs + 1):
                    if c < ncs:
                        project_main(base, c, ncs)
                    if c + 1 < ncs:
                        project_pre(base, c + 1)
                    conv_dve(c - 1)
                    if c >= 2:
                        conv_post(base, c - 2)
                conv_post(base, ncs - 1)
            S.barrier()
    _sweep_f1()

    def _sweep_f2():
        with contextlib.ExitStack() as st:
            sel2 = sb(cx, st, "sel2" + tag, [128, 64, 128], BF16)
            tri = sb(cx, st, "tri" + tag, [128, 2, 128], F32)
            nmask = sb(cx, st, "nmask" + tag, [128, 2, 128], BF16)
            onesf = sb(cx, st, "onesf" + tag, [128, 128], F32)
            dtb = sb(cx, st, "dtb" + tag, [128, 64], F32)
            Abc = sb(cx, st, "Abc" + tag, [128, 64], F32)
            Dbc = sb(cx, st, "Dbc" + tag, [128, 32], F32)
            xtk = [sb(cx, st, f"f2xtk{i}" + tag, [128, 2048], BF16) for i in range(2)]
            btk = [sb(cx, st, f"f2btk{i}" + tag, [128, 1024], BF16) for i in range(2)]
            bcs = [sb(cx, st, f"f2bct{i}" + tag, [128, 16, 128], BF16) for i in range(2)]
            dts = [sb(cx, st, f"f2dt{i}" + tag, [128, 64], F32) for i in range(2)]
            scs = [sb(cx, st, f"f2sc{i}" + tag, [128, 16, 64], F32) for i in range(2)]
            acss = [sb(cx, st, f"f2acs{i}" + tag, [128, 128], F32) for i in range(2)]
            hl = sb(cx, st, "f2hl" + tag, [128, 128], BF16)
            nhl = sb(cx, st, "f2nhl" + tag, [128, 128], BF16)
            hlTs = [sb(cx, st, f"f2hlT{i}" + tag, [128, 128], BF16) for i in range(2)]
            nhlTs = [sb(cx, st, f"f2nhlT{i}" + tag, [128, 128], BF16) for i in range(2)]
            CBTs = [sb(cx, st, f"f2CBT{i}" + tag, [128, 8, 128], BF16) for i in range(2)]
            xdds = [[sb(cx, st, f"f2xdd{j}_{i}" + tag, [128, 2048], BF16) for i in range(2)] for j in range(2)]
            identD = sb(cx, st, "f2identD" + tag, [128, 32, 128], BF16)
            seg = [sb(cx, st, f"f2seg{i}" + tag, [128, 4, 128], BF16) for i in range(4)]
            MT = [sb(cx, st, f"f2MT{i}" + tag, [128, 4, 128], BF16) for i in range(4)]
            Sf = sb(cx, st, "f2Sf" + tag, [128, 2048], F32)
            Sbf = sb(cx, st, "f2Sbf" + tag, [128, 2048], BF16)
            tmp = sb(cx, st, "f2tmp" + tag, [128, 2048], F32)
            yacc = sb(cx, st, "f2yacc" + tag, [128, 2048], F32)
            scbs = [sb(cx, st, f"f2scb{i}" + tag, [128, 64], F32) for i in range(2)]
            pp = ps(cx, st, "f2ps" + tag, [128, 8, 512], F32)
            rr = {}

            def r(name):
                if name not in rr:
                    rr[name] = Res("f2_" + name + tag)
                return rr[name]
            r_b = [r(f"b{i}") for i in range(8)]
            S.dma(S.sp, lambda h: h.dma_start(out=sel2[:, :, :].rearrange("p a b -> p (a b)"), in_=c_sel[:, :]), slot=r("sel2"), writes=[r("sel2")])
            S.dma(S.sp, lambda h: h.dma_start(out=tri[:, 0, :], in_=c_tri[0]), slot=r("tri"), writes=[r("tri")])
            S.dma(S.sp, lambda h: h.dma_start(out=tri[:, 1, :], in_=c_tri[1]), slot=r("tri"), writes=[r("tri")])
            S.dma(S.pool, lambda h: h.dma_start(out=nmask[:, 0, :], in_=c_tri[2]), slot=r("nmask"), writes=[r("nmask")])
            S.dma(S.pool, lambda h: h.dma_start(out=nmask[:, 1, :], in_=c_tri[3]), slot=r("nmask"), writes=[r("nmask")])
            S.op(S.pool, lambda h: h.memset(onesf[:, :], 1.0), writes=[r("onesf")])
            load_bcast_row(cx, S.sp, dtb, r("dtb"), dt_bias.rearrange("a b -> (a b)"))
            load_bcast_row(cx, S.sp, Abc, r("Abc"), A_log.rearrange("a b -> (a b)"))
            load_bcast_row(cx, S.sp, Dbc, r("Dbc"), Dp)
            S.op(S.act, lambda h: h.activation(Abc[:, :], Abc[:, :], AF.Exp), reads=[r("Abc")], writes=[r("Abc")])
            S.op(S.pool, lambda h: h.tensor_scalar(Abc[:, :], Abc[:, :], -1.0, None, ALU.mult), reads=[r("Abc")], writes=[r("Abc")])
            for hd_ in range(32):
                e_ = S.dve
                S.op(e_, (lambda hd_: lambda h: h.tensor_scalar(identD[:, hd_, :], cx.ident[:, :], Dbc[:, hd_:hd_ + 1], None, ALU.mult))(hd_),
                     reads=[r("Dbc")], writes=[r("identD")])
            cnt = {"c": 0, "dk": 0}

            def load_chunk(gt):
                i = cnt["c"] % 2
                cnt["c"] += 1
                rows = slice(gt * 128, (gt + 1) * 128)
                S.dma(S.sp, lambda h: h.dma_start(out=xtk[i][:, :], in_=xtok[rows, :]), slot=r(f"xtk{i}"), reads=[r_xtok[gt]], writes=[r(f"xtk{i}")])
                S.dma(S.sp, lambda h: h.dma_start(out=btk[i][:, :], in_=btok[rows, :]), slot=r(f"btk{i}"), reads=[r_btok[gt]], writes=[r(f"btk{i}")])
                S.dma(S.sp, lambda h: h.dma_start(out=bcs[i][:, :, :].rearrange("p a t -> p (a t)"), in_=bct[gt]),
                      slot=r(f"bcs{i}"), reads=[r_bct[gt]], writes=[r(f"bcs{i}")])
                S.dma(S.sp, lambda h: h.dma_start(out=dts[i][:, :], in_=dtr[rows, :]), slot=r(f"dts{i}"), reads=[r_dtr[gt]], writes=[r(f"dts{i}")])
                return i

            def prep_stage(gt, i, p, stage):
                rows = slice(gt * 128, (gt + 1) * 128)
                X, BC, DT = xtk[i], bcs[i], dts[i]
                rX, rBC, rDT = r(f"xtk{i}"), r(f"bcs{i}"), r(f"dts{i}")
                sc = scs[p]
                acs = acss[p]
                V = lambda k: sc[:, k, :]
                q = lambda n_: r(f"{n_}_{p}")
                if stage == 1:
                    S.op(S.dve, lambda h: h.tensor_tensor(V(0), DT[:, :], dtb[:, :], ALU.add), reads=[rDT, r("dtb")], writes=[q("u")])
                    S.op(S.act, lambda h: h.activation(V(1), V(0), AF.Abs), reads=[q("u")], writes=[q("au")])
                    S.op(S.act, lambda h: h.activation(V(2), V(1), AF.Exp, scale=-1.0), reads=[q("au")], writes=[q("e")])
                    S.op(S.act, lambda h: h.activation(V(3), V(2), AF.Ln, bias=1.0), reads=[q("e")], writes=[q("l")])
                    S.op(S.dve, lambda h: h.scalar_tensor_tensor(V(4), V(0), 0.0, V(3), ALU.max, ALU.add), reads=[q("u"), q("l")], writes=[q("dt")])
                    S.op(S.dve, lambda h: h.tensor_tensor(V(5), V(4), Abc[:, :], ALU.mult), reads=[q("dt"), r("Abc")], writes=[q("a")])
                elif stage == 2:
                    for d in range(2):
                        S.op(S.pe, (lambda d: lambda h: h.matmul(pp[:, 7, d * 32:(d + 1) * 32], tri[:, d, :], sc[:, 5, d * 32:(d + 1) * 32],
                                                                 start=True, stop=True))(d),
                             reads=[r("tri"), q("a")], writes=[r_b[7]])
                    S.op(S.pe, lambda h: h.matmul(pp[:, 7, 64:128], onesf[:, :], V(5), start=True, stop=True),
                         reads=[r("onesf"), q("a")], writes=[r_b[7]])
                elif stage == 3:
                    S.op(S.act, lambda h: h.activation(acs[:, :], pp[:, 7, 0:128], AF.Copy), reads=[r_b[7]], writes=[q("acs")])
                    S.op(S.act, lambda h: h.activation(V(6), acs[:, 0:64], AF.Exp), reads=[q("acs")], writes=[q("eacs")])
                    S.op(S.dve, lambda h: h.tensor_tensor(V(7), acs[:, 64:128], acs[:, 0:64], ALU.subtract), reads=[q("acs")], writes=[q("dd")])
                    S.op(S.act, lambda h: h.activation(V(8), V(7), AF.Exp), reads=[q("dd")], writes=[q("dte")])
                    S.op(S.dve, lambda h: h.tensor_tensor(V(9), V(4), V(8), ALU.mult), reads=[q("dt"), q("dte")], writes=[q("w")])
                    S.op(S.act, lambda h: h.activation(V(10), acs[:, 64:128], AF.Exp), reads=[q("acs")], writes=[q("cdb")])
                    S.op(S.dve, lambda h: h.tensor_copy(hl[:, 0:64], acs[:, 0:64]), reads=[q("acs")], writes=[r("hl")])
                    S.op(S.dve, lambda h: h.tensor_tensor(hl[:, 64:128], acs[:, 0:64], hl[:, 0:64], ALU.subtract), reads=[q("acs"), r("hl")], writes=[r("hl")])
                    S.op(S.act, lambda h: h.activation(V(11), V(4), AF.Ln), reads=[q("dt")], writes=[q("lnd")])
                    S.op(S.dve, lambda h: h.tensor_tensor(V(12), V(11), acs[:, 0:64], ALU.subtract), reads=[q("lnd"), q("acs")], writes=[q("gg")])
                    S.op(S.dve, lambda h: h.tensor_copy(nhl[:, 0:64], V(12)), reads=[q("gg")], writes=[r("nhl")])
                    S.op(S.dve, lambda h: h.tensor_tensor(nhl[:, 64:128], V(12), nhl[:, 0:64], ALU.subtract), reads=[q("gg"), r("nhl")], writes=[r("nhl")])
                elif stage == 4:
                    pst = pp[:, 7, :].bitcast(BF16)
                    S.op(S.pe, lambda h: h.transpose(pst[:, 256:384], hl[:, :], cx.ident[:, :]), reads=[r("hl")], writes=[r_b[7]])
                    S.op(S.pe, lambda h: h.transpose(pst[:, 384:512], nhl[:, :], cx.ident[:, :]), reads=[r("nhl")], writes=[r_b[7]])
                elif stage == 5:
                    pst = pp[:, 7, :].bitcast(BF16)
                    S.op(S.dve, lambda h: h.tensor_copy(hlTs[p][:, :], pst[:, 256:384]), reads=[r_b[7]], writes=[q("hlT")])
                    S.op(S.dve, lambda h: h.tensor_copy(nhlTs[p][:, :], pst[:, 384:512]), reads=[r_b[7]], writes=[q("nhlT")])
                    S.op(S.pool, lambda h: h.tensor_copy(scbs[p][:, 0:32], sc[:, 6, 32:64]), reads=[q("eacs")], writes=[q("scb")])
                    S.op(S.pool, lambda h: h.tensor_copy(scbs[p][:, 32:64], sc[:, 10, 32:64]), reads=[q("cdb"), q("scb")], writes=[q("scb")])
                    S.dma(S.sp, lambda h: h.dma_start(out=scb_d[rows, :], in_=scbs[p][:, :]), slot=q("scb"), reads=[q("scb")], writes=[r_scb[gt]])
                elif stage in (6, 7):
                    hf = stage - 6
                    for gg in range(4):
                        g = hf * 4 + gg
                        S.op(S.pe, (lambda g, gg: lambda h: h.matmul(pp[:, 7, gg * 128:(gg + 1) * 128], BC[:, g, :], BC[:, 8 + g, :],
                                                                     start=True, stop=True))(g, gg),
                             reads=[rBC], writes=[r_b[7]])
                    S.op(S.act, lambda h: h.activation(CBTs[p][:, hf * 4:(hf + 1) * 4, :],
                                                       pp[:, 7, :].rearrange("p (u t) -> p u t", u=4), AF.Copy),
                         reads=[r_b[7]], writes=[q("CBT")])
                elif stage == 8:
                    X3 = X[:, :].rearrange("p (a d) -> p a d", a=32)
                    for d in range(2):
                        e_ = S.dve if d == 0 else S.pool
                        S.op(e_, (lambda d: lambda h: h.tensor_tensor(xdds[p][d][:, :].rearrange("p (a d) -> p a d", a=32), X3,
                                                                      bc3(sc[:, 9, d * 32:(d + 1) * 32], 64), ALU.mult))(d),
                             reads=[rX, q("w")], writes=[q(f"xdd{d}")])
                    S.dma(S.sp, lambda h: h.dma_start(out=xddb_d[rows, :], in_=xdds[p][1][:, :]), slot=q("xdd1"), reads=[q("xdd1")], writes=[r_xddb[gt]])

            PREP_AT = {0: 1, 2: 2, 3: 3, 6: 4, 7: 5, 9: 6, 11: 7, 12: 8}

            def f2_chunk(gt, i, p, first, nxt):
                rows = slice(gt * 128, (gt + 1) * 128)
                X, Bk, BC = xtk[i], btk[i], bcs[i]
                rX, rB, rBC = r(f"xtk{i}"), r(f"btk{i}"), r(f"bcs{i}")
                sc = scs[p]
                q = lambda n_: r(f"{n_}_{p}")
                hlT, nhlT, CBT, xdd = hlTs[p], nhlTs[p], CBTs[p], xdds[p]
                for hd in range(32):
                    S.op(S.pe, (lambda hd: lambda h: h.matmul(pp[:, 2 + hd // 8, (hd % 8) * 64:(hd % 8 + 1) * 64], identD[:, hd, :],
                                                              X[:, hd * 64:(hd + 1) * 64], start=(hd % 8 == 0), stop=False))(hd),
                         reads=[r("identD"), rX], writes=[r_b[2 + hd // 8]])
                dbanks = [0, 1, 6]
                groups = [(d, g) for d in range(2) for g in range(8)]

                def dec(ix):
                    d, g = groups[ix]
                    k = ix % 3
                    b = dbanks[k]
                    for hh in range(4):
                        col = d * 32 + g * 4 + hh
                        o = pp[:, b, hh * 128:(hh + 1) * 128]
                        S.op(S.pe, (lambda o, col: lambda h: h.matmul(o, sel2[:, col, :], hlT[:, :], start=True, stop=False))(o, col),
                             reads=[r("sel2"), q("hlT")], writes=[r_b[b]])
                        S.op(S.pe, (lambda o, col: lambda h: h.matmul(o, nhlT[:, :], sel2[:, col, :], start=False, stop=False))(o, col),
                             reads=[r("sel2"), q("nhlT")], writes=[r_b[b]])
                        S.op(S.pe, (lambda o, d: lambda h: h.matmul(o, cx.ident[:, :], nmask[:, d, :], start=False, stop=True))(o, d),
                             reads=[r("nmask")], writes=[r_b[b]])
                    S.op(S.act, (lambda b, k: lambda h: h.activation(seg[k][:, :, :], pp[:, b, :].rearrange("p (u t) -> p u t", u=4), AF.Exp))(b, k),
                         reads=[r_b[b]], writes=[r(f"seg{k}")])
                    S.op(S.dve, (lambda k, g: lambda h: h.tensor_tensor(MT[k][:, :, :], seg[k][:, :, :],
                                                                         CBT[:, g, :].unsqueeze(1).broadcast_to([128, 4, 128]), ALU.mult))(k, g),
                         reads=[r(f"seg{k}"), q("CBT")], writes=[r(f"MT{k}")])

                def ymm(ix):
                    d, g = groups[ix]
                    k = ix % 3
                    for hh in range(4):
                        hd = g * 4 + hh
                        S.op(S.pe, (lambda k, hh, hd: lambda h: h.matmul(
                            pp[:, 2 + hd // 8, (hd % 8) * 64:(hd % 8 + 1) * 64], MT[k][:, hh, :], X[:, hd * 64:(hd + 1) * 64],
                            start=False, stop=(ix >= 8)))(k, hh, hd),
                            reads=[r(f"MT{k}"), rX], writes=[r_b[2 + hd // 8]])
                for ix in range(2):
                    dec(ix)
                for ix in range(16):
                    if nxt is not None and ix in PREP_AT:
                        prep_stage(nxt[0], nxt[1], nxt[2], PREP_AT[ix])
                    if ix + 2 < 16:
                        dec(ix + 2)
                    ymm(ix)
                for hf in range(2):
                    if not first:
                        for gg in range(4):
                            g = hf * 4 + gg
                            b = 6 + gg // 2
                            S.op(S.pe, (lambda g, b, gg: lambda h: h.matmul(pp[:, b, (gg % 2) * 256:(gg % 2 + 1) * 256], BC[:, 8 + g, :],
                                                                            Sbf[:, g * 256:(g + 1) * 256], start=True, stop=True))(g, b, gg),
                                 reads=[rBC, r("Sbf")], writes=[r_b[b]])
                        S.op(S.dve, (lambda hf: lambda h: h.tensor_tensor(
                            tmp[:, hf * 1024:(hf + 1) * 1024].rearrange("p (a d) -> p a d", a=16),
                            pp[:, 6:8, :].rearrange("p b (a d) -> p (b a) d", d=64),
                            bc3(sc[:, 6, hf * 16:(hf + 1) * 16], 64), ALU.mult))(hf),
                            reads=[r_b[6], r_b[7], q("eacs")], writes=[r("tmp")])
                        S.op(S.dve, (lambda hf: lambda h: h.tensor_tensor(
                            yacc[:, hf * 1024:(hf + 1) * 1024].rearrange("p (b c) -> p b c", b=2),
                            tmp[:, hf * 1024:(hf + 1) * 1024].rearrange("p (b c) -> p b c", b=2),
                            pp[:, 2 + 2 * hf:4 + 2 * hf, :], ALU.add))(hf),
                            reads=[r("tmp"), r_b[2 + 2 * hf], r_b[3 + 2 * hf]], writes=[r("yacc")])
                    else:
                        S.op(S.act, (lambda hf: lambda h: h.activation(
                            yacc[:, hf * 1024:(hf + 1) * 1024].rearrange("p (b c) -> p b c", b=2),
                            pp[:, 2 + 2 * hf:4 + 2 * hf, :], AF.Copy))(hf),
                            reads=[r_b[2 + 2 * hf], r_b[3 + 2 * hf]], writes=[r("yacc")])
                S.dma(S.sp, lambda h: h.dma_start(out=yacc_d[rows, :], in_=yacc[:, :]), slot=r("yacc"), reads=[r("yacc")], writes=[r_yacc[gt]])
                for hf in range(2):
                    for gg in range(4):
                        g = hf * 4 + gg
                        b = 6 + gg // 2
                        S.op(S.pe, (lambda g, b, gg: lambda h: h.matmul(pp[:, b, (gg % 2) * 256:(gg % 2 + 1) * 256], Bk[:, g * 128:(g + 1) * 128],
                                                                        xdd[0][:, g * 256:(g + 1) * 256], start=True, stop=True))(g, b, gg),
                             reads=[rB, q("xdd0")], writes=[r_b[b]])
                    sl_ = slice(hf * 1024, (hf + 1) * 1024)
                    if first:
                        S.op(S.dve, (lambda sl_: lambda h: h.tensor_copy(Sf[:, sl_].rearrange("p (b c) -> p b c", b=2), pp[:, 6:8, :]))(sl_),
                             reads=[r_b[6], r_b[7]], writes=[r("Sf")])
                    else:
                        S.op(S.dve, (lambda sl_, hf: lambda h: h.tensor_tensor(
                            tmp[:, sl_].rearrange("p (a d) -> p a d", a=16), Sf[:, sl_].rearrange("p (a d) -> p a d", a=16),
                            bc3(sc[:, 10, hf * 16:(hf + 1) * 16], 64), ALU.mult))(sl_, hf),
                            reads=[r("Sf"), q("cdb"), r("tmp")], writes=[r("tmp")])
                        S.op(S.dve, (lambda sl_: lambda h: h.tensor_tensor(Sf[:, sl_].rearrange("p (b c) -> p b c", b=2),
                                                                            tmp[:, sl_].rearrange("p (b c) -> p b c", b=2), pp[:, 6:8, :], ALU.add))(sl_),
                             reads=[r("tmp"), r_b[6], r_b[7]], writes=[r("Sf")])
                S.op(S.act, lambda h: h.activation(Sbf[:, :], Sf[:, :], AF.Copy), reads=[r("Sf")], writes=[r("Sbf")])

            kk = 0
            for (base, slen) in seq_list:
                ncs = slen // 128
                g0 = base // 128
                i = load_chunk(g0)
                for stg in range(1, 9):
                    prep_stage(g0, i, kk % 2, stg)
                for c in range(ncs):
                    inext = load_chunk(g0 + c + 1) if c + 1 < ncs else None
                    nxt = (g0 + c + 1, inext, (kk + 1) % 2) if inext is not None else None
                    f2_chunk(g0 + c, i, kk % 2, first=(c == 0), nxt=nxt)
                    i = inext
                    kk += 1
            S.barrier()

    _sweep_f2()

    def _sweep_b():
        with contextlib.ExitStack() as st:
            Wout = sb(cx, st, "Wout" + tag, [128, 16, D], BF16)
            gpost = sb(cx, st, "sgpost" + tag, [128, D], F32)
            ngb = sb(cx, st, "ngb" + tag, [128, 2048], F32)
            yin = [sb(cx, st, f"byacc{i}" + tag, [128, 2048], F32) for i in range(3)]
            zin = [sb(cx, st, f"bz{i}" + tag, [128, 2048], F32) for i in range(3)]
            xdb = [sb(cx, st, f"bxdd{i}" + tag, [128, 2048], BF16) for i in range(2)]
            btk = [sb(cx, st, f"bbtk{i}" + tag, [128, 1024], BF16) for i in range(2)]
            bcs = [sb(cx, st, f"bbct{i}" + tag, [128, 16, 128], BF16) for i in range(2)]
            scb = [sb(cx, st, f"bscb{i}" + tag, [128, 64], F32) for i in range(2)]
            xb3 = [sb(cx, st, f"bxb{i}" + tag, [128, D], F32) for i in range(4)]
            tmpS = sb(cx, st, "btmpS" + tag, [128, 2048], F32)
            Sb = sb(cx, st, "bSb" + tag, [128, 2048], F32)
            Sbb = sb(cx, st, "bSbb" + tag, [128, 2048], BF16)
            tmp = sb(cx, st, "btmp" + tag, [128, 2048], F32)
            yn2 = [sb(cx, st, f"byn{i}" + tag, [128, 2048], BF16) for i in range(2)]
            ynT = sb(cx, st, "bynT" + tag, [128, 16, 128], BF16)
            junk = sb(cx, st, "bjunk" + tag, [128, 512], BF16)
            tt = [sb(cx, st, f"btt{i}" + tag, [128, 512], F32) for i in range(2)]
            small = sb(cx, st, "bsmall" + tag, [128, 64], F32)
            pp = ps(cx, st, "bps" + tag, [128, 8, 512], F32)
            rr = {}

            def r(name):
                if name not in rr:
                    rr[name] = Res("b_" + name + tag)
                return rr[name]
            r_b = [r(f"b{i}") for i in range(8)]
            for k in range(16):
                S.dma(S.pool, (lambda k: lambda h: h.dma_start(out=Wout[:, k, :], in_=w_out[k * 128:(k + 1) * 128, :]))(k),
                      slot=r("Wout"), writes=[r("Wout")])
            load_bcast_row(cx, S.sp, gpost, r("gpost"), g_post)
            load_bcast_row(cx, S.sp, ngb, r("ngb"), ng)
            cnt = {"c": 0}

            def load_chunk(gt, kk):
                i = cnt["c"] % 2
                cnt["c"] += 1
                rows = slice(gt * 128, (gt + 1) * 128)
                y3 = kk % 3
                for (t, dsrc, rs, nm, ii) in ((yin, yacc_d, r_yacc, "yin", y3), (zin, zs, r_zs, "zin", y3), (xdb, xddb_d, r_xddb, "xdb", i),
                                              (btk, btok, r_btok, "btk", i), (scb, scb_d, r_scb, "scb", i)):
                    S.dma(S.sp, (lambda t, dsrc, ii: lambda h: h.dma_start(out=t[ii][:, :], in_=dsrc[rows, :]))(t, dsrc, ii),
                          slot=r(f"{nm}{ii}"), reads=[rs[gt]], writes=[r(f"{nm}{ii}")])
                S.dma(S.sp, lambda h: h.dma_start(out=bcs[i][:, :, :].rearrange("p a t -> p (a t)"), in_=bct[gt]),
                      slot=r(f"bcs{i}"), reads=[r_bct[gt]], writes=[r(f"bcs{i}")])
                x3 = kk % 4
                S.dma(S.sp, lambda h: h.dma_start(out=xb3[x3][:, :], in_=src[rows, :]),
                      slot=r(f"xb{x3}"), reads=[src_res[gt]], writes=[r(f"xb{x3}")])
                return i

            def b_stage12(gt, i, first, kk):
                y3 = kk % 3
                Y, Z, XD, Bk, BC, SC = yin[y3], zin[y3], xdb[i], btk[i], bcs[i], scb[i]
                rY, rZ, rXD, rB, rBC, rSC = (r(f"yin{y3}"), r(f"zin{y3}"), r(f"xdb{i}"), r(f"btk{i}"), r(f"bcs{i}"), r(f"scb{i}"))
                YN, rYN = yn2[kk % 2], r(f"yn{kk % 2}")
                if not first:
                    for hf in range(2):
                        for gg in range(4):
                            g = hf * 4 + gg
                            b = 2 * hf + gg // 2
                            S.op(S.pe, (lambda g, b, gg: lambda h: h.matmul(pp[:, b, (gg % 2) * 256:(gg % 2 + 1) * 256], BC[:, 8 + g, :],
                                                                            Sbb[:, g * 256:(g + 1) * 256], start=True, stop=True))(g, b, gg),
                                 reads=[rBC, r("Sbb")], writes=[r_b[b]])
                for g in range(8):
                    b = 4 + g // 2
                    S.op(S.pe, (lambda g, b: lambda h: h.matmul(pp[:, b, (g % 2) * 256:(g % 2 + 1) * 256], Bk[:, g * 128:(g + 1) * 128],
                                                                XD[:, g * 256:(g + 1) * 256], start=True, stop=True))(g, b),
                         reads=[rB, rXD], writes=[r_b[b]])
                if first:
                    S.op(S.dve, lambda h: h.tensor_copy(Sb[:, :].rearrange("p (b c) -> p b c", b=4), pp[:, 4:8, :]),
                         reads=[r_b[4], r_b[5], r_b[6], r_b[7]], writes=[r("Sb")])
                else:
                    S.op(S.dve, lambda h: h.tensor_tensor(tmpS[:, :].rearrange("p (a d) -> p a d", a=32), Sb[:, :].rearrange("p (a d) -> p a d", a=32),
                                                          bc3(SC[:, 32:64], 64), ALU.mult),
                         reads=[r("Sb"), rSC], writes=[r("tmpS")])
                    S.op(S.dve, lambda h: h.tensor_tensor(Sb[:, :].rearrange("p (b c) -> p b c", b=4), tmpS[:, :].rearrange("p (b c) -> p b c", b=4),
                                                          pp[:, 4:8, :], ALU.add),
                         reads=[r("tmpS"), r_b[4], r_b[5], r_b[6], r_b[7]], writes=[r("Sb")])
                S.op(S.act, lambda h: h.activation(Sbb[:, :], Sb[:, :], AF.Copy), reads=[r("Sb")], writes=[r("Sbb")])
                if not first:
                    S.op(S.dve, lambda h: h.tensor_tensor(tmp[:, :].rearrange("p (a d) -> p a d", a=32),
                                                          pp[:, 0:4, :].rearrange("p b (a d) -> p (b a) d", d=64),
                                                          bc3(SC[:, 0:32], 64), ALU.mult),
                         reads=[r_b[0], r_b[1], r_b[2], r_b[3], rSC], writes=[r("tmp")])
                    S.op(S.pool, lambda h: h.tensor_tensor(Y[:, :], Y[:, :], tmp[:, :], ALU.add), reads=[rY, r("tmp")], writes=[rY])

            def b_gate(gt, i, first, kk):
                y3 = kk % 3
                Y, Z = yin[y3], zin[y3]
                rY, rZ = r(f"yin{y3}"), r(f"zin{y3}")
                YN, rYN = yn2[kk % 2], r(f"yn{kk % 2}")
                S.op(S.act, lambda h: h.activation(Z[:, :], Z[:, :], AF.Silu), reads=[rZ], writes=[rZ])
                S.op(S.dve, lambda h: h.tensor_tensor(Y[:, :], Y[:, :], Z[:, :], ALU.mult), reads=[rY, rZ], writes=[rY])
                for g in range(8):
                    S.op(S.act, (lambda g: lambda h: h.activation(Z[:, g * 256:(g + 1) * 256], Y[:, g * 256:(g + 1) * 256], AF.Square,
                                                                  accum_out=small[:, g:g + 1]))(g),
                         reads=[rY, rZ], writes=[rZ, r(f"gss{g}")])
                S.op(S.pool, lambda h: h.tensor_scalar(small[:, 8:16], small[:, 0:8], 1.0 / 256, EPS, ALU.mult, ALU.add),
                     reads=[r(f"gss{g}") for g in range(8)], writes=[r("gv")])
                S.op(S.pool, lambda h: h.tensor_tensor(small[:, 16:24], small[:, 8:16], cx.mhalf[:, 0:1].broadcast_to([128, 8]), ALU.pow),
                     reads=[r("gv")], writes=[r("grstd")])
                for g in range(8):
                    S.op(S.dve, (lambda g: lambda h: h.scalar_tensor_tensor(YN[:, g * 256:(g + 1) * 256], Y[:, g * 256:(g + 1) * 256],
                                                                            small[:, 16 + g:17 + g], ngb[:, g * 256:(g + 1) * 256],
                                                                            ALU.mult, ALU.mult))(g),
                         reads=[rY, r("grstd"), r("ngb")], writes=[rYN])

            def b_stage3(gt, kk):
                rows = slice(gt * 128, (gt + 1) * 128)
                YN, rYN = yn2[kk % 2], r(f"yn{kk % 2}")
                XB, rXB = xb3[kk % 4], r(f"xb{kk % 4}")
                for hf in range(2):
                    pst = pp[:, hf, :].bitcast(BF16)
                    for u in range(8):
                        k = hf * 8 + u
                        S.op(S.pe, (lambda pst, u, k: lambda h: h.transpose(pst[:, u * 128:(u + 1) * 128], YN[:, k * 128:(k + 1) * 128], cx.ident[:, :]))(pst, u, k),
                             reads=[rYN], writes=[r_b[hf]])
                    S.op(S.act, (lambda pst, hf: lambda h: h.activation(ynT[:, hf * 8:(hf + 1) * 8, :], pst.rearrange("p (k c) -> p k c", k=8), AF.Copy))(pst, hf),
                         reads=[r_b[hf]], writes=[r("ynT")])

            def b_stage3b(gt, kk):
                rows = slice(gt * 128, (gt + 1) * 128)
                XB, rXB = xb3[kk % 4], r(f"xb{kk % 4}")
                for half in range(2):
                    b = 2 + half
                    for k in range(16):
                        S.op(S.pe, (lambda b, k, half: lambda h: h.matmul(pp[:, b, :], ynT[:, k, :], Wout[:, k, half * 512:(half + 1) * 512],
                                                                          start=(k == 0), stop=(k == 15)))(b, k, half),
                             reads=[r("ynT"), r("Wout")], writes=[r_b[b]])
                    S.op(S.act, (lambda b, half: lambda h: h.activation(junk[:, :], pp[:, b, :], AF.Square, accum_out=small[:, 32 + half:33 + half]))(b, half),
                         reads=[r_b[b]], writes=[r("junk"), r(f"ss2{half}")])
                S.op(S.pool, lambda h: h.tensor_tensor(small[:, 34:35], small[:, 32:33], small[:, 33:34], ALU.add),
                     reads=[r("ss20"), r("ss21")], writes=[r("sss")])
                emit_rstd(cx, small[:, 34:35], small[:, 35:36], small[:, 36:37], r("sss"), r("v2"), r("rstd2"), D)
                for half in range(2):
                    b = 2 + half
                    S.op(S.dve, (lambda b, half: lambda h: h.scalar_tensor_tensor(tt[half][:, :], pp[:, b, :], small[:, 36:37],
                                                                                  gpost[:, half * 512:(half + 1) * 512], ALU.mult, ALU.mult))(b, half),
                         reads=[r_b[b], r("rstd2"), r("gpost")], writes=[r(f"tt{half}")])
                    S.op(S.pool, (lambda half: lambda h: h.tensor_tensor(XB[:, half * 512:(half + 1) * 512], tt[half][:, :],
                                                                         XB[:, half * 512:(half + 1) * 512], ALU.add))(half),
                         reads=[r(f"tt{half}"), rXB], writes=[rXB])
                S.dma(S.sp, lambda h: h.dma_start(out=dst[rows, :], in_=XB[:, :]), slot=rXB, reads=[rXB], writes=[dst_res[gt]])

            chunks = []
            for (base, slen) in seq_list:
                ncs = slen // 128
                g0 = base // 128
                for c in range(ncs - 1, -1, -1):
                    chunks.append((g0 + c, c == ncs - 1))
            nchunks = len(chunks)
            slots = {}
            slots[0] = load_chunk(chunks[0][0], 0)
            for kk in range(nchunks + 2):
                if kk + 1 < nchunks:
                    slots[kk + 1] = load_chunk(chunks[kk + 1][0], kk + 1)
                if kk < nchunks:
                    b_stage12(chunks[kk][0], slots[kk], first=chunks[kk][1], kk=kk)
                if 0 <= kk - 2 < nchunks:
                    b_stage3(chunks[kk - 2][0], kk - 2)
                if 0 <= kk - 1 < nchunks:
                    b_gate(chunks[kk - 1][0], slots[kk - 1], first=chunks[kk - 1][1], kk=kk - 1)
                if 0 <= kk - 2 < nchunks:
                    b_stage3b(chunks[kk - 2][0], kk - 2)
            S.barrier()
    _sweep_b()


_NC_CACHE = {}


def run_encoder(xs_per_core, weights, seqs):
    key = tuple(seqs)
    if key not in _NC_CACHE:
        _NC_CACHE[key] = build_program(list(seqs))
    nc = _NC_CACHE[key]
    cst = consts()
    in_maps = []
    for x in xs_per_core:
        m = {"x": np.ascontiguousarray(x, dtype=np.float32)}
        m.update(weights)
        m.update(cst)
        in_maps.append(m)
    res = run_bass_kernel_spmd(nc, in_maps, core_ids=list(range(len(in_maps))))
    return [r["y"] for r in res.results]


_WNAMES = ("norm_g", "ffn_w_gate", "ffn_w_up", "ffn_w_down", "ssd_w_in", "ssd_conv_w", "ssd_conv_b", "ssd_dt_bias",
           "ssd_A_log", "ssd_D", "ssd_norm_g", "ssd_w_out", "attn_w_qkv", "attn_sink", "attn_w_out", "rel_bias")


def kernel(x_prompt, x_sample, **w):
    x_prompt = np.asarray(x_prompt, dtype=np.float32)
    x_sample = np.asarray(x_sample, dtype=np.float32)
    weights = {k: np.ascontiguousarray(np.asarray(w[k], dtype=np.float32)) for k in _WNAMES}
    nb, sp, _ = x_prompt.shape
    _, ss, _ = x_sample.shape
    assert nb == 8 and x_sample.shape[0] == 8
    xs = [np.concatenate([x_prompt[c], x_sample[c]], axis=0) for c in range(nb)]
    ys = run_encoder(xs, weights, (sp, ss))
    y_prompt = np.stack([ys[c][:sp] for c in range(nb)], axis=0)
    y_sample = np.stack([ys[c][sp:] for c in range(nb)], axis=0)
    return (y_prompt, y_sample)
```

```python
import contextlib
import numpy as np
import ml_dtypes
import concourse.bass as bass
import concourse.mybir as mybir
from concourse.bass_utils import run_bass_kernel_spmd

F32 = mybir.dt.float32
BF16 = mybir.dt.bfloat16
AF = mybir.ActivationFunctionType
ALU = mybir.AluOpType
AX = mybir.AxisListType

D = 1024
DFF = 2816
NFC = DFF // 128
EPS = 1e-6
DEPTH = 4
D_INNER = 2048
SSD_HEADS = 32
N_GROUPS = 8
D_STATE = 128
GN = 1024
CONV_DIM = 4096
D_IN_PROJ = 6208
N_HEADS = 16
N_KV = 4
HD = 64
QKV_DIM = 1536
N_BUCKETS = 32
NEG = -30000.0

SAME_ENGINE_SYNC = True


class Res:
    __slots__ = ("name", "last_w", "reads", "dsem", "dcnt")

    def __init__(self, name):
        self.name = name
        self.last_w = None
        self.reads = {}
        self.dsem = None
        self.dcnt = 0


class Eng:
    def __init__(self, name, kind, handle):
        self.name = name
        self.kind = kind
        self.h = handle
        self.sem = None
        self.tick = 0
        self.ops = []
        self.seen = {}


class Sched:
    def __init__(self, nc, stack):
        self.nc = nc
        self.stack = stack
        self.sems = {}
        self.nsem = 0
        self.pe = self._eng("pe", "pe", nc.tensor)
        self.act = self._eng("act", "act", nc.scalar)
        self.dve = self._eng("dve", "dve", nc.vector)
        self.pool = self._eng("pool", "pool", nc.gpsimd)
        self.sp = self._eng("sp", "sp", nc.sync)
        self.engines = [self.pe, self.act, self.dve, self.pool, self.sp]
        self.dma_slots = []
        self.bar_done = {}
        self.free_dsems = []

    def _newsem(self, name):
        s = self.stack.enter_context(self.nc.semaphore(name))
        self.nsem += 1
        sid = self.nsem
        self.sems[sid] = s
        return sid

    def _eng(self, name, kind, handle):
        e = Eng(name, kind, handle)
        e.sem = self._newsem("sem_" + name)
        return e

    def _deps(self, reads, writes):
        deps = {}

        def add(ev):
            if ev is None:
                return
            s, v = ev
            if deps.get(s, 0) < v:
                deps[s] = v
        for r in reads:
            add(r.last_w)
        for w in writes:
            add(w.last_w)
            for ev in w.reads.items():
                add(ev)
        return deps

    def _waits(self, eng, deps):
        waits = []
        for s, v in deps.items():
            if s == eng.sem and (eng.kind in ("pe", "sp") or not SAME_ENGINE_SYNC):
                continue
            if eng.seen.get(s, 0) >= v:
                continue
            eng.seen[s] = v
            waits.append((s, v))
        return waits

    def _commit(self, ev, reads, writes):
        for r in reads:
            if r.reads.get(ev[0], 0) < ev[1]:
                r.reads[ev[0]] = ev[1]
        for w in writes:
            w.last_w = ev
            w.reads = {}

    def op(self, eng, fn, reads=(), writes=()):
        deps = self._deps(reads, writes)
        waits = self._waits(eng, deps)
        eng.tick += 1
        ev = (eng.sem, eng.tick)
        eng.ops.append((waits, fn, eng.sem, 1))
        self._commit(ev, reads, writes)

    def dma(self, eng, fn, slot, reads=(), writes=()):
        if slot.dsem is None:
            if self.free_dsems:
                slot.dsem, slot.dcnt = self.free_dsems.pop()
            else:
                slot.dsem = self._newsem("d%d" % self.nsem)
            self.dma_slots.append(slot)
        deps = self._deps(reads, writes)
        waits = self._waits(eng, deps)
        slot.dcnt += 16
        ev = (slot.dsem, slot.dcnt)
        eng.ops.append((waits, fn, slot.dsem, 16))
        self._commit(ev, reads, writes)

    def barrier(self):
        evs = {}
        for e in self.engines:
            if e.kind != "sp" and e.tick > 0:
                evs[e.sem] = e.tick
        for sl in self.dma_slots:
            if sl.dcnt > self.bar_done.get(sl.dsem, 0):
                evs[sl.dsem] = sl.dcnt
                self.bar_done[sl.dsem] = sl.dcnt
        for e in self.engines:
            waits = []
            for s, v in evs.items():
                if s == e.sem and e.kind in ("pe", "sp"):
                    continue
                if e.seen.get(s, 0) >= v:
                    continue
                e.seen[s] = v
                waits.append((s, v))
            if waits:
                e.ops.append((waits, None, None, 0))
        for sl in self.dma_slots:
            self.free_dsems.append((sl.dsem, sl.dcnt))
            sl.dsem = None
        self.dma_slots = []

    def finish(self, out_res):
        deps = {}
        for r in out_res:
            if r.last_w is not None:
                s, v = r.last_w
                deps[s] = max(deps.get(s, 0), v)
        self.final_waits = list(deps.items())

    def emit(self):
        nc = self.nc
        sems = self.sems
        final_waits = getattr(self, "final_waits", [])

        def replay(e, h):
            for waits, fn, isem, inc in e.ops:
                for s, v in waits:
                    h.wait_ge(sems[s], v)
                if fn is not None:
                    fn(h).then_inc(sems[isem], inc)

        with nc.Block() as block:
            @block.tensor
            def _(h):
                replay(self.pe, h)

            @block.scalar
            def _(h):
                replay(self.act, h)

            @block.vector
            def _(h):
                replay(self.dve, h)

            @block.gpsimd
            def _(h):
                replay(self.pool, h)

            @block.sync
            def _(h):
                replay(self.sp, h)
                for s, v in final_waits:
                    h.wait_ge(sems[s], v)


class Ctx:
    pass


def sb(cx, stack, name, shape, dtype):
    t = stack.enter_context(cx.nc.sbuf_tensor(name, list(shape), dtype))
    return t


def ps(cx, stack, name, shape, dtype):
    t = stack.enter_context(cx.nc.psum_tensor(name, list(shape), dtype))
    return t


def emit_rstd(cx, ss_ap, v_ap, rstd_ap, r_ss, r_v, r_rstd, n):
    S = cx.S
    S.op(S.pool, lambda h: h.tensor_scalar(v_ap, ss_ap, 1.0 / n, EPS, ALU.mult, ALU.add),
         reads=[r_ss], writes=[r_v])
    S.op(S.pool, lambda h: h.tensor_tensor(rstd_ap, v_ap, cx.mhalf[:, 0:1], ALU.pow),
         reads=[r_v], writes=[r_rstd])


def load_bcast_row(cx, eng, dst_tile, dst_res, src_row_ap):
    S = cx.S
    S.dma(eng, lambda h: h.dma_start(out=dst_tile[:, :], in_=src_row_ap.partition_broadcast(128)),
          slot=dst_res, writes=[dst_res])


def ffn_phase(cx, wg, wu, wd, g_pre, g_post, src, dst, src_res, dst_res, ntok, tag):
    S, nc = cx.S, cx.nc
    T = cx.ffn_T
    NS = T // 128
    ntiles = ntok // T
    assert ntok % T == 0
    with contextlib.ExitStack() as st:
        Wg = sb(cx, st, "Wg" + tag, [128, 8, DFF], BF16)
        Wu = sb(cx, st, "Wu" + tag, [128, 8, DFF], BF16)
        Wd = sb(cx, st, "Wd" + tag, [128, NFC, D], BF16)
        gpre = sb(cx, st, "gpre" + tag, [128, D], F32)
        gpost = sb(cx, st, "gpost" + tag, [128, D], F32)
        xa = [sb(cx, st, f"xa{i}" + tag, [128, D], F32) for i in range(2)]
        xb = [sb(cx, st, f"xb{i}" + tag, [128, D], F32) for i in range(2)]
        hb = [sb(cx, st, f"hb{i}" + tag, [128, D], BF16) for i in range(4)]
        hT = sb(cx, st, "hT" + tag, [128, 8, T], BF16)
        actT = sb(cx, st, "actT" + tag, [128, NFC, T], BF16)
        junk = sb(cx, st, "junk" + tag, [128, D], BF16)
        sg = [sb(cx, st, f"sg{i}" + tag, [128, T], BF16) for i in range(2)]
        tt = [sb(cx, st, f"tt{i}" + tag, [128, 512], F32) for i in range(2)]
        small = sb(cx, st, "small" + tag, [128, 64], F32)
        psm = ps(cx, st, "psm" + tag, [128, 7, 512], F32)
        pst2 = [ps(cx, st, "pst" + tag, [128, D], BF16)] * 2

        r_Wg = [Res(f"Wg{k}" + tag) for k in range(8)]
        r_Wu = [Res(f"Wu{k}" + tag) for k in range(8)]
        r_Wd = [Res(f"Wd{k}" + tag) for k in range(NFC)]
        r_gpre, r_gpost = Res("gpre" + tag), Res("gpost" + tag)
        r_xa = [Res(f"xa{i}" + tag) for i in range(2)]
        r_xb = [Res(f"xb{i}" + tag) for i in range(2)]
        r_hb = [Res(f"hb{i}" + tag) for i in range(4)]
        r_hT = [Res(f"hT{s}" + tag) for s in range(NS)]
        r_act = [Res(f"act{f}" + tag) for f in range(NFC)]
        r_junk = Res("junk" + tag)
        r_sg = [Res(f"sg{i}" + tag) for i in range(2)]
        r_tt = [Res(f"tt{i}" + tag) for i in range(2)]
        r_bank = [Res(f"bank{i}" + tag) for i in range(7)]
        r_pst2 = [Res("pst" + tag)] * 2
        r_sm = [Res(f"sm{i}" + tag) for i in range(64)]

        for k in range(8):
            S.dma(S.pool, (lambda k: lambda h: h.dma_start(out=Wg[:, k, :], in_=wg[k * 128:(k + 1) * 128, :]))(k),
                  slot=r_Wg[k], writes=[r_Wg[k]])
            S.dma(S.pool, (lambda k: lambda h: h.dma_start(out=Wu[:, k, :], in_=wu[k * 128:(k + 1) * 128, :]))(k),
                  slot=r_Wu[k], writes=[r_Wu[k]])
        for f in range(NFC):
            S.dma(S.pool, (lambda f: lambda h: h.dma_start(out=Wd[:, f, :], in_=wd[f * 128:(f + 1) * 128, :]))(f),
                  slot=r_Wd[f], writes=[r_Wd[f]])
        load_bcast_row(cx, S.sp, gpre, r_gpre, g_pre)
        load_bcast_row(cx, S.sp, gpost, r_gpost, g_post)
        S.op(S.pool, lambda h: h.tensor_scalar(gpost[:, :], gpost[:, :], 0.5, None, ALU.mult),
             reads=[r_gpost], writes=[r_gpost])

        cnt = {"sub": 0, "gu": 0, "dn": 0, "ep": 0}

        def prologue_pre(t, only=None):
            for s in range(NS):
                if only is not None and s != only:
                    continue
                i = cnt["sub"] % 2
                cnt["sub"] += 1
                hi_ = s % 4
                gt = t * NS + s
                rows = slice(gt * 128, (gt + 1) * 128)
                S.dma(S.sp, (lambda i, rows: lambda h: h.dma_start(out=xa[i][:, :], in_=src[rows, :]))(i, rows),
                      slot=r_xa[i], reads=[src_res[gt]], writes=[r_xa[i]])
                S.op(S.act, (lambda i: lambda h: h.activation(junk[:, :], xa[i][:, :], AF.Square,
                                                              accum_out=small[:, i:i + 1]))(i),
                     reads=[r_xa[i]], writes=[r_junk, r_sm[i]])
                emit_rstd(cx, small[:, i:i + 1], small[:, 2 + i:3 + i], small[:, 4 + i:5 + i],
                          r_sm[i], r_sm[2 + i], r_sm[4 + i], D)
                S.op(S.dve, (lambda i, hi_: lambda h: h.scalar_tensor_tensor(
                    hb[hi_][:, :], xa[i][:, :], small[:, 4 + i:5 + i], gpre[:, :], ALU.mult, ALU.mult))(i, hi_),
                    reads=[r_xa[i], r_sm[4 + i], r_gpre], writes=[r_hb[hi_]])

        def prologue_post(t):
            for s in range(NS):
                hi_ = s % 4
                if s == 0:
                    pst, r_pst = pst2[0], r_pst2[0]
                else:
                    pst, r_pst = psm[:, 3 + s, :].bitcast(BF16), r_bank[3 + s]
                for k in range(8):
                    S.op(S.pe, (lambda hi_, k, pst: lambda h: h.transpose(
                        pst[:, k * 128:(k + 1) * 128], hb[hi_][:, k * 128:(k + 1) * 128], cx.ident[:, :]))(hi_, k, pst),
                        reads=[r_hb[hi_]], writes=[r_pst])
                S.op(S.act, (lambda s, pst: lambda h: h.activation(
                    hT[:, :, s * 128:(s + 1) * 128], pst.rearrange("p (k c) -> p k c", k=8), AF.Copy))(s, pst),
                    reads=[r_pst], writes=[r_hT[s]])

        def main(t, hook=None):
            for f in range(NFC):
                if hook is not None and f in (3, 7, 11, 15):
                    hook((f - 3) // 4)
                q = cnt["gu"] % 2
                cnt["gu"] += 1
                bg, bu = 2 * q, 2 * q + 1
                for (W, rW, b) in ((Wg, r_Wg, bg), (Wu, r_Wu, bu)):
                    for k in range(8):
                        S.op(S.pe, (lambda W, b, k, f: lambda h: h.matmul(
                            psm[:, b, 0:T], W[:, k, f * 128:(f + 1) * 128], hT[:, k, :],
                            start=(k == 0), stop=(k == 7)))(W, b, k, f),
                            reads=[rW[k]] + r_hT, writes=[r_bank[b]])
                S.op(S.act, (lambda q, bg: lambda h: h.activation(sg[q][:, :], psm[:, bg, 0:T], AF.Silu))(q, bg),
                     reads=[r_bank[bg]], writes=[r_sg[q]])
                S.op(S.dve, (lambda q, bu, f: lambda h: h.tensor_tensor(
                    actT[:, f, :], sg[q][:, :], psm[:, bu, 0:T], ALU.mult))(q, bu, f),
                    reads=[r_sg[q], r_bank[bu]], writes=[r_act[f]])

        def down(t):
            for s in range(NS):
                gt = t * NS + s
                rows = slice(gt * 128, (gt + 1) * 128)
                j = cnt["ep"] % 2
                cnt["ep"] += 1
                S.dma(S.sp, (lambda j, rows: lambda h: h.dma_start(out=xb[j][:, :], in_=src[rows, :]))(j, rows),
                      slot=r_xb[j], reads=[src_res[gt]], writes=[r_xb[j]])
                banks = []
                for half in range(2):
                    b = 4 + cnt["dn"] % 3
                    cnt["dn"] += 1
                    banks.append(b)
                    for f in range(NFC):
                        S.op(S.pe, (lambda b, f, s, half: lambda h: h.matmul(
                            psm[:, b, :], actT[:, f, s * 128:(s + 1) * 128], Wd[:, f, half * 512:(half + 1) * 512],
                            start=(f == 0), stop=(f == NFC - 1)))(b, f, s, half),
                            reads=[r_act[f], r_Wd[f]], writes=[r_bank[b]])
                    c = 8 + 2 * j + half
                    S.op(S.act, (lambda b, c: lambda h: h.activation(junk[:, 0:512], psm[:, b, :], AF.Square,
                                                                     accum_out=small[:, c:c + 1]))(b, c),
                         reads=[r_bank[b]], writes=[r_junk, r_sm[c]])
                c0 = 8 + 2 * j
                S.op(S.pool, (lambda c0, j: lambda h: h.tensor_tensor(
                    small[:, 16 + j:17 + j], small[:, c0:c0 + 1], small[:, c0 + 1:c0 + 2], ALU.add))(c0, j),
                    reads=[r_sm[c0], r_sm[c0 + 1]], writes=[r_sm[16 + j]])
                emit_rstd(cx, small[:, 16 + j:17 + j], small[:, 12 + j:13 + j], small[:, 14 + j:15 + j],
                          r_sm[16 + j], r_sm[12 + j], r_sm[14 + j], D)
                for half in range(2):
                    b = banks[half]
                    u = half
                    S.op(S.dve, (lambda b, u, j, half: lambda h: h.scalar_tensor_tensor(
                        tt[u][:, :], psm[:, b, :], small[:, 14 + j:15 + j], gpost[:, half * 512:(half + 1) * 512],
                        ALU.mult, ALU.mult))(b, u, j, half),
                        reads=[r_bank[b], r_sm[14 + j], r_gpost], writes=[r_tt[u]])
                    S.op(S.pool, (lambda u, j, half: lambda h: h.tensor_tensor(
                        xb[j][:, half * 512:(half + 1) * 512], tt[u][:, :], xb[j][:, half * 512:(half + 1) * 512],
                        ALU.add))(u, j, half),
                        reads=[r_tt[u], r_xb[j]], writes=[r_xb[j]])
                S.dma(S.sp, (lambda j, rows: lambda h: h.dma_start(out=dst[rows, :], in_=xb[j][:, :]))(j, rows),
                      slot=r_xb[j], reads=[r_xb[j]], writes=[dst_res[gt]])

        prologue_pre(0)
        prologue_post(0)
        for t in range(ntiles):
            if t + 1 < ntiles:
                main(t, hook=(lambda s, t=t: prologue_pre(t + 1, only=s)))
                prologue_post(t + 1)
            else:
                main(t)
            down(t)
        S.barrier()


def build_program(seqs, plan=None, ffn_T=512):
    ntok = sum(seqs)
    nc = bass.Bass("TRN2", target_bir_lowering=False)
    cx = Ctx()
    cx.nc = nc
    cx.ffn_T = ffn_T
    cx.seqs = seqs
    cx.scr = {}
    tens = {}

    def dram(name, shape, kind="ExternalInput", dtype=F32):
        if name not in tens:
            tens[name] = nc.dram_tensor(name, list(shape), dtype, kind=kind).ap()
        return tens[name]
    cx.dram = dram
    if plan is None:
        plan = []
        for i in range(DEPTH):
            plan.append(("ffn", i, 0))
            plan.append(("ssd", i // 2) if i % 2 == 0 else ("attn", i // 2))
            plan[-1] = plan[-1] + (i,)
            plan.append(("ffn", i, 1))
    x_in = dram("x", [ntok, D])
    y_out = dram("y", [ntok, D], "ExternalOutput")
    norm_g = dram("norm_g", [DEPTH, 6, D])
    c_ident = dram("c_ident", [128, 128])
    xres = dram("xres", [ntok, D], "Internal")
    nt128 = ntok // 128
    r_x = [Res(f"xin{t}") for t in range(nt128)]
    r_xres = [Res(f"xres{t}") for t in range(nt128)]
    r_y = [Res(f"y{t}") for t in range(nt128)]
    seq_list = []
    b0 = 0
    for sl in seqs:
        seq_list.append((b0, sl))
        b0 += sl

    with contextlib.ExitStack() as top:
        S = Sched(nc, top)
        cx.S = S
        cx.ident = sb(cx, top, "ident", [128, 128], BF16)
        cx.mhalf = sb(cx, top, "mhalf", [128, 1], F32)
        r_ident, r_mhalf = Res("ident"), Res("mhalf")
        S.dma(S.pool, lambda h: h.dma_start(out=cx.ident[:, :], in_=c_ident[:, :]), slot=r_ident, writes=[r_ident])
        S.op(S.pool, lambda h: h.memset(cx.mhalf[:, :], -0.5), writes=[r_mhalf])
        S.barrier()

        for pi, p in enumerate(plan):
            src, src_res = (x_in, r_x) if pi == 0 else (xres, r_xres)
            dst, dst_res = (y_out, r_y) if pi == len(plan) - 1 else (xres, r_xres)
            tag = f"_p{pi}"
            if p[0] == "ffn":
                _, i, w = p
                wg = dram("ffn_w_gate", [DEPTH, 2, D, DFF])
                wu = dram("ffn_w_up", [DEPTH, 2, D, DFF])
                wd = dram("ffn_w_down", [DEPTH, 2, DFF, D])
                ffn_phase(cx, wg[i, w], wu[i, w], wd[i, w], norm_g[i, 4 * w], norm_g[i, 4 * w + 1],
                          src, dst, src_res, dst_res, ntok, tag)
            elif p[0] == "attn":
                _, j, i = p
                attn_phase(cx, dram("attn_w_qkv", [2, D, QKV_DIM])[j], dram("attn_w_out", [2, D, D])[j],
                           dram("attn_sink", [2, N_HEADS])[j], norm_g[i, 2], norm_g[i, 3],
                           src, dst, src_res, dst_res, seq_list, tag)
            elif p[0] == "ssd":
                _, j, i = p
                ssd_phase(cx, j, norm_g[i, 2], norm_g[i, 3], src, dst, src_res, dst_res, seq_list, tag)
        S.finish(r_y)
        S.emit()
    return nc


def _t5_bucket_np(rel):
    half, max_exact = 16, 8
    ret = np.where(rel > 0, half, 0)
    n = np.abs(rel)
    large = max_exact + (np.log(np.maximum(n, 1).astype(np.float32) / np.float32(max_exact))
                         / np.float32(np.log(128 / max_exact)) * np.float32(half - max_exact)).astype(np.int32)
    large = np.minimum(large, half - 1)
    return ret + np.where(n < max_exact, n, large)


def consts():
    i = np.arange(512)
    rel = i - 255
    inwin = np.abs(rel) <= 128
    bucket = _t5_bucket_np(rel)
    oh = np.zeros((33, 512), np.float32)
    oh[bucket[inwin], i[inwin]] = 1.0
    oh[32, ~inwin] = 1.0
    u = np.arange(128)[:, None]
    l = np.arange(128)[None, :]
    tri_f = (u <= l).astype(np.float32)
    tri_b = (u >= l).astype(np.float32)
    nm_f = np.where(l >= u, 0.0, NEG).astype(np.float32)
    nm_b = np.where(l <= u, 0.0, NEG).astype(np.float32)
    sel2 = np.zeros((128, 64, 128), np.float32)
    for k in range(128):
        sel2[k, k % 64, :] = 1.0
    return {"c_ident": np.eye(128, dtype=np.float32),
            "c_anti": np.ascontiguousarray(np.eye(128, dtype=np.float32)[::-1]),
            "c_onehot": oh,
            "c_tri": np.stack([tri_f, tri_b, nm_f, nm_b]),
            "c_sel2": sel2.reshape(128, 64 * 128).astype(ml_dtypes.bfloat16)}


def bias_setup(cx, rel_bias, c_onehot, c_anti, tvec, tag=""):
    S, nc = cx.S, cx.nc
    with contextlib.ExitStack() as st:
        rb = sb(cx, st, "rb33" + tag, [33, 16], F32)
        oh = sb(cx, st, "oh33" + tag, [33, 512], F32)
        anti = sb(cx, st, "anti" + tag, [128, 128], F32)
        tv = sb(cx, st, "tv" + tag, [128, 4, 16], F32)
        hk = sb(cx, st, "hankel" + tag, [128, 384 * 16], F32)
        pp = ps(cx, st, "ps_bias" + tag, [128, 8, 512], F32)
        r_rb, r_oh, r_anti, r_tv, r_hk = Res("rb"), Res("oh"), Res("anti" + tag), Res("tv" + tag), Res("hk")
        r_tvec = Res("tvec")
        r_b = [Res(f"bb{i}") for i in range(8)]
        S.op(S.pool, lambda h: h.memset(rb[:, :], NEG), writes=[r_rb])
        S.dma(S.sp, lambda h: h.dma_start(out=rb[0:32, :], in_=rel_bias[:, :]), slot=r_rb, writes=[r_rb])
        S.dma(S.sp, lambda h: h.dma_start(out=oh[:, :], in_=c_onehot[:, :]), slot=r_oh, writes=[r_oh])
        S.dma(S.sp, lambda h: h.dma_start(out=anti[:, :], in_=c_anti[:, :]), slot=r_anti, writes=[r_anti])
        for c in range(4):
            S.op(S.pe, (lambda c: lambda h: h.matmul(pp[:, 0, c * 16:(c + 1) * 16], oh[:, c * 128:(c + 1) * 128],
                                                     rb[:, :], start=True, stop=True))(c),
                 reads=[r_rb, r_oh], writes=[r_b[0]])
        S.op(S.dve, lambda h: h.tensor_copy(tv[:, :, :], pp[:, 0, 0:64].rearrange("p (c h) -> p c h", c=4)),
             reads=[r_b[0]], writes=[r_tv])
        S.dma(S.sp, lambda h: h.dma_start(out=tvec.rearrange("(c p) h -> p c h", p=128), in_=tv[:, :, :]),
              slot=r_tv, reads=[r_tv], writes=[r_tvec])
        hank_src = bass.AP(tvec.tensor, 0, [[16, 128], [1, 384 * 16]])
        S.dma(S.sp, lambda h: h.dma_start(out=hk[:, :], in_=hank_src), slot=r_hk, reads=[r_tvec], writes=[r_hk])
        for m in range(12):
            b = 1 + m % 4
            S.op(S.pe, (lambda m, b: lambda h: h.matmul(pp[:, b, :], anti[:, :], hk[:, m * 512:(m + 1) * 512],
                                                        start=True, stop=True))(m, b),
                 reads=[r_anti, r_hk], writes=[r_b[b]])
            S.op(S.dve, (lambda m, b: lambda h: h.tensor_copy(
                cx.biasH[:, :, m * 32:(m + 1) * 32], pp[:, b, :].rearrange("p (j h) -> p h j", h=16)))(m, b),
                reads=[r_b[b]], writes=[cx.r_biasH])
        S.barrier()


def attn_phase(cx, wqkv, wout, sink, g_pre, g_post, src, dst, src_res, dst_res, seq_list, tag):
    S, nc = cx.S, cx.nc
    with contextlib.ExitStack() as st:
        cx.biasH = sb(cx, st, "biasH" + tag, [128, 16, 384], F32)
        cx.r_biasH = Res("biasH" + tag)
        bias_setup(cx, cx.dram("rel_bias", [N_BUCKETS, N_HEADS]), cx.dram("c_onehot", [33, 512]),
                   cx.dram("c_anti", [128, 128]), cx.dram("tvec", [512, 16], "Internal"), tag)
        Wqkv = sb(cx, st, "Wqkv" + tag, [128, 8, 1280 + 512], BF16)
        Wo = sb(cx, st, "Wo" + tag, [128, 8, D], BF16)
        gpre = sb(cx, st, "agpre" + tag, [128, D], F32)
        gpost = sb(cx, st, "agpost" + tag, [128, D], F32)
        sinkb = sb(cx, st, "sinkb" + tag, [128, 16], F32)
        xa = [sb(cx, st, f"axa{i}" + tag, [128, D], F32) for i in range(2)]
        xb = [sb(cx, st, f"axb{i}" + tag, [128, D], F32) for i in range(2)]
        hb = [sb(cx, st, f"ahb{i}" + tag, [128, D], BF16) for i in range(2)]
        hT = [sb(cx, st, f"ahT{i}" + tag, [128, 8, 128], BF16) for i in range(2)]
        qT = [sb(cx, st, f"qT{i}" + tag, [128, 8, 128], BF16) for i in range(3)]
        klo = sb(cx, st, "klo" + tag, [128, 4, 4, 128], BF16)
        khi = sb(cx, st, "khi" + tag, [128, 4, 4, 128], BF16)
        vv = sb(cx, st, "vv" + tag, [128, 4, 256], BF16)
        Ssb = sb(cx, st, "Ssb" + tag, [128, 16, 384], F32)
        Pb2 = [sb(cx, st, f"Pb{i}" + tag, [128, 16, 384], BF16) for i in range(2)]
        PT = sb(cx, st, "PT" + tag, [128, 16, 3, 128], BF16)
        osb = sb(cx, st, "osb" + tag, [128, D], BF16)
        oT = sb(cx, st, "oT" + tag, [128, 8, 128], BF16)
        junk = sb(cx, st, "ajunk" + tag, [128, D], BF16)
        tt = [sb(cx, st, f"att{i}" + tag, [128, 512], F32) for i in range(2)]
        small = sb(cx, st, "asmall" + tag, [128, 192], F32)
        pp = ps(cx, st, "aps" + tag, [128, 8, 512], F32)

        r_W, r_Wo = Res("Wqkv" + tag), Res("Wo" + tag)
        r_gpre, r_gpost, r_sink = Res("agpre" + tag), Res("agpost" + tag), Res("sinkb" + tag)
        r_xa = [Res(f"axa{i}" + tag) for i in range(2)]
        r_xb = [Res(f"axb{i}" + tag) for i in range(2)]
        r_hb = [Res(f"ahb{i}" + tag) for i in range(2)]
        r_hT = [Res(f"ahT{i}" + tag) for i in range(2)]
        r_qT = [Res(f"qT{i}" + tag) for i in range(3)]
        r_k = [Res(f"kslot{i}" + tag) for i in range(4)]
        r_v = [Res(f"vslot{i}" + tag) for i in range(4)]
        r_S, r_PT, r_osb, r_oT, r_junk = (Res("Ssb" + tag), Res("PT" + tag),
                                          Res("osb" + tag), Res("oT" + tag), Res("ajunk" + tag))
        r_P2 = [[Res(f"Pb{i}_{hp}" + tag) for hp in range(8)] for i in range(2)]
        r_PTh = [Res(f"PTh{i}" + tag) for i in range(8)]
        r_Sh = [Res(f"Sh{i}" + tag) for i in range(16)]
        r_tt = [Res(f"att{i}" + tag) for i in range(2)]
        r_b = [Res(f"abank{i}" + tag) for i in range(8)]
        r_sm = {}

        def rsm(name):
            if name not in r_sm:
                r_sm[name] = Res("asm_" + name + tag)
            return r_sm[name]
        C_SS, C_V, C_RSTD = 0, 2, 4
        C_M, C_NEGM, C_ROW, C_ES, C_DEN, C_RDEN = 16, 32, 48, 64, 80, 96
        C_SS2, C_SSS, C_V2, C_RSTD2 = 112, 116, 118, 120

        wq3 = wqkv.rearrange("(k p) f -> p k f", p=128)
        for k in range(8):
            S.dma(S.pool, (lambda k: lambda h: h.dma_start(out=Wqkv[:, k, 0:1024], in_=wqkv[k * 128:(k + 1) * 128, 0:1024]))(k),
                  slot=r_W, writes=[r_W])
            S.dma(S.pool, (lambda k: lambda h: h.dma_start(out=Wqkv[:, k, 1024:1280], in_=wqkv[k * 128:(k + 1) * 128, 1280:1536]))(k),
                  slot=r_W, writes=[r_W])
            for rep in range(2):
                S.dma(S.pool, (lambda k, rep: lambda h: h.dma_start(
                    out=Wqkv[:, k, 1280:1792].rearrange("p (a r d) -> p a r d", a=4, r=2)[:, :, rep, :],
                    in_=wqkv[k * 128:(k + 1) * 128, 1024:1280].rearrange("p (a d) -> p a d", a=4)))(k, rep),
                    slot=r_W, writes=[r_W])
            S.dma(S.pool, (lambda k: lambda h: h.dma_start(out=Wo[:, k, :], in_=wout[k * 128:(k + 1) * 128, :]))(k),
                  slot=r_Wo, writes=[r_Wo])
        load_bcast_row(cx, S.sp, gpre, r_gpre, g_pre)
        load_bcast_row(cx, S.sp, gpost, r_gpost, g_post)
        load_bcast_row(cx, S.sp, sinkb, r_sink, sink)
        S.op(S.pool, lambda h: h.memset(klo[:, :, :, :], 0.0), writes=r_k)
        S.op(S.pool, lambda h: h.memset(khi[:, :, :, :], 0.0), writes=r_k)

        cnt = {"a": 0, "ep": 0}

        def a_pre(base, n):
            i = n % 2
            gt = base // 128 + n
            rows = slice(gt * 128, (gt + 1) * 128)
            S.dma(S.sp, lambda h: h.dma_start(out=xa[i][:, :], in_=src[rows, :]),
                  slot=r_xa[i], reads=[src_res[gt]], writes=[r_xa[i]])
            S.op(S.act, lambda h: h.activation(junk[:, :], xa[i][:, :], AF.Square,
                                               accum_out=small[:, C_SS + i:C_SS + i + 1]),
                 reads=[r_xa[i]], writes=[r_junk, rsm(f"ss{i}")])
            emit_rstd(cx, small[:, C_SS + i:C_SS + i + 1], small[:, C_V + i:C_V + i + 1],
                      small[:, C_RSTD + i:C_RSTD + i + 1], rsm(f"ss{i}"), rsm(f"v{i}"), rsm(f"rstd{i}"), D)
            S.op(S.dve, lambda h: h.scalar_tensor_tensor(
                hb[i][:, :], xa[i][:, :], small[:, C_RSTD + i:C_RSTD + i + 1], gpre[:, :], ALU.mult, ALU.mult),
                reads=[r_xa[i], rsm(f"rstd{i}"), r_gpre], writes=[r_hb[i]])

        def a_main(base, n):
            i = n % 2
            sl = n % 4
            pst = pp[:, 0, :].bitcast(BF16)
            for k in range(8):
                S.op(S.pe, (lambda k: lambda h: h.transpose(
                    pst[:, k * 128:(k + 1) * 128], hb[i][:, k * 128:(k + 1) * 128], cx.ident[:, :]))(k),
                    reads=[r_hb[i]], writes=[r_b[0]])
            S.op(S.dve, lambda h: h.tensor_copy(hT[i][:, :, :], pst.rearrange("p (k c) -> p k c", k=8)),
                 reads=[r_b[0]], writes=[r_hT[i]])
            for c in range(8):
                b = 1 + c // 4
                for k in range(8):
                    S.op(S.pe, (lambda c, k, b: lambda h: h.matmul(
                        pp[:, b, (c % 4) * 128:(c % 4 + 1) * 128], Wqkv[:, k, c * 128:(c + 1) * 128], hT[i][:, k, :],
                        start=(k == 0), stop=(k == 7)))(c, k, b),
                        reads=[r_W, r_hT[i]], writes=[r_b[b]])
            qi = n % 3
            for b in (1, 2):
                S.op(S.dve, (lambda b: lambda h: h.tensor_copy(
                    qT[qi][:, (b - 1) * 4:b * 4, :], pp[:, b, :].rearrange("p (c t) -> p c t", c=4)))(b),
                    reads=[r_b[b]], writes=[r_qT[qi]])
            for a in range(4):
                for k in range(8):
                    S.op(S.pe, (lambda a, k: lambda h: h.matmul(
                        pp[:, 3, a * 128:(a + 1) * 128], Wqkv[:, k, 1280 + a * 128:1280 + (a + 1) * 128], hT[i][:, k, :],
                        start=(k == 0), stop=(k == 7)))(a, k),
                        reads=[r_W, r_hT[i]], writes=[r_b[3]])
            S.op(S.dve, lambda h: h.tensor_copy(klo[0:64, sl, :, :], pp[0:64, 3, :].rearrange("p (a t) -> p a t", a=4)),
                 reads=[r_b[3]], writes=[r_k[sl]])
            S.op(S.dve, lambda h: h.tensor_copy(khi[64:128, sl, :, :], pp[64:128, 3, :].rearrange("p (a t) -> p a t", a=4)),
                 reads=[r_b[3]], writes=[r_k[sl]])
            for k in range(8):
                S.op(S.pe, (lambda k: lambda h: h.matmul(
                    pp[:, 4, 0:256], hT[i][:, k, :], Wqkv[:, k, 1024:1280], start=(k == 0), stop=(k == 7)))(k),
                    reads=[r_W, r_hT[i]], writes=[r_b[4]])
            S.op(S.dve, lambda h: h.tensor_copy(vv[:, sl, :], pp[:, 4, 0:256]),
                 reads=[r_b[4]], writes=[r_v[sl]])

        def keys_of(n, nb):
            jlist = [j for j in range(3) if 0 <= n - 1 + j < nb]
            return jlist, jlist[0] * 128, (jlist[-1] + 1) * 128

        def pv_part(prev, hp):
            n, nb, par = prev
            Pb, r_P = Pb2[par], r_P2[par]
            jlist, _, _ = keys_of(n, nb)
            b = 3 if hp % 2 == 0 else 0
            pb = pp[:, b, :].bitcast(BF16)
            for u in range(2):
                hd = hp * 2 + u
                for j in jlist:
                    S.op(S.pe, (lambda pb, u, j, hd: lambda h: h.transpose(
                        pb[:, (u * 3 + j) * 128:(u * 3 + j + 1) * 128], Pb[:, hd, j * 128:(j + 1) * 128],
                        cx.ident[:, :]))(pb, u, j, hd),
                        reads=[r_P[hp]], writes=[r_b[b]])
            j0, j1 = jlist[0], jlist[-1] + 1
            S.op(S.act, lambda h: h.activation(
                PT[:, hp * 2:hp * 2 + 2, j0:j1, :],
                pb[:, 0:768].rearrange("p (u j t) -> p u j t", u=2, j=3)[:, :, j0:j1, :], AF.Copy),
                reads=[r_b[b]], writes=[r_PTh[hp]])
            for u in range(2):
                hd = hp * 2 + u
                kvh = hd // 4
                bo = 1 + hd // 8
                for idx, j in enumerate(jlist):
                    sl = (n - 1 + j) % 4
                    S.op(S.pe, (lambda hd, bo, j, sl, kvh, idx: lambda h: h.matmul(
                        pp[:, bo, (hd % 8) * 64:(hd % 8 + 1) * 64], PT[:, hd, j, :], vv[:, sl, kvh * 64:(kvh + 1) * 64],
                        start=(idx == 0), stop=(idx == len(jlist) - 1)))(hd, bo, j, sl, kvh, idx),
                        reads=[r_PTh[hp], r_v[sl]], writes=[r_b[bo]])

        def b_scores(n, nb, prev):
            qi = n % 3
            jlist, c0, c1 = keys_of(n, nb)
            for hd in range(16):
                kvh, half, c = hd // 4, hd % 2, hd // 2
                b = 6 + hd % 2
                kt = klo if half == 0 else khi
                for j in jlist:
                    sl = (n - 1 + j) % 4
                    S.op(S.pe, (lambda b, c, j, sl, kvh, kt: lambda h: h.matmul(
                        pp[:, b, j * 128:(j + 1) * 128], qT[qi][:, c, :], kt[:, sl, kvh, :], start=True, stop=True))(
                        b, c, j, sl, kvh, kt),
                        reads=[r_qT[qi], r_k[sl]], writes=[r_b[b]])
                S.op(S.dve, (lambda b, hd: lambda h: h.scalar_tensor_tensor(
                    Ssb[:, hd, c0:c1], pp[:, b, c0:c1], 0.125, cx.biasH[:, hd, c0:c1], ALU.mult, ALU.add))(b, hd),
                    reads=[r_b[b], cx.r_biasH], writes=[r_Sh[hd]])
                S.op(S.dve, (lambda hd: lambda h: h.tensor_reduce(small[:, C_M + hd:C_M + hd + 1], Ssb[:, hd, c0:c1], AX.X, ALU.max))(hd),
                     reads=[r_Sh[hd]], writes=[rsm(f"m{hd}")])
                if prev is not None and hd % 2 == 1:
                    pv_part(prev, hd // 2)

        def b_soft(n, nb, par):
            Pb, r_P = Pb2[par], r_P2[par]
            C_RDEN_ = C_RDEN + 48 * par
            jlist, c0, c1 = keys_of(n, nb)
            sm = lambda c: small[:, c:c + 16]
            S.op(S.dve, lambda h: h.tensor_tensor(sm(C_M), sm(C_M), sinkb[:, :], ALU.max),
                 reads=[rsm(f"m{hd}") for hd in range(16)] + [r_sink], writes=[rsm("m")])
            S.op(S.dve, lambda h: h.tensor_scalar(sm(C_NEGM), sm(C_M), -1.0, None, ALU.mult),
                 reads=[rsm("m")], writes=[rsm("negm")])
            for hd in range(16):
                S.op(S.act, (lambda hd: lambda h: h.activation(
                    Pb[:, hd, c0:c1], Ssb[:, hd, c0:c1], AF.Exp, bias=small[:, C_NEGM + hd:C_NEGM + hd + 1],
                    accum_out=small[:, C_ROW + hd:C_ROW + hd + 1]))(hd),
                    reads=[r_Sh[hd], rsm("negm")], writes=[r_P[hd // 2], rsm(f"row{hd}")])
            S.op(S.dve, lambda h: h.tensor_tensor(sm(C_ES), sinkb[:, :], sm(C_NEGM), ALU.add),
                 reads=[r_sink, rsm("negm")], writes=[rsm("es")])
            S.op(S.act, lambda h: h.activation(sm(C_ES), sm(C_ES), AF.Exp), reads=[rsm("es")], writes=[rsm("es")])

        def b_tail(base, prev):
            n, nb, par = prev
            gt = base // 128 + n
            rows = slice(gt * 128, (gt + 1) * 128)
            C_RDEN_ = C_RDEN + 48 * par
            sm = lambda c: small[:, c:c + 16]
            S.op(S.dve, lambda h: h.tensor_tensor(sm(C_DEN), sm(C_ROW), sm(C_ES), ALU.add),
                 reads=[rsm("es")] + [rsm(f"row{hd}") for hd in range(16)], writes=[rsm("den")])
            S.op(S.dve, lambda h: h.reciprocal(sm(C_RDEN_), sm(C_DEN)), reads=[rsm("den")], writes=[rsm(f"rden{par}")])
            for b in (1, 2):
                S.op(S.dve, (lambda b: lambda h: h.tensor_tensor(
                    osb[:, (b - 1) * 512:b * 512].rearrange("p (a d) -> p a d", a=8),
                    pp[:, b, :].rearrange("p (a d) -> p a d", a=8),
                    small[:, C_RDEN_ + (b - 1) * 8:C_RDEN_ + b * 8].unsqueeze(2).broadcast_to([128, 8, 64]),
                    ALU.mult))(b),
                    reads=[r_b[b], rsm(f"rden{par}")], writes=[r_osb])
            pst = pp[:, 0, :].bitcast(BF16)
            for k in range(8):
                S.op(S.pe, (lambda k: lambda h: h.transpose(
                    pst[:, k * 128:(k + 1) * 128], osb[:, k * 128:(k + 1) * 128], cx.ident[:, :]))(k),
                    reads=[r_osb], writes=[r_b[0]])
            S.op(S.act, lambda h: h.activation(oT[:, :, :], pst.rearrange("p (k c) -> p k c", k=8), AF.Copy),
                 reads=[r_b[0]], writes=[r_oT])
            jx = cnt["ep"] % 2
            cnt["ep"] += 1
            S.dma(S.sp, lambda h: h.dma_start(out=xb[jx][:, :], in_=src[rows, :]),
                  slot=r_xb[jx], reads=[src_res[gt]], writes=[r_xb[jx]])
            for half in range(2):
                b = 4 + half
                for k in range(8):
                    S.op(S.pe, (lambda b, k, half: lambda h: h.matmul(
                        pp[:, b, :], oT[:, k, :], Wo[:, k, half * 512:(half + 1) * 512],
                        start=(k == 0), stop=(k == 7)))(b, k, half),
                        reads=[r_oT, r_Wo], writes=[r_b[b]])
                S.op(S.act, (lambda b, half: lambda h: h.activation(
                    junk[:, 0:512], pp[:, b, :], AF.Square, accum_out=small[:, C_SS2 + half:C_SS2 + half + 1]))(b, half),
                    reads=[r_b[b]], writes=[r_junk, rsm(f"ss2{half}")])
            S.op(S.pool, lambda h: h.tensor_tensor(small[:, C_SSS:C_SSS + 1], small[:, C_SS2:C_SS2 + 1],
                                                   small[:, C_SS2 + 1:C_SS2 + 2], ALU.add),
                 reads=[rsm("ss20"), rsm("ss21")], writes=[rsm("sss")])
            emit_rstd(cx, small[:, C_SSS:C_SSS + 1], small[:, C_V2:C_V2 + 1], small[:, C_RSTD2:C_RSTD2 + 1],
                      rsm("sss"), rsm("v2"), rsm("rstd2"), D)
            for half in range(2):
                b = 4 + half
                S.op(S.dve, (lambda b, half: lambda h: h.scalar_tensor_tensor(
                    tt[half][:, :], pp[:, b, :], small[:, C_RSTD2:C_RSTD2 + 1], gpost[:, half * 512:(half + 1) * 512],
                    ALU.mult, ALU.mult))(b, half),
                    reads=[r_b[b], rsm("rstd2"), r_gpost], writes=[r_tt[half]])
                S.op(S.pool, (lambda half: lambda h: h.tensor_tensor(
                    xb[jx][:, half * 512:(half + 1) * 512], tt[half][:, :], xb[jx][:, half * 512:(half + 1) * 512],
                    ALU.add))(half),
                    reads=[r_tt[half], r_xb[jx]], writes=[r_xb[jx]])
            S.dma(S.sp, lambda h: h.dma_start(out=dst[rows, :], in_=xb[jx][:, :]),
                  slot=r_xb[jx], reads=[r_xb[jx]], writes=[dst_res[gt]])

        kk = 0
        for (base, slen) in seq_list:
            nb = slen // 128
            a_pre(base, 0)
            a_main(base, 0)
            if nb > 1:
                a_pre(base, 1)
                a_main(base, 1)
            if nb > 2:
                a_pre(base, 2)
            prev = None
            for n in range(nb):
                b_scores(n, nb, prev)
                if prev is not None:
                    b_tail(base, prev)
                b_soft(n, nb, kk % 2)
                if n + 2 < nb:
                    a_main(base, n + 2)
                if n + 3 < nb:
                    a_pre(base, n + 3)
                prev = (n, nb, kk % 2)
                kk += 1
            for hp in range(8):
                pv_part(prev, hp)
            b_tail(base, prev)
        S.barrier()


def bc3(ap2, n):
    return ap2.unsqueeze(2).broadcast_to([ap2.shape[0], ap2.shape[1], n])


def ssd_phase(cx, j, g_pre, g_post, src, dst, src_res, dst_res, seq_list, tag):
    S, nc = cx.S, cx.nc
    dram = cx.dram
    ntok = sum(s for _, s in seq_list)
    nch = ntok // 128
    w_in = dram("ssd_w_in", [2, D, D_IN_PROJ])[j]
    conv_w = dram("ssd_conv_w", [2, 5, CONV_DIM])[j]
    conv_b = dram("ssd_conv_b", [2, CONV_DIM])[j]
    dt_bias = dram("ssd_dt_bias", [2, 2, SSD_HEADS])[j]
    A_log = dram("ssd_A_log", [2, 2, SSD_HEADS])[j]
    Dp = dram("ssd_D", [2, SSD_HEADS])[j]
    ng = dram("ssd_norm_g", [2, D_INNER])[j]
    w_out = dram("ssd_w_out", [2, D_INNER, D])[j]
    c_tri = dram("c_tri", [4, 128, 128])
    c_sel = dram("c_sel2", [128, 64 * 128], dtype=BF16)
    c_ident = dram("c_ident", [128, 128])

    def scr(name, shape, dtype=F32):
        key = name
        if key not in cx.scr:
            cx.scr[key] = nc.dram_tensor(name, list(shape), dtype, kind="Internal").ap()
        return cx.scr[key]
    zs = scr("s_z", [ntok, 2048])
    dtr = scr("s_dtr", [ntok, 64])
    xtok = scr("s_xtok", [ntok, 2048], BF16)
    btok = scr("s_btok", [ntok, 1024], BF16)
    bct = scr("s_bct", [nch, 128, 16 * 128], BF16)
    yacc_d = scr("s_yacc", [ntok, 2048])
    xddb_d = scr("s_xddb", [ntok, 2048], BF16)
    scb_d = scr("s_scb", [ntok, 64])
    R = lambda n: [Res(f"{n}{t}{tag}") for t in range(nch)]
    r_zs, r_dtr, r_xtok, r_btok, r_bct, r_yacc, r_xddb, r_scb = (R("zs"), R("dtr"), R("xtok"), R("btok"), R("bct"),
                                                               R("yacc"), R("xddb"), R("scb"))

    def _sweep_f1():
        with contextlib.ExitStack() as st:
            Win = sb(cx, st, "Win" + tag, [128, 8, D_IN_PROJ], BF16)
            acc2 = [sb(cx, st, f"acc{i}" + tag, [128, 32, 128], F32) for i in range(2)]
            tmpA = sb(cx, st, "tmpA" + tag, [128, 32, 128], BF16)
            pc = [sb(cx, st, f"pc{i}" + tag, [128, 32, 132], BF16) for i in range(3)]
            xbcT = sb(cx, st, "xbcT" + tag, [128, 32, 128], BF16)
            xtk = sb(cx, st, "xtk" + tag, [128, 2048], BF16)
            btk = sb(cx, st, "btk" + tag, [128, 1024], BF16)
            zst = [sb(cx, st, f"zst{i}" + tag, [128, 1024], F32) for i in range(2)]
            dtst = sb(cx, st, "dtst" + tag, [128, 64], F32)
            xa2 = [sb(cx, st, f"sxa{i}" + tag, [128, D], F32) for i in range(2)]
            hb2 = [sb(cx, st, f"shb{i}" + tag, [128, D], BF16) for i in range(2)]
            hT = [sb(cx, st, f"shT{i}" + tag, [128, 8, 128], BF16) for i in range(2)]
            gpre = sb(cx, st, "sgpre" + tag, [128, D], F32)
            idf = sb(cx, st, "idf" + tag, [128, 128], F32)
            cw = [sb(cx, st, f"cw{i}" + tag, [128, 128], F32) for i in range(2)]
            wT = sb(cx, st, "wT" + tag, [128, 192], F32)
            wTb = sb(cx, st, "wTb" + tag, [128, 192], BF16)
            small = sb(cx, st, "f1small" + tag, [128, 8], F32)
            pp = ps(cx, st, "f1ps" + tag, [128, 8, 512], F32)
            r_Win, r_diag, r_gpre, r_cbrow, r_ones, r_idf, r_cw, r_wT = (Res("Win" + tag), Res("diag" + tag), Res("sgpre" + tag),
                                                                          Res("cbrow" + tag), Res("onesr" + tag), Res("idf" + tag),
                                                                          [Res("cw0" + tag), Res("cw1" + tag)], Res("wT" + tag))
            r_pc = [Res(f"pc{i}" + tag) for i in range(3)]
            r_xbcT, r_xtk, r_btk, r_dtst = (Res("xbcT" + tag), Res("xtk" + tag), Res("btk" + tag), Res("dtst" + tag))
            r_xa2 = [Res(f"sxa{i}" + tag) for i in range(2)]
            r_hb2 = [Res(f"shb{i}" + tag) for i in range(2)]
            r_acc2 = [Res(f"acc{i}" + tag) for i in range(2)]
            r_tmpA = Res("tmpA" + tag)
            r_zst = [Res(f"zst{i}" + tag) for i in range(2)]
            r_hT = [Res(f"shT{i}" + tag) for i in range(2)]
            r_b = [Res(f"f1b{i}" + tag) for i in range(8)]
            r_sm = [Res(f"f1sm{i}" + tag) for i in range(8)]

            for k in range(8):
                for q in range(4):
                    c0, c1 = q * 1552, (q + 1) * 1552
                    S.dma(S.pool, (lambda k, c0, c1: lambda h: h.dma_start(out=Win[:, k, c0:c1], in_=w_in[k * 128:(k + 1) * 128, c0:c1]))(k, c0, c1),
                          slot=r_Win, writes=[r_Win])
            load_bcast_row(cx, S.sp, gpre, r_gpre, g_pre)
            S.dma(S.sp, lambda h: h.dma_start(out=idf[:, :], in_=c_ident[:, :]), slot=r_idf, writes=[r_idf])
            cwv = conv_w.rearrange("k (c p) -> (k c) p", p=128)
            S.dma(S.sp, lambda h: h.dma_start(out=cw[0][:, :], in_=cwv[0:128, :]), slot=r_cw[0], writes=[r_cw[0]])
            S.dma(S.sp, lambda h: h.dma_start(out=cw[1][0:32, :], in_=cwv[128:160, :]), slot=r_cw[1], writes=[r_cw[1]])
            S.dma(S.sp, lambda h: h.dma_start(out=cw[1][32:64, :], in_=conv_b.rearrange("(c p) -> c p", p=128)), slot=r_cw[1], writes=[r_cw[1]])
            S.op(S.pe, lambda h: h.transpose(pp[:, 0, 0:128], cw[0][:, :], idf[:, :]), reads=[r_cw[0], r_idf], writes=[r_b[0]])
            S.op(S.pe, lambda h: h.transpose(pp[:, 0, 128:192], cw[1][0:64, :], idf[0:64, 0:64]), reads=[r_cw[1], r_idf], writes=[r_b[0]])
            S.op(S.dve, lambda h: h.tensor_copy(wT[:, :], pp[:, 0, 0:192]), reads=[r_b[0]], writes=[r_wT])
            S.op(S.dve, lambda h: h.tensor_copy(wTb[:, :], wT[:, :]), reads=[r_wT], writes=[r_wT])
            for i in range(3):
                S.op(S.pool, (lambda i: lambda h: h.memset(pc[i][:, :, :], 0.0))(i), writes=[r_pc[i]])

            cnt = {"p": 0, "z": 0}

            def project_pre(base, c):
                i = c % 2
                xa, hb, r_xa, r_hb = xa2[i], hb2[i], r_xa2[i], r_hb2[i]
                gt = base // 128 + c
                rows = slice(gt * 128, (gt + 1) * 128)
                S.dma(S.sp, lambda h: h.dma_start(out=xa[:, :], in_=src[rows, :]), slot=r_xa, reads=[src_res[gt]], writes=[r_xa])
                S.op(S.act, lambda h: h.activation(hb[:, :], xa[:, :], AF.Square, accum_out=small[:, 3 * i:3 * i + 1]),
                     reads=[r_xa], writes=[r_hb, r_sm[3 * i]])
                emit_rstd(cx, small[:, 3 * i:3 * i + 1], small[:, 3 * i + 1:3 * i + 2], small[:, 3 * i + 2:3 * i + 3],
                          r_sm[3 * i], r_sm[3 * i + 1], r_sm[3 * i + 2], D)
                S.op(S.dve, lambda h: h.scalar_tensor_tensor(hb[:, :], xa[:, :], small[:, 3 * i + 2:3 * i + 3], gpre[:, :], ALU.mult, ALU.mult),
                     reads=[r_xa, r_sm[3 * i + 2], r_gpre], writes=[r_hb])

            def project_main(base, c, ncs):
                i = c % 2
                hb, r_hb = hb2[i], r_hb2[i]
                gt = base // 128 + c
                rows = slice(gt * 128, (gt + 1) * 128)
                sl = c % 3
                pst = pp[:, 0, :].bitcast(BF16)
                for k in range(8):
                    S.op(S.pe, (lambda k: lambda h: h.transpose(pst[:, k * 128:(k + 1) * 128], hb[:, k * 128:(k + 1) * 128], cx.ident[:, :]))(k),
                         reads=[r_hb], writes=[r_b[0]])
                S.op(S.act, lambda h: h.activation(hT[i][:, :, :], pst.rearrange("p (k c) -> p k c", k=8), AF.Copy),
                     reads=[r_b[0]], writes=[r_hT[i]])
                for q in range(4):
                    for k in range(8):
                        S.op(S.pe, (lambda q, k: lambda h: h.matmul(pp[:, 1 + q, :], hT[i][:, k, :], Win[:, k, q * 512:(q + 1) * 512],
                                                                    start=(k == 0), stop=(k == 7)))(q, k),
                             reads=[r_hT[i], r_Win], writes=[r_b[1 + q]])
                for hf in range(2):
                    zi = cnt["z"] % 2
                    cnt["z"] += 1
                    S.op(S.act, (lambda hf, zi: lambda h: h.activation(zst[zi][:, :].rearrange("p (a b) -> p a b", a=2),
                                                                       pp[:, 1 + 2 * hf:3 + 2 * hf, :], AF.Copy))(hf, zi),
                         reads=[r_b[1 + 2 * hf], r_b[2 + 2 * hf]], writes=[r_zst[zi]])
                    S.dma(S.sp, (lambda hf, zi: lambda h: h.dma_start(out=zs[rows, hf * 1024:(hf + 1) * 1024], in_=zst[zi][:, :]))(hf, zi),
                          slot=r_zst[zi], reads=[r_zst[zi]], writes=[r_zs[gt]])
                for k in range(8):
                    S.op(S.pe, (lambda k: lambda h: h.matmul(pp[:, 5, 0:64], hT[i][:, k, :], Win[:, k, 6144:6208],
                                                             start=(k == 0), stop=(k == 7)))(k),
                         reads=[r_hT[i], r_Win], writes=[r_b[5]])
                S.op(S.act, lambda h: h.activation(dtst[:, :], pp[:, 5, 0:64], AF.Copy), reads=[r_b[5]], writes=[r_dtst])
                S.dma(S.sp, lambda h: h.dma_start(out=dtr[rows, :], in_=dtst[:, :]), slot=r_dtst, reads=[r_dtst], writes=[r_dtr[gt]])
                for c4 in range(8):
                    b = 6 + c4 % 2
                    for u in range(4):
                        cc = c4 * 4 + u
                        for k in range(8):
                            S.op(S.pe, (lambda b, u, cc, k: lambda h: h.matmul(
                                pp[:, b, u * 128:(u + 1) * 128], Win[:, k, 2048 + cc * 128:2048 + (cc + 1) * 128], hT[i][:, k, :],
                                start=(k == 0), stop=(k == 7)))(b, u, cc, k),
                                reads=[r_hT[i], r_Win], writes=[r_b[b]])
                    S.op(S.act, (lambda b, c4: lambda h: h.activation(pc[sl][:, c4 * 4:(c4 + 1) * 4, 2:130],
                                                                      pp[:, b, :].rearrange("p (u t) -> p u t", u=4), AF.Copy))(b, c4),
                         reads=[r_b[b]], writes=[r_pc[sl]])
                if c > 0:
                    pv_ = (c - 1) % 3
                    S.op(S.pool, lambda h: h.tensor_copy(pc[pv_][:, :, 130:132], pc[sl][:, :, 2:4]), reads=[r_pc[sl]], writes=[r_pc[pv_]])
                    S.op(S.pool, lambda h: h.tensor_copy(pc[sl][:, :, 0:2], pc[pv_][:, :, 128:130]), reads=[r_pc[pv_]], writes=[r_pc[sl]])
                else:
                    S.op(S.pool, lambda h: h.memset(pc[sl][:, :, 0:2], 0.0), writes=[r_pc[sl]])
                if c == ncs - 1:
                    S.op(S.pool, lambda h: h.memset(pc[sl][:, :, 130:132], 0.0), writes=[r_pc[sl]])

            wv = lambda k: wT[:, k * 32:(k + 1) * 32].unsqueeze(2).broadcast_to([128, 32, 128])

            def conv_dve(c):
                sl = c % 3
                P, rP = pc[sl], r_pc[sl]
                A, rA = acc2[c % 2], r_acc2[c % 2]
                S.op(S.dve, lambda h: h.tensor_tensor(A[:, :, :], P[:, :, 0:128], wv(0), ALU.mult), reads=[rP, r_wT], writes=[rA])
                for k in (1, 2, 3, 4):
                    S.op(S.dve, (lambda k: lambda h: h.tensor_tensor(tmpA[:, :, :], P[:, :, k:k + 128], wv(k), ALU.mult))(k),
                         reads=[rP, r_wT], writes=[r_tmpA])
                    S.op(S.dve, lambda h: h.tensor_tensor(A[:, :, :], A[:, :, :], tmpA[:, :, :], ALU.add), reads=[rA, r_tmpA], writes=[rA])

            def conv_post(base, c):
                gt = base // 128 + c
                rows = slice(gt * 128, (gt + 1) * 128)
                A, rA = acc2[c % 2], r_acc2[c % 2]
                for cc in range(32):
                    S.op(S.act, (lambda cc: lambda h: h.activation(xbcT[:, cc, :], A[:, cc, :], AF.Silu, bias=wT[:, 160 + cc:161 + cc]))(cc),
                         reads=[rA, r_wT], writes=[r_xbcT])
                for hf in range(2):
                    pst = pp[:, 3 + hf, :].bitcast(BF16)
                    for u in range(8):
                        cc = hf * 8 + u
                        S.op(S.pe, (lambda pst, u, cc: lambda h: h.transpose(pst[:, u * 128:(u + 1) * 128], xbcT[:, cc, :], cx.ident[:, :]))(pst, u, cc),
                             reads=[r_xbcT], writes=[r_b[3 + hf]])
                    S.op(S.act, (lambda pst, hf: lambda h: h.activation(xtk[:, hf * 1024:(hf + 1) * 1024], pst, AF.Copy))(pst, hf),
                         reads=[r_b[3 + hf]], writes=[r_xtk])
                pst = pp[:, 5, :].bitcast(BF16)
                for u in range(8):
                    S.op(S.pe, (lambda u: lambda h: h.transpose(pst[:, u * 128:(u + 1) * 128], xbcT[:, 16 + u, :], cx.ident[:, :]))(u),
                         reads=[r_xbcT], writes=[r_b[5]])
                S.op(S.act, lambda h: h.activation(btk[:, :], pst, AF.Copy), reads=[r_b[5]], writes=[r_btk])
                S.dma(S.sp, lambda h: h.dma_start(out=xtok[rows, :], in_=xtk[:, :]), slot=r_xtk, reads=[r_xtk], writes=[r_xtok[gt]])
                S.dma(S.sp, lambda h: h.dma_start(out=btok[rows, :], in_=btk[:, :]), slot=r_btk, reads=[r_btk], writes=[r_btok[gt]])
                S.dma(S.sp, lambda h: h.dma_start(out=bct[gt], in_=xbcT[:, 16:32, :].rearrange("p a t -> p (a t)")),
                      slot=r_xbcT, reads=[r_xbcT], writes=[r_bct[gt]])

            for (base, slen) in seq_list:
                ncs = slen // 128
                project_pre(base, 0)
                project_main(base, 0, ncs)
                if ncs > 1:
                    project_pre(base, 1)
                for c in range(1, ncs + 1):
                    if c < ncs:
                        project_main(base, c, ncs)
                    if c + 1 < ncs:
                        project_pre(base, c + 1)
                    conv_dve(c - 1)
                    if c >= 2:
                        conv_post(base, c - 2)
                conv_post(base, ncs - 1)
            S.barrier()
    _sweep_f1()

    def _sweep_f2():
        with contextlib.ExitStack() as st:
            sel2 = sb(cx, st, "sel2" + tag, [128, 64, 128], BF16)
            tri = sb(cx, st, "tri" + tag, [128, 2, 128], F32)
            nmask = sb(cx, st, "nmask" + tag, [128, 2, 128], BF16)
            onesf = sb(cx, st, "onesf" + tag, [128, 128], F32)
            dtb = sb(cx, st, "dtb" + tag, [128, 64], F32)
            Abc = sb(cx, st, "Abc" + tag, [128, 64], F32)
            Dbc = sb(cx, st, "Dbc" + tag, [128, 32], F32)
            xtk = [sb(cx, st, f"f2xtk{i}" + tag, [128, 2048], BF16) for i in range(2)]
            btk = [sb(cx, st, f"f2btk{i}" + tag, [128, 1024], BF16) for i in range(2)]
            bcs = [sb(cx, st, f"f2bct{i}" + tag, [128, 16, 128], BF16) for i in range(2)]
            dts = [sb(cx, st, f"f2dt{i}" + tag, [128, 64], F32) for i in range(2)]
            scs = [sb(cx, st, f"f2sc{i}" + tag, [128, 16, 64], F32) for i in range(2)]
            acss = [sb(cx, st, f"f2acs{i}" + tag, [128, 128], F32) for i in range(2)]
            hl = sb(cx, st, "f2hl" + tag, [128, 128], BF16)
            nhl = sb(cx, st, "f2nhl" + tag, [128, 128], BF16)
            hlTs = [sb(cx, st, f"f2hlT{i}" + tag, [128, 128], BF16) for i in range(2)]
            nhlTs = [sb(cx, st, f"f2nhlT{i}" + tag, [128, 128], BF16) for i in range(2)]
            CBTs = [sb(cx, st, f"f2CBT{i}" + tag, [128, 8, 128], BF16) for i in range(2)]
            xdds = [[sb(cx, st, f"f2xdd{j}_{i}" + tag, [128, 2048], BF16) for i in range(2)] for j in range(2)]
            identD = sb(cx, st, "f2identD" + tag, [128, 32, 128], BF16)
            seg = [sb(cx, st, f"f2seg{i}" + tag, [128, 4, 128], BF16) for i in range(4)]
            MT = [sb(cx, st, f"f2MT{i}" + tag, [128, 4, 128], BF16) for i in range(4)]
            Sf = sb(cx, st, "f2Sf" + tag, [128, 2048], F32)
            Sbf = sb(cx, st, "f2Sbf" + tag, [128, 2048], BF16)
            tmp = sb(cx, st, "f2tmp" + tag, [128, 2048], F32)
            yacc = sb(cx, st, "f2yacc" + tag, [128, 2048], F32)
            scbs = [sb(cx, st, f"f2scb{i}" + tag, [128, 64], F32) for i in range(2)]
            pp = ps(cx, st, "f2ps" + tag, [128, 8, 512], F32)
            rr = {}

            def r(name):
                if name not in rr:
                    rr[name] = Res("f2_" + name + tag)
                return rr[name]
            r_b = [r(f"b{i}") for i in range(8)]
            S.dma(S.sp, lambda h: h.dma_start(out=sel2[:, :, :].rearrange("p a b -> p (a b)"), in_=c_sel[:, :]), slot=r("sel2"), writes=[r("sel2")])
            S.dma(S.sp, lambda h: h.dma_start(out=tri[:, 0, :], in_=c_tri[0]), slot=r("tri"), writes=[r("tri")])
            S.dma(S.sp, lambda h: h.dma_start(out=tri[:, 1, :], in_=c_tri[1]), slot=r("tri"), writes=[r("tri")])
            S.dma(S.pool, lambda h: h.dma_start(out=nmask[:, 0, :], in_=c_tri[2]), slot=r("nmask"), writes=[r("nmask")])
            S.dma(S.pool, lambda h: h.dma_start(out=nmask[:, 1, :], in_=c_tri[3]), slot=r("nmask"), writes=[r("nmask")])
            S.op(S.pool, lambda h: h.memset(onesf[:, :], 1.0), writes=[r("onesf")])
            load_bcast_row(cx, S.sp, dtb, r("dtb"), dt_bias.rearrange("a b -> (a b)"))
            load_bcast_row(cx, S.sp, Abc, r("Abc"), A_log.rearrange("a b -> (a b)"))
            load_bcast_row(cx, S.sp, Dbc, r("Dbc"), Dp)
            S.op(S.act, lambda h: h.activation(Abc[:, :], Abc[:, :], AF.Exp), reads=[r("Abc")], writes=[r("Abc")])
            S.op(S.pool, lambda h: h.tensor_scalar(Abc[:, :], Abc[:, :], -1.0, None, ALU.mult), reads=[r("Abc")], writes=[r("Abc")])
            for hd_ in range(32):
                e_ = S.dve
                S.op(e_, (lambda hd_: lambda h: h.tensor_scalar(identD[:, hd_, :], cx.ident[:, :], Dbc[:, hd_:hd_ + 1], None, ALU.mult))(hd_),
                     reads=[r("Dbc")], writes=[r("identD")])
            cnt = {"c": 0, "dk": 0}

            def load_chunk(gt):
                i = cnt["c"] % 2
                cnt["c"] += 1
                rows = slice(gt * 128, (gt + 1) * 128)
                S.dma(S.sp, lambda h: h.dma_start(out=xtk[i][:, :], in_=xtok[rows, :]), slot=r(f"xtk{i}"), reads=[r_xtok[gt]], writes=[r(f"xtk{i}")])
                S.dma(S.sp, lambda h: h.dma_start(out=btk[i][:, :], in_=btok[rows, :]), slot=r(f"btk{i}"), reads=[r_btok[gt]], writes=[r(f"btk{i}")])
                S.dma(S.sp, lambda h: h.dma_start(out=bcs[i][:, :, :].rearrange("p a t -> p (a t)"), in_=bct[gt]),
                      slot=r(f"bcs{i}"), reads=[r_bct[gt]], writes=[r(f"bcs{i}")])
                S.dma(S.sp, lambda h: h.dma_start(out=dts[i][:, :], in_=dtr[rows, :]), slot=r(f"dts{i}"), reads=[r_dtr[gt]], writes=[r(f"dts{i}")])
                return i

            def prep_stage(gt, i, p, stage):
                rows = slice(gt * 128, (gt + 1) * 128)
                X, BC, DT = xtk[i], bcs[i], dts[i]
                rX, rBC, rDT = r(f"xtk{i}"), r(f"bcs{i}"), r(f"dts{i}")
                sc = scs[p]
                acs = acss[p]
                V = lambda k: sc[:, k, :]
                q = lambda n_: r(f"{n_}_{p}")
                if stage == 1:
                    S.op(S.dve, lambda h: h.tensor_tensor(V(0), DT[:, :], dtb[:, :], ALU.add), reads=[rDT, r("dtb")], writes=[q("u")])
                    S.op(S.act, lambda h: h.activation(V(1), V(0), AF.Abs), reads=[q("u")], writes=[q("au")])
                    S.op(S.act, lambda h: h.activation(V(2), V(1), AF.Exp, scale=-1.0), reads=[q("au")], writes=[q("e")])
                    S.op(S.act, lambda h: h.activation(V(3), V(2), AF.Ln, bias=1.0), reads=[q("e")], writes=[q("l")])
                    S.op(S.dve, lambda h: h.scalar_tensor_tensor(V(4), V(0), 0.0, V(3), ALU.max, ALU.add), reads=[q("u"), q("l")], writes=[q("dt")])
                    S.op(S.dve, lambda h: h.tensor_tensor(V(5), V(4), Abc[:, :], ALU.mult), reads=[q("dt"), r("Abc")], writes=[q("a")])
                elif stage == 2:
                    for d in range(2):
                        S.op(S.pe, (lambda d: lambda h: h.matmul(pp[:, 7, d * 32:(d + 1) * 32], tri[:, d, :], sc[:, 5, d * 32:(d + 1) * 32],
                                                                 start=True, stop=True))(d),
                             reads=[r("tri"), q("a")], writes=[r_b[7]])
                    S.op(S.pe, lambda h: h.matmul(pp[:, 7, 64:128], onesf[:, :], V(5), start=True, stop=True),
                         reads=[r("onesf"), q("a")], writes=[r_b[7]])
                elif stage == 3:
                    S.op(S.act, lambda h: h.activation(acs[:, :], pp[:, 7, 0:128], AF.Copy), reads=[r_b[7]], writes=[q("acs")])
                    S.op(S.act, lambda h: h.activation(V(6), acs[:, 0:64], AF.Exp), reads=[q("acs")], writes=[q("eacs")])
                    S.op(S.dve, lambda h: h.tensor_tensor(V(7), acs[:, 64:128], acs[:, 0:64], ALU.subtract), reads=[q("acs")], writes=[q("dd")])
                    S.op(S.act, lambda h: h.activation(V(8), V(7), AF.Exp), reads=[q("dd")], writes=[q("dte")])
                    S.op(S.dve, lambda h: h.tensor_tensor(V(9), V(4), V(8), ALU.mult), reads=[q("dt"), q("dte")], writes=[q("w")])
                    S.op(S.act, lambda h: h.activation(V(10), acs[:, 64:128], AF.Exp), reads=[q("acs")], writes=[q("cdb")])
                    S.op(S.dve, lambda h: h.tensor_copy(hl[:, 0:64], acs[:, 0:64]), reads=[q("acs")], writes=[r("hl")])
                    S.op(S.dve, lambda h: h.tensor_tensor(hl[:, 64:128], acs[:, 0:64], hl[:, 0:64], ALU.subtract), reads=[q("acs"), r("hl")], writes=[r("hl")])
                    S.op(S.act, lambda h: h.activation(V(11), V(4), AF.Ln), reads=[q("dt")], writes=[q("lnd")])
                    S.op(S.dve, lambda h: h.tensor_tensor(V(12), V(11), acs[:, 0:64], ALU.subtract), reads=[q("lnd"), q("acs")], writes=[q("gg")])
                    S.op(S.dve, lambda h: h.tensor_copy(nhl[:, 0:64], V(12)), reads=[q("gg")], writes=[r("nhl")])
                    S.op(S.dve, lambda h: h.tensor_tensor(nhl[:, 64:128], V(12), nhl[:, 0:64], ALU.subtract), reads=[q("gg"), r("nhl")], writes=[r("nhl")])
                elif stage == 4:
                    pst = pp[:, 7, :].bitcast(BF16)
                    S.op(S.pe, lambda h: h.transpose(pst[:, 256:384], hl[:, :], cx.ident[:, :]), reads=[r("hl")], writes=[r_b[7]])
                    S.op(S.pe, lambda h: h.transpose(pst[:, 384:512], nhl[:, :], cx.ident[:, :]), reads=[r("nhl")], writes=[r_b[7]])
                elif stage == 5:
                    pst = pp[:, 7, :].bitcast(BF16)
                    S.op(S.dve, lambda h: h.tensor_copy(hlTs[p][:, :], pst[:, 256:384]), reads=[r_b[7]], writes=[q("hlT")])
                    S.op(S.dve, lambda h: h.tensor_copy(nhlTs[p][:, :], pst[:, 384:512]), reads=[r_b[7]], writes=[q("nhlT")])
                    S.op(S.pool, lambda h: h.tensor_copy(scbs[p][:, 0:32], sc[:, 6, 32:64]), reads=[q("eacs")], writes=[q("scb")])
                    S.op(S.pool, lambda h: h.tensor_copy(scbs[p][:, 32:64], sc[:, 10, 32:64]), reads=[q("cdb"), q("scb")], writes=[q("scb")])
                    S.dma(S.sp, lambda h: h.dma_start(out=scb_d[rows, :], in_=scbs[p][:, :]), slot=q("scb"), reads=[q("scb")], writes=[r_scb[gt]])
                elif stage in (6, 7):
                    hf = stage - 6
                    for gg in range(4):
                        g = hf * 4 + gg
                        S.op(S.pe, (lambda g, gg: lambda h: h.matmul(pp[:, 7, gg * 128:(gg + 1) * 128], BC[:, g, :], BC[:, 8 + g, :],
                                                                     start=True, stop=True))(g, gg),
                             reads=[rBC], writes=[r_b[7]])
                    S.op(S.act, lambda h: h.activation(CBTs[p][:, hf * 4:(hf + 1) * 4, :],
                                                       pp[:, 7, :].rearrange("p (u t) -> p u t", u=4), AF.Copy),
                         reads=[r_b[7]], writes=[q("CBT")])
                elif stage == 8:
                    X3 = X[:, :].rearrange("p (a d) -> p a d", a=32)
                    for d in range(2):
                        e_ = S.dve if d == 0 else S.pool
                        S.op(e_, (lambda d: lambda h: h.tensor_tensor(xdds[p][d][:, :].rearrange("p (a d) -> p a d", a=32), X3,
                                                                      bc3(sc[:, 9, d * 32:(d + 1) * 32], 64), ALU.mult))(d),
                             reads=[rX, q("w")], writes=[q(f"xdd{d}")])
                    S.dma(S.sp, lambda h: h.dma_start(out=xddb_d[rows, :], in_=xdds[p][1][:, :]), slot=q("xdd1"), reads=[q("xdd1")], writes=[r_xddb[gt]])

            PREP_AT = {0: 1, 2: 2, 3: 3, 6: 4, 7: 5, 9: 6, 11: 7, 12: 8}

            def f2_chunk(gt, i, p, first, nxt):
                rows = slice(gt * 128, (gt + 1) * 128)
                X, Bk, BC = xtk[i], btk[i], bcs[i]
                rX, rB, rBC = r(f"xtk{i}"), r(f"btk{i}"), r(f"bcs{i}")
                sc = scs[p]
                q = lambda n_: r(f"{n_}_{p}")
                hlT, nhlT, CBT, xdd = hlTs[p], nhlTs[p], CBTs[p], xdds[p]
                for hd in range(32):
                    S.op(S.pe, (lambda hd: lambda h: h.matmul(pp[:, 2 + hd // 8, (hd % 8) * 64:(hd % 8 + 1) * 64], identD[:, hd, :],
                                                              X[:, hd * 64:(hd + 1) * 64], start=(hd % 8 == 0), stop=False))(hd),
                         reads=[r("identD"), rX], writes=[r_b[2 + hd // 8]])
                dbanks = [0, 1, 6]
                groups = [(d, g) for d in range(2) for g in range(8)]

                def dec(ix):
                    d, g = groups[ix]
                    k = ix % 3
                    b = dbanks[k]
                    for hh in range(4):
                        col = d * 32 + g * 4 + hh
                        o = pp[:, b, hh * 128:(hh + 1) * 128]
                        S.op(S.pe, (lambda o, col: lambda h: h.matmul(o, sel2[:, col, :], hlT[:, :], start=True, stop=False))(o, col),
                             reads=[r("sel2"), q("hlT")], writes=[r_b[b]])
                        S.op(S.pe, (lambda o, col: lambda h: h.matmul(o, nhlT[:, :], sel2[:, col, :], start=False, stop=False))(o, col),
                             reads=[r("sel2"), q("nhlT")], writes=[r_b[b]])
                        S.op(S.pe, (lambda o, d: lambda h: h.matmul(o, cx.ident[:, :], nmask[:, d, :], start=False, stop=True))(o, d),
                             reads=[r("nmask")], writes=[r_b[b]])
                    S.op(S.act, (lambda b, k: lambda h: h.activation(seg[k][:, :, :], pp[:, b, :].rearrange("p (u t) -> p u t", u=4), AF.Exp))(b, k),
                         reads=[r_b[b]], writes=[r(f"seg{k}")])
                    S.op(S.dve, (lambda k, g: lambda h: h.tensor_tensor(MT[k][:, :, :], seg[k][:, :, :],
                                                                         CBT[:, g, :].unsqueeze(1).broadcast_to([128, 4, 128]), ALU.mult))(k, g),
                         reads=[r(f"seg{k}"), q("CBT")], writes=[r(f"MT{k}")])

                def ymm(ix):
                    d, g = groups[ix]
                    k = ix % 3
                    for hh in range(4):
                        hd = g * 4 + hh
                        S.op(S.pe, (lambda k, hh, hd: lambda h: h.matmul(
                            pp[:, 2 + hd // 8, (hd % 8) * 64:(hd % 8 + 1) * 64], MT[k][:, hh, :], X[:, hd * 64:(hd + 1) * 64],
                            start=False, stop=(ix >= 8)))(k, hh, hd),
                            reads=[r(f"MT{k}"), rX], writes=[r_b[2 + hd // 8]])
                for ix in range(2):
                    dec(ix)
                for ix in range(16):
                    if nxt is not None and ix in PREP_AT:
                        prep_stage(nxt[0], nxt[1], nxt[2], PREP_AT[ix])
                    if ix + 2 < 16:
                        dec(ix + 2)
                    ymm(ix)
                for hf in range(2):
                    if not first:
                        for gg in range(4):
                            g = hf * 4 + gg
                            b = 6 + gg // 2
                            S.op(S.pe, (lambda g, b, gg: lambda h: h.matmul(pp[:, b, (gg % 2) * 256:(gg % 2 + 1) * 256], BC[:, 8 + g, :],
                                                                            Sbf[:, g * 256:(g + 1) * 256], start=True, stop=True))(g, b, gg),
                                 reads=[rBC, r("Sbf")], writes=[r_b[b]])
                        S.op(S.dve, (lambda hf: lambda h: h.tensor_tensor(
                            tmp[:, hf * 1024:(hf + 1) * 1024].rearrange("p (a d) -> p a d", a=16),
                            pp[:, 6:8, :].rearrange("p b (a d) -> p (b a) d", d=64),
                            bc3(sc[:, 6, hf * 16:(hf + 1) * 16], 64), ALU.mult))(hf),
                            reads=[r_b[6], r_b[7], q("eacs")], writes=[r("tmp")])
                        S.op(S.dve, (lambda hf: lambda h: h.tensor_tensor(
                            yacc[:, hf * 1024:(hf + 1) * 1024].rearrange("p (b c) -> p b c", b=2),
                            tmp[:, hf * 1024:(hf + 1) * 1024].rearrange("p (b c) -> p b c", b=2),
                            pp[:, 2 + 2 * hf:4 + 2 * hf, :], ALU.add))(hf),
                            reads=[r("tmp"), r_b[2 + 2 * hf], r_b[3 + 2 * hf]], writes=[r("yacc")])
                    else:
                        S.op(S.act, (lambda hf: lambda h: h.activation(
                            yacc[:, hf * 1024:(hf + 1) * 1024].rearrange("p (b c) -> p b c", b=2),
                            pp[:, 2 + 2 * hf:4 + 2 * hf, :], AF.Copy))(hf),
                            reads=[r_b[2 + 2 * hf], r_b[3 + 2 * hf]], writes=[r("yacc")])
                S.dma(S.sp, lambda h: h.dma_start(out=yacc_d[rows, :], in_=yacc[:, :]), slot=r("yacc"), reads=[r("yacc")], writes=[r_yacc[gt]])
                for hf in range(2):
                    for gg in range(4):
                        g = hf * 4 + gg
                        b = 6 + gg // 2
                        S.op(S.pe, (lambda g, b, gg: lambda h: h.matmul(pp[:, b, (gg % 2) * 256:(gg % 2 + 1) * 256], Bk[:, g * 128:(g + 1) * 128],
                                                                        xdd[0][:, g * 256:(g + 1) * 256], start=True, stop=True))(g, b, gg),
                             reads=[rB, q("xdd0")], writes=[r_b[b]])
                    sl_ = slice(hf * 1024, (hf + 1) * 1024)
                    if first:
                        S.op(S.dve, (lambda sl_: lambda h: h.tensor_copy(Sf[:, sl_].rearrange("p (b c) -> p b c", b=2), pp[:, 6:8, :]))(sl_),
                             reads=[r_b[6], r_b[7]], writes=[r("Sf")])
                    else:
                        S.op(S.dve, (lambda sl_, hf: lambda h: h.tensor_tensor(
                            tmp[:, sl_].rearrange("p (a d) -> p a d", a=16), Sf[:, sl_].rearrange("p (a d) -> p a d", a=16),
                            bc3(sc[:, 10, hf * 16:(hf + 1) * 16], 64), ALU.mult))(sl_, hf),
                            reads=[r("Sf"), q("cdb"), r("tmp")], writes=[r("tmp")])
                        S.op(S.dve, (lambda sl_: lambda h: h.tensor_tensor(Sf[:, sl_].rearrange("p (b c) -> p b c", b=2),
                                                                            tmp[:, sl_].rearrange("p (b c) -> p b c", b=2), pp[:, 6:8, :], ALU.add))(sl_),
                             reads=[r("tmp"), r_b[6], r_b[7]], writes=[r("Sf")])
                S.op(S.act, lambda h: h.activation(Sbf[:, :], Sf[:, :], AF.Copy), reads=[r("Sf")], writes=[r("Sbf")])

            kk = 0
            for (base, slen) in seq_list:
                ncs = slen // 128
                g0 = base // 128
                i = load_chunk(g0)
                for stg in range(1, 9):
                    prep_stage(g0, i, kk % 2, stg)
                for c in range(ncs):
                    inext = load_chunk(g0 + c + 1) if c + 1 < ncs else None
                    nxt = (g0 + c + 1, inext, (kk + 1) % 2) if inext is not None else None
                    f2_chunk(g0 + c, i, kk % 2, first=(c == 0), nxt=nxt)
                    i = inext
                    kk += 1
            S.barrier()

    _sweep_f2()

    def _sweep_b():
        with contextlib.ExitStack() as st:
            Wout = sb(cx, st, "Wout" + tag, [128, 16, D], BF16)
            gpost = sb(cx, st, "sgpost" + tag, [128, D], F32)
            ngb = sb(cx, st, "ngb" + tag, [128, 2048], F32)
            yin = [sb(cx, st, f"byacc{i}" + tag, [128, 2048], F32) for i in range(3)]
            zin = [sb(cx, st, f"bz{i}" + tag, [128, 2048], F32) for i in range(3)]
            xdb = [sb(cx, st, f"bxdd{i}" + tag, [128, 2048], BF16) for i in range(2)]
            btk = [sb(cx, st, f"bbtk{i}" + tag, [128, 1024], BF16) for i in range(2)]
            bcs = [sb(cx, st, f"bbct{i}" + tag, [128, 16, 128], BF16) for i in range(2)]
            scb = [sb(cx, st, f"bscb{i}" + tag, [128, 64], F32) for i in range(2)]
            xb3 = [sb(cx, st, f"bxb{i}" + tag, [128, D], F32) for i in range(4)]
            tmpS = sb(cx, st, "btmpS" + tag, [128, 2048], F32)
            Sb = sb(cx, st, "bSb" + tag, [128, 2048], F32)
            Sbb = sb(cx, st, "bSbb" + tag, [128, 2048], BF16)
            tmp = sb(cx, st, "btmp" + tag, [128, 2048], F32)
            yn2 = [sb(cx, st, f"byn{i}" + tag, [128, 2048], BF16) for i in range(2)]
            ynT = sb(cx, st, "bynT" + tag, [128, 16, 128], BF16)
            junk = sb(cx, st, "bjunk" + tag, [128, 512], BF16)
            tt = [sb(cx, st, f"btt{i}" + tag, [128, 512], F32) for i in range(2)]
            small = sb(cx, st, "bsmall" + tag, [128, 64], F32)
            pp = ps(cx, st, "bps" + tag, [128, 8, 512], F32)
            rr = {}

            def r(name):
                if name not in rr:
                    rr[name] = Res("b_" + name + tag)
                return rr[name]
            r_b = [r(f"b{i}") for i in range(8)]
            for k in range(16):
                S.dma(S.pool, (lambda k: lambda h: h.dma_start(out=Wout[:, k, :], in_=w_out[k * 128:(k + 1) * 128, :]))(k),
                      slot=r("Wout"), writes=[r("Wout")])
            load_bcast_row(cx, S.sp, gpost, r("gpost"), g_post)
            load_bcast_row(cx, S.sp, ngb, r("ngb"), ng)
            cnt = {"c": 0}

            def load_chunk(gt, kk):
                i = cnt["c"] % 2
                cnt["c"] += 1
                rows = slice(gt * 128, (gt + 1) * 128)
                y3 = kk % 3
                for (t, dsrc, rs, nm, ii) in ((yin, yacc_d, r_yacc, "yin", y3), (zin, zs, r_zs, "zin", y3), (xdb, xddb_d, r_xddb, "xdb", i),
                                              (btk, btok, r_btok, "btk", i), (scb, scb_d, r_scb, "scb", i)):
                    S.dma(S.sp, (lambda t, dsrc, ii: lambda h: h.dma_start(out=t[ii][:, :], in_=dsrc[rows, :]))(t, dsrc, ii),
                          slot=r(f"{nm}{ii}"), reads=[rs[gt]], writes=[r(f"{nm}{ii}")])
                S.dma(S.sp, lambda h: h.dma_start(out=bcs[i][:, :, :].rearrange("p a t -> p (a t)"), in_=bct[gt]),
                      slot=r(f"bcs{i}"), reads=[r_bct[gt]], writes=[r(f"bcs{i}")])
                x3 = kk % 4
                S.dma(S.sp, lambda h: h.dma_start(out=xb3[x3][:, :], in_=src[rows, :]),
                      slot=r(f"xb{x3}"), reads=[src_res[gt]], writes=[r(f"xb{x3}")])
                return i

            def b_stage12(gt, i, first, kk):
                y3 = kk % 3
                Y, Z, XD, Bk, BC, SC = yin[y3], zin[y3], xdb[i], btk[i], bcs[i], scb[i]
                rY, rZ, rXD, rB, rBC, rSC = (r(f"yin{y3}"), r(f"zin{y3}"), r(f"xdb{i}"), r(f"btk{i}"), r(f"bcs{i}"), r(f"scb{i}"))
                YN, rYN = yn2[kk % 2], r(f"yn{kk % 2}")
                if not first:
                    for hf in range(2):
                        for gg in range(4):
                            g = hf * 4 + gg
                            b = 2 * hf + gg // 2
                            S.op(S.pe, (lambda g, b, gg: lambda h: h.matmul(pp[:, b, (gg % 2) * 256:(gg % 2 + 1) * 256], BC[:, 8 + g, :],
                                                                            Sbb[:, g * 256:(g + 1) * 256], start=True, stop=True))(g, b, gg),
                                 reads=[rBC, r("Sbb")], writes=[r_b[b]])
                for g in range(8):
                    b = 4 + g // 2
                    S.op(S.pe, (lambda g, b: lambda h: h.matmul(pp[:, b, (g % 2) * 256:(g % 2 + 1) * 256], Bk[:, g * 128:(g + 1) * 128],
                                                                XD[:, g * 256:(g + 1) * 256], start=True, stop=True))(g, b),
                         reads=[rB, rXD], writes=[r_b[b]])
                if first:
                    S.op(S.dve, lambda h: h.tensor_copy(Sb[:, :].rearrange("p (b c) -> p b c", b=4), pp[:, 4:8, :]),
                         reads=[r_b[4], r_b[5], r_b[6], r_b[7]], writes=[r("Sb")])
                else:
                    S.op(S.dve, lambda h: h.tensor_tensor(tmpS[:, :].rearrange("p (a d) -> p a d", a=32), Sb[:, :].rearrange("p (a d) -> p a d", a=32),
                                                          bc3(SC[:, 32:64], 64), ALU.mult),
                         reads=[r("Sb"), rSC], writes=[r("tmpS")])
                    S.op(S.dve, lambda h: h.tensor_tensor(Sb[:, :].rearrange("p (b c) -> p b c", b=4), tmpS[:, :].rearrange("p (b c) -> p b c", b=4),
                                                          pp[:, 4:8, :], ALU.add),
                         reads=[r("tmpS"), r_b[4], r_b[5], r_b[6], r_b[7]], writes=[r("Sb")])
                S.op(S.act, lambda h: h.activation(Sbb[:, :], Sb[:, :], AF.Copy), reads=[r("Sb")], writes=[r("Sbb")])
                if not first:
                    S.op(S.dve, lambda h: h.tensor_tensor(tmp[:, :].rearrange("p (a d) -> p a d", a=32),
                                                          pp[:, 0:4, :].rearrange("p b (a d) -> p (b a) d", d=64),
                                                          bc3(SC[:, 0:32], 64), ALU.mult),
                         reads=[r_b[0], r_b[1], r_b[2], r_b[3], rSC], writes=[r("tmp")])
                    S.op(S.pool, lambda h: h.tensor_tensor(Y[:, :], Y[:, :], tmp[:, :], ALU.add), reads=[rY, r("tmp")], writes=[rY])

            def b_gate(gt, i, first, kk):
                y3 = kk % 3
                Y, Z = yin[y3], zin[y3]
                rY, rZ = r(f"yin{y3}"), r(f"zin{y3}")
                YN, rYN = yn2[kk % 2], r(f"yn{kk % 2}")
                S.op(S.act, lambda h: h.activation(Z[:, :], Z[:, :], AF.Silu), reads=[rZ], writes=[rZ])
                S.op(S.dve, lambda h: h.tensor_tensor(Y[:, :], Y[:, :], Z[:, :], ALU.mult), reads=[rY, rZ], writes=[rY])
                for g in range(8):
                    S.op(S.act, (lambda g: lambda h: h.activation(Z[:, g * 256:(g + 1) * 256], Y[:, g * 256:(g + 1) * 256], AF.Square,
                                                                  accum_out=small[:, g:g + 1]))(g),
                         reads=[rY, rZ], writes=[rZ, r(f"gss{g}")])
                S.op(S.pool, lambda h: h.tensor_scalar(small[:, 8:16], small[:, 0:8], 1.0 / 256, EPS, ALU.mult, ALU.add),
                     reads=[r(f"gss{g}") for g in range(8)], writes=[r("gv")])
                S.op(S.pool, lambda h: h.tensor_tensor(small[:, 16:24], small[:, 8:16], cx.mhalf[:, 0:1].broadcast_to([128, 8]), ALU.pow),
                     reads=[r("gv")], writes=[r("grstd")])
                for g in range(8):
                    S.op(S.dve, (lambda g: lambda h: h.scalar_tensor_tensor(YN[:, g * 256:(g + 1) * 256], Y[:, g * 256:(g + 1) * 256],
                                                                            small[:, 16 + g:17 + g], ngb[:, g * 256:(g + 1) * 256],
                                                                            ALU.mult, ALU.mult))(g),
                         reads=[rY, r("grstd"), r("ngb")], writes=[rYN])

            def b_stage3(gt, kk):
                rows = slice(gt * 128, (gt + 1) * 128)
                YN, rYN = yn2[kk % 2], r(f"yn{kk % 2}")
                XB, rXB = xb3[kk % 4], r(f"xb{kk % 4}")
                for hf in range(2):
                    pst = pp[:, hf, :].bitcast(BF16)
                    for u in range(8):
                        k = hf * 8 + u
                        S.op(S.pe, (lambda pst, u, k: lambda h: h.transpose(pst[:, u * 128:(u + 1) * 128], YN[:, k * 128:(k + 1) * 128], cx.ident[:, :]))(pst, u, k),
                             reads=[rYN], writes=[r_b[hf]])
                    S.op(S.act, (lambda pst, hf: lambda h: h.activation(ynT[:, hf * 8:(hf + 1) * 8, :], pst.rearrange("p (k c) -> p k c", k=8), AF.Copy))(pst, hf),
                         reads=[r_b[hf]], writes=[r("ynT")])

            def b_stage3b(gt, kk):
                rows = slice(gt * 128, (gt + 1) * 128)
                XB, rXB = xb3[kk % 4], r(f"xb{kk % 4}")
                for half in range(2):
                    b = 2 + half
                    for k in range(16):
                        S.op(S.pe, (lambda b, k, half: lambda h: h.matmul(pp[:, b, :], ynT[:, k, :], Wout[:, k, half * 512:(half + 1) * 512],
                                                                          start=(k == 0), stop=(k == 15)))(b, k, half),
                             reads=[r("ynT"), r("Wout")], writes=[r_b[b]])
                    S.op(S.act, (lambda b, half: lambda h: h.activation(junk[:, :], pp[:, b, :], AF.Square, accum_out=small[:, 32 + half:33 + half]))(b, half),
                         reads=[r_b[b]], writes=[r("junk"), r(f"ss2{half}")])
                S.op(S.pool, lambda h: h.tensor_tensor(small[:, 34:35], small[:, 32:33], small[:, 33:34], ALU.add),
                     reads=[r("ss20"), r("ss21")], writes=[r("sss")])
                emit_rstd(cx, small[:, 34:35], small[:, 35:36], small[:, 36:37], r("sss"), r("v2"), r("rstd2"), D)
                for half in range(2):
                    b = 2 + half
                    S.op(S.dve, (lambda b, half: lambda h: h.scalar_tensor_tensor(tt[half][:, :], pp[:, b, :], small[:, 36:37],
                                                                                  gpost[:, half * 512:(half + 1) * 512], ALU.mult, ALU.mult))(b, half),
                         reads=[r_b[b], r("rstd2"), r("gpost")], writes=[r(f"tt{half}")])
                    S.op(S.pool, (lambda half: lambda h: h.tensor_tensor(XB[:, half * 512:(half + 1) * 512], tt[half][:, :],
                                                                         XB[:, half * 512:(half + 1) * 512], ALU.add))(half),
                         reads=[r(f"tt{half}"), rXB], writes=[rXB])
                S.dma(S.sp, lambda h: h.dma_start(out=dst[rows, :], in_=XB[:, :]), slot=rXB, reads=[rXB], writes=[dst_res[gt]])

            chunks = []
            for (base, slen) in seq_list:
                ncs = slen // 128
                g0 = base // 128
                for c in range(ncs - 1, -1, -1):
                    chunks.append((g0 + c, c == ncs - 1))
            nchunks = len(chunks)
            slots = {}
            slots[0] = load_chunk(chunks[0][0], 0)
            for kk in range(nchunks + 2):
                if kk + 1 < nchunks:
                    slots[kk + 1] = load_chunk(chunks[kk + 1][0], kk + 1)
                if kk < nchunks:
                    b_stage12(chunks[kk][0], slots[kk], first=chunks[kk][1], kk=kk)
                if 0 <= kk - 2 < nchunks:
                    b_stage3(chunks[kk - 2][0], kk - 2)
                if 0 <= kk - 1 < nchunks:
                    b_gate(chunks[kk - 1][0], slots[kk - 1], first=chunks[kk - 1][1], kk=kk - 1)
                if 0 <= kk - 2 < nchunks:
                    b_stage3b(chunks[kk - 2][0], kk - 2)
            S.barrier()
    _sweep_b()


_NC_CACHE = {}


def run_encoder(xs_per_core, weights, seqs):
    key = tuple(seqs)
    if key not in _NC_CACHE:
        _NC_CACHE[key] = build_program(list(seqs))
    nc = _NC_CACHE[key]
    cst = consts()
    in_maps = []
    for x in xs_per_core:
        m = {"x": np.ascontiguousarray(x, dtype=np.float32)}
        m.update(weights)
        m.update(cst)
        in_maps.append(m)
    res = run_bass_kernel_spmd(nc, in_maps, core_ids=list(range(len(in_maps))))
    return [r["y"] for r in res.results]


_WNAMES = ("norm_g", "ffn_w_gate", "ffn_w_up", "ffn_w_down", "ssd_w_in", "ssd_conv_w", "ssd_conv_b", "ssd_dt_bias",
           "ssd_A_log", "ssd_D", "ssd_norm_g", "ssd_w_out", "attn_w_qkv", "attn_sink", "attn_w_out", "rel_bias")


def kernel(x_prompt, x_sample, **w):
    x_prompt = np.asarray(x_prompt, dtype=np.float32)
    x_sample = np.asarray(x_sample, dtype=np.float32)
    weights = {k: np.ascontiguousarray(np.asarray(w[k], dtype=np.float32)) for k in _WNAMES}
    nb, sp, _ = x_prompt.shape
    _, ss, _ = x_sample.shape
    assert nb == 8 and x_sample.shape[0] == 8
    xs = [np.concatenate([x_prompt[c], x_sample[c]], axis=0) for c in range(nb)]
    ys = run_encoder(xs, weights, (sp, ss))
    y_prompt = np.stack([ys[c][:sp] for c in range(nb)], axis=0)
    y_sample = np.stack([ys[c][sp:] for c in range(nb)], axis=0)
    return (y_prompt, y_sample)
```

```python
import contextlib
import numpy as np
import ml_dtypes
import concourse.bass as bass
import concourse.mybir as mybir
from concourse.bass_utils import run_bass_kernel_spmd

F32 = mybir.dt.float32
BF16 = mybir.dt.bfloat16
AF = mybir.ActivationFunctionType
ALU = mybir.AluOpType
AX = mybir.AxisListType

D = 1024
DFF = 2816
NFC = DFF // 128
EPS = 1e-6
DEPTH = 4
D_INNER = 2048
SSD_HEADS = 32
N_GROUPS = 8
D_STATE = 128
GN = 1024
CONV_DIM = 4096
D_IN_PROJ = 6208
N_HEADS = 16
N_KV = 4
HD = 64
QKV_DIM = 1536
N_BUCKETS = 32
NEG = -30000.0

SAME_ENGINE_SYNC = True


class Res:
    __slots__ = ("name", "last_w", "reads", "dsem", "dcnt")

    def __init__(self, name):
        self.name = name
        self.last_w = None
        self.reads = {}
        self.dsem = None
        self.dcnt = 0


class Eng:
    def __init__(self, name, kind, handle):
        self.name = name
        self.kind = kind
        self.h = handle
        self.sem = None
        self.tick = 0
        self.ops = []
        self.seen = {}


class Sched:
    def __init__(self, nc, stack):
        self.nc = nc
        self.stack = stack
        self.sems = {}
        self.nsem = 0
        self.pe = self._eng("pe", "pe", nc.tensor)
        self.act = self._eng("act", "act", nc.scalar)
        self.dve = self._eng("dve", "dve", nc.vector)
        self.pool = self._eng("pool", "pool", nc.gpsimd)
        self.sp = self._eng("sp", "sp", nc.sync)
        self.engines = [self.pe, self.act, self.dve, self.pool, self.sp]
        self.dma_slots = []
        self.bar_done = {}
        self.free_dsems = []

    def _newsem(self, name):
        s = self.stack.enter_context(self.nc.semaphore(name))
        self.nsem += 1
        sid = self.nsem
        self.sems[sid] = s
        return sid

    def _eng(self, name, kind, handle):
        e = Eng(name, kind, handle)
        e.sem = self._newsem("sem_" + name)
        return e

    def _deps(self, reads, writes):
        deps = {}
        raw = {}

        def add(ev, d):
            if ev is None:
                return
            s, v = ev
            if d.get(s, 0) < v:
                d[s] = v
        for r in reads:
            add(r.last_w, deps)
            add(r.last_w, raw)
        for w in writes:
            add(w.last_w, deps)
            for ev in w.reads.items():
                add(ev, deps)
        self._raw = raw
        return deps

    def _waits(self, eng, deps):
        waits = []
        for s, v in deps.items():
            if s == eng.sem:
                if eng.kind in ("pe", "sp") or not SAME_ENGINE_SYNC:
                    continue
                if eng.kind in ("act", "dve"):
                    v = self._raw.get(s, 0)
                    if v == 0:
                        continue
            if eng.seen.get(s, 0) >= v:
                continue
            eng.seen[s] = v
            waits.append((s, v))
        return waits

    def _commit(self, ev, reads, writes):
        for r in reads:
            if r.reads.get(ev[0], 0) < ev[1]:
                r.reads[ev[0]] = ev[1]
        for w in writes:
            w.last_w = ev
            w.reads = {}

    def op(self, eng, fn, reads=(), writes=()):
        deps = self._deps(reads, writes)
        waits = self._waits(eng, deps)
        eng.tick += 1
        ev = (eng.sem, eng.tick)
        eng.ops.append((waits, fn, eng.sem, 1))
        self._commit(ev, reads, writes)

    def dma(self, eng, fn, slot, reads=(), writes=()):
        if slot.dsem is None:
            if self.free_dsems:
                slot.dsem, slot.dcnt = self.free_dsems.pop()
            else:
                slot.dsem = self._newsem("d%d" % self.nsem)
            self.dma_slots.append(slot)
        deps = self._deps(reads, writes)
        waits = self._waits(eng, deps)
        slot.dcnt += 16
        ev = (slot.dsem, slot.dcnt)
        eng.ops.append((waits, fn, slot.dsem, 16))
        self._commit(ev, reads, writes)

    def barrier(self):
        evs = {}
        for e in self.engines:
            if e.kind != "sp" and e.tick > 0:
                evs[e.sem] = e.tick
        for sl in self.dma_slots:
            if sl.dcnt > self.bar_done.get(sl.dsem, 0):
                evs[sl.dsem] = sl.dcnt
                self.bar_done[sl.dsem] = sl.dcnt
        for e in self.engines:
            waits = []
            for s, v in evs.items():
                if s == e.sem and e.kind in ("pe", "sp"):
                    continue
                if e.seen.get(s, 0) >= v:
                    continue
                e.seen[s] = v
                waits.append((s, v))
            if waits:
                e.ops.append((waits, None, None, 0))
        for sl in self.dma_slots:
            self.free_dsems.append((sl.dsem, sl.dcnt))
            sl.dsem = None
        self.dma_slots = []

    def finish(self, out_res):
        deps = {}
        for r in out_res:
            if r.last_w is not None:
                s, v = r.last_w
                deps[s] = max(deps.get(s, 0), v)
        self.final_waits = list(deps.items())

    def emit(self):
        nc = self.nc
        sems = self.sems
        final_waits = getattr(self, "final_waits", [])

        def replay(e, h):
            for waits, fn, isem, inc in e.ops:
                for s, v in waits:
                    h.wait_ge(sems[s], v)
                if fn is not None:
                    fn(h).then_inc(sems[isem], inc)

        with nc.Block() as block:
            @block.tensor
            def _(h):
                replay(self.pe, h)

            @block.scalar
            def _(h):
                replay(self.act, h)

            @block.vector
            def _(h):
                replay(self.dve, h)

            @block.gpsimd
            def _(h):
                replay(self.pool, h)

            @block.sync
            def _(h):
                replay(self.sp, h)
                for s, v in final_waits:
                    h.wait_ge(sems[s], v)


class Ctx:
    pass


def sb(cx, stack, name, shape, dtype):
    t = stack.enter_context(cx.nc.sbuf_tensor(name, list(shape), dtype))
    return t


def ps(cx, stack, name, shape, dtype):
    t = stack.enter_context(cx.nc.psum_tensor(name, list(shape), dtype))
    return t


def emit_rstd(cx, ss_ap, v_ap, rstd_ap, r_ss, r_v, r_rstd, n):
    S = cx.S
    S.op(S.pool, lambda h: h.tensor_scalar(v_ap, ss_ap, 1.0 / n, EPS, ALU.mult, ALU.add),
         reads=[r_ss], writes=[r_v])
    S.op(S.pool, lambda h: h.tensor_tensor(rstd_ap, v_ap, cx.mhalf[:, 0:1], ALU.pow),
         reads=[r_v], writes=[r_rstd])


def load_bcast_row(cx, eng, dst_tile, dst_res, src_row_ap):
    S = cx.S
    S.dma(eng, lambda h: h.dma_start(out=dst_tile[:, :], in_=src_row_ap.partition_broadcast(128)),
          slot=dst_res, writes=[dst_res])


def ffn_phase(cx, wg, wu, wd, g_pre, g_post, src, dst, src_res, dst_res, ntok, tag):
    S, nc = cx.S, cx.nc
    T = cx.ffn_T
    NS = T // 128
    ntiles = ntok // T
    assert ntok % T == 0
    with contextlib.ExitStack() as st:
        Wg = sb(cx, st, "Wg" + tag, [128, 8, DFF], BF16)
        Wu = sb(cx, st, "Wu" + tag, [128, 8, DFF], BF16)
        Wd = sb(cx, st, "Wd" + tag, [128, NFC, D], BF16)
        gpre = sb(cx, st, "gpre" + tag, [128, D], F32)
        gpost = sb(cx, st, "gpost" + tag, [128, D], F32)
        xa = [sb(cx, st, f"xa{i}" + tag, [128, D], F32) for i in range(2)]
        xb = [sb(cx, st, f"xb{i}" + tag, [128, D], F32) for i in range(2)]
        hb = [sb(cx, st, f"hb{i}" + tag, [128, D], BF16) for i in range(4)]
        hT = sb(cx, st, "hT" + tag, [128, 8, T], BF16)
        actT = sb(cx, st, "actT" + tag, [128, NFC, T], BF16)
        junk = sb(cx, st, "junk" + tag, [128, D], BF16)
        sg = [sb(cx, st, f"sg{i}" + tag, [128, T], BF16) for i in range(2)]
        tt = [sb(cx, st, f"tt{i}" + tag, [128, 512], F32) for i in range(2)]
        small = sb(cx, st, "small" + tag, [128, 64], F32)
        psm = ps(cx, st, "psm" + tag, [128, 7, 512], F32)
        pst2 = [ps(cx, st, "pst" + tag, [128, D], BF16)] * 2

        r_Wg = [Res(f"Wg{k}" + tag) for k in range(8)]
        r_Wu = [Res(f"Wu{k}" + tag) for k in range(8)]
        r_Wd = [Res(f"Wd{k}" + tag) for k in range(NFC)]
        r_gpre, r_gpost = Res("gpre" + tag), Res("gpost" + tag)
        r_xa = [Res(f"xa{i}" + tag) for i in range(2)]
        r_xb = [Res(f"xb{i}" + tag) for i in range(2)]
        r_hb = [Res(f"hb{i}" + tag) for i in range(4)]
        r_hT = [Res(f"hT{s}" + tag) for s in range(NS)]
        r_act = [Res(f"act{f}" + tag) for f in range(NFC)]
        r_junk = Res("junk" + tag)
        r_sg = [Res(f"sg{i}" + tag) for i in range(2)]
        r_tt = [Res(f"tt{i}" + tag) for i in range(2)]
        r_bank = [Res(f"bank{i}" + tag) for i in range(7)]
        r_pst2 = [Res("pst" + tag)] * 2
        r_sm = [Res(f"sm{i}" + tag) for i in range(64)]

        for k in range(8):
            S.dma(S.pool, (lambda k: lambda h: h.dma_start(out=Wg[:, k, :], in_=wg[k * 128:(k + 1) * 128, :]))(k),
                  slot=r_Wg[k], writes=[r_Wg[k]])
            S.dma(S.pool, (lambda k: lambda h: h.dma_start(out=Wu[:, k, :], in_=wu[k * 128:(k + 1) * 128, :]))(k),
                  slot=r_Wu[k], writes=[r_Wu[k]])
        for f in range(NFC):
            S.dma(S.pool, (lambda f: lambda h: h.dma_start(out=Wd[:, f, :], in_=wd[f * 128:(f + 1) * 128, :]))(f),
                  slot=r_Wd[f], writes=[r_Wd[f]])
        load_bcast_row(cx, S.sp, gpre, r_gpre, g_pre)
        load_bcast_row(cx, S.sp, gpost, r_gpost, g_post)
        S.op(S.pool, lambda h: h.tensor_scalar(gpost[:, :], gpost[:, :], 0.5, None, ALU.mult),
             reads=[r_gpost], writes=[r_gpost])

        cnt = {"sub": 0, "gu": 0, "dn": 0, "ep": 0}

        def prologue_pre(t, only=None):
            for s in range(NS):
                if only is not None and s != only:
                    continue
                i = cnt["sub"] % 2
                cnt["sub"] += 1
                hi_ = s % 4
                gt = t * NS + s
                rows = slice(gt * 128, (gt + 1) * 128)
                S.dma(S.sp, (lambda i, rows: lambda h: h.dma_start(out=xa[i][:, :], in_=src[rows, :]))(i, rows),
                      slot=r_xa[i], reads=[src_res[gt]], writes=[r_xa[i]])
                S.op(S.act, (lambda i: lambda h: h.activation(junk[:, :], xa[i][:, :], AF.Square,
                                                              accum_out=small[:, i:i + 1]))(i),
                     reads=[r_xa[i]], writes=[r_junk, r_sm[i]])
                emit_rstd(cx, small[:, i:i + 1], small[:, 2 + i:3 + i], small[:, 4 + i:5 + i],
                          r_sm[i], r_sm[2 + i], r_sm[4 + i], D)
                S.op(S.dve, (lambda i, hi_: lambda h: h.scalar_tensor_tensor(
                    hb[hi_][:, :], xa[i][:, :], small[:, 4 + i:5 + i], gpre[:, :], ALU.mult, ALU.mult))(i, hi_),
                    reads=[r_xa[i], r_sm[4 + i], r_gpre], writes=[r_hb[hi_]])

        def prologue_post(t):
            for s in range(NS):
                hi_ = s % 4
                if s == 0:
                    pst, r_pst = pst2[0], r_pst2[0]
                else:
                    pst, r_pst = psm[:, 3 + s, :].bitcast(BF16), r_bank[3 + s]
                for k in range(8):
                    S.op(S.pe, (lambda hi_, k, pst: lambda h: h.transpose(
                        pst[:, k * 128:(k + 1) * 128], hb[hi_][:, k * 128:(k + 1) * 128], cx.ident[:, :]))(hi_, k, pst),
                        reads=[r_hb[hi_]], writes=[r_pst])
                S.op(S.act, (lambda s, pst: lambda h: h.activation(
                    hT[:, :, s * 128:(s + 1) * 128], pst.rearrange("p (k c) -> p k c", k=8), AF.Copy))(s, pst),
                    reads=[r_pst], writes=[r_hT[s]])

        def main(t, hook=None):
            for f in range(NFC):
                if hook is not None and f in (3, 7, 11, 15):
                    hook((f - 3) // 4)
                q = cnt["gu"] % 2
                cnt["gu"] += 1
                bg, bu = 2 * q, 2 * q + 1
                for (W, rW, b) in ((Wg, r_Wg, bg), (Wu, r_Wu, bu)):
                    for k in range(8):
                        S.op(S.pe, (lambda W, b, k, f: lambda h: h.matmul(
                            psm[:, b, 0:T], W[:, k, f * 128:(f + 1) * 128], hT[:, k, :],
                            start=(k == 0), stop=(k == 7)))(W, b, k, f),
                            reads=[rW[k]] + r_hT, writes=[r_bank[b]])
                S.op(S.act, (lambda q, bg: lambda h: h.activation(sg[q][:, :], psm[:, bg, 0:T], AF.Silu))(q, bg),
                     reads=[r_bank[bg]], writes=[r_sg[q]])
                S.op(S.dve, (lambda q, bu, f: lambda h: h.tensor_tensor(
                    actT[:, f, :], sg[q][:, :], psm[:, bu, 0:T], ALU.mult))(q, bu, f),
                    reads=[r_sg[q], r_bank[bu]], writes=[r_act[f]])

        def down(t):
            for s in range(NS):
                gt = t * NS + s
                rows = slice(gt * 128, (gt + 1) * 128)
                j = cnt["ep"] % 2
                cnt["ep"] += 1
                S.dma(S.sp, (lambda j, rows: lambda h: h.dma_start(out=xb[j][:, :], in_=src[rows, :]))(j, rows),
                      slot=r_xb[j], reads=[src_res[gt]], writes=[r_xb[j]])
                banks = []
                for half in range(2):
                    b = 4 + cnt["dn"] % 3
                    cnt["dn"] += 1
                    banks.append(b)
                    for f in range(NFC):
                        S.op(S.pe, (lambda b, f, s, half: lambda h: h.matmul(
                            psm[:, b, :], actT[:, f, s * 128:(s + 1) * 128], Wd[:, f, half * 512:(half + 1) * 512],
                            start=(f == 0), stop=(f == NFC - 1)))(b, f, s, half),
                            reads=[r_act[f], r_Wd[f]], writes=[r_bank[b]])
                    c = 8 + 2 * j + half
                    S.op(S.act, (lambda b, c: lambda h: h.activation(junk[:, 0:512], psm[:, b, :], AF.Square,
                                                                     accum_out=small[:, c:c + 1]))(b, c),
                         reads=[r_bank[b]], writes=[r_junk, r_sm[c]])
                c0 = 8 + 2 * j
                S.op(S.pool, (lambda c0, j: lambda h: h.tensor_tensor(
                    small[:, 16 + j:17 + j], small[:, c0:c0 + 1], small[:, c0 + 1:c0 + 2], ALU.add))(c0, j),
                    reads=[r_sm[c0], r_sm[c0 + 1]], writes=[r_sm[16 + j]])
                emit_rstd(cx, small[:, 16 + j:17 + j], small[:, 12 + j:13 + j], small[:, 14 + j:15 + j],
                          r_sm[16 + j], r_sm[12 + j], r_sm[14 + j], D)
                for half in range(2):
                    b = banks[half]
                    u = half
                    S.op(S.dve, (lambda b, u, j, half: lambda h: h.scalar_tensor_tensor(
                        tt[u][:, :], psm[:, b, :], small[:, 14 + j:15 + j], gpost[:, half * 512:(half + 1) * 512],
                        ALU.mult, ALU.mult))(b, u, j, half),
                        reads=[r_bank[b], r_sm[14 + j], r_gpost], writes=[r_tt[u]])
                    S.op(S.pool, (lambda u, j, half: lambda h: h.tensor_tensor(
                        xb[j][:, half * 512:(half + 1) * 512], tt[u][:, :], xb[j][:, half * 512:(half + 1) * 512],
                        ALU.add))(u, j, half),
                        reads=[r_tt[u], r_xb[j]], writes=[r_xb[j]])
                S.dma(S.sp, (lambda j, rows: lambda h: h.dma_start(out=dst[rows, :], in_=xb[j][:, :]))(j, rows),
                      slot=r_xb[j], reads=[r_xb[j]], writes=[dst_res[gt]])

        prologue_pre(0)
        prologue_post(0)
        for t in range(ntiles):
            if t + 1 < ntiles:
                main(t, hook=(lambda s, t=t: prologue_pre(t + 1, only=s)))
                prologue_post(t + 1)
            else:
                main(t)
            down(t)
        S.barrier()


def build_program(seqs, plan=None, ffn_T=512):
    ntok = sum(seqs)
    nc = bass.Bass("TRN2", target_bir_lowering=False)
    cx = Ctx()
    cx.nc = nc
    cx.ffn_T = ffn_T
    cx.seqs = seqs
    cx.scr = {}
    tens = {}

    def dram(name, shape, kind="ExternalInput", dtype=F32):
        if name not in tens:
            tens[name] = nc.dram_tensor(name, list(shape), dtype, kind=kind).ap()
        return tens[name]
    cx.dram = dram
    if plan is None:
        plan = []
        for i in range(DEPTH):
            plan.append(("ffn", i, 0))
            plan.append(("ssd", i // 2) if i % 2 == 0 else ("attn", i // 2))
            plan[-1] = plan[-1] + (i,)
            plan.append(("ffn", i, 1))
    x_in = dram("x", [ntok, D])
    y_out = dram("y", [ntok, D], "ExternalOutput")
    norm_g = dram("norm_g", [DEPTH, 6, D])
    c_ident = dram("c_ident", [128, 128])
    xres = dram("xres", [ntok, D], "Internal")
    nt128 = ntok // 128
    r_x = [Res(f"xin{t}") for t in range(nt128)]
    r_xres = [Res(f"xres{t}") for t in range(nt128)]
    r_y = [Res(f"y{t}") for t in range(nt128)]
    seq_list = []
    b0 = 0
    for sl in seqs:
        seq_list.append((b0, sl))
        b0 += sl

    with contextlib.ExitStack() as top:
        S = Sched(nc, top)
        cx.S = S
        cx.ident = sb(cx, top, "ident", [128, 128], BF16)
        cx.mhalf = sb(cx, top, "mhalf", [128, 1], F32)
        r_ident, r_mhalf = Res("ident"), Res("mhalf")
        S.dma(S.pool, lambda h: h.dma_start(out=cx.ident[:, :], in_=c_ident[:, :]), slot=r_ident, writes=[r_ident])
        S.op(S.pool, lambda h: h.memset(cx.mhalf[:, :], -0.5), writes=[r_mhalf])
        S.barrier()

        for pi, p in enumerate(plan):
            src, src_res = (x_in, r_x) if pi == 0 else (xres, r_xres)
            dst, dst_res = (y_out, r_y) if pi == len(plan) - 1 else (xres, r_xres)
            tag = f"_p{pi}"
            if p[0] == "ffn":
                _, i, w = p
                wg = dram("ffn_w_gate", [DEPTH, 2, D, DFF])
                wu = dram("ffn_w_up", [DEPTH, 2, D, DFF])
                wd = dram("ffn_w_down", [DEPTH, 2, DFF, D])
                ffn_phase(cx, wg[i, w], wu[i, w], wd[i, w], norm_g[i, 4 * w], norm_g[i, 4 * w + 1],
                          src, dst, src_res, dst_res, ntok, tag)
            elif p[0] == "attn":
                _, j, i = p
                attn_phase(cx, dram("attn_w_qkv", [2, D, QKV_DIM])[j], dram("attn_w_out", [2, D, D])[j],
                           dram("attn_sink", [2, N_HEADS])[j], norm_g[i, 2], norm_g[i, 3],
                           src, dst, src_res, dst_res, seq_list, tag)
            elif p[0] == "ssd":
                _, j, i = p
                ssd_phase(cx, j, norm_g[i, 2], norm_g[i, 3], src, dst, src_res, dst_res, seq_list, tag)
        S.finish(r_y)
        S.emit()
    return nc


def _t5_bucket_np(rel):
    half, max_exact = 16, 8
    ret = np.where(rel > 0, half, 0)
    n = np.abs(rel)
    large = max_exact + (np.log(np.maximum(n, 1).astype(np.float32) / np.float32(max_exact))
                         / np.float32(np.log(128 / max_exact)) * np.float32(half - max_exact)).astype(np.int32)
    large = np.minimum(large, half - 1)
    return ret + np.where(n < max_exact, n, large)


def consts():
    i = np.arange(512)
    rel = i - 255
    inwin = np.abs(rel) <= 128
    bucket = _t5_bucket_np(rel)
    oh = np.zeros((33, 512), np.float32)
    oh[bucket[inwin], i[inwin]] = 1.0
    oh[32, ~inwin] = 1.0
    u = np.arange(128)[:, None]
    l = np.arange(128)[None, :]
    tri_f = (u <= l).astype(np.float32)
    tri_b = (u >= l).astype(np.float32)
    nm_f = np.where(l >= u, 0.0, NEG).astype(np.float32)
    nm_b = np.where(l <= u, 0.0, NEG).astype(np.float32)
    sel2 = np.zeros((128, 64, 128), np.float32)
    for k in range(128):
        sel2[k, k % 64, :] = 1.0
    return {"c_ident": np.eye(128, dtype=np.float32),
            "c_anti": np.ascontiguousarray(np.eye(128, dtype=np.float32)[::-1]),
            "c_onehot": oh,
            "c_tri": np.stack([tri_f, tri_b, nm_f, nm_b]),
            "c_sel2": sel2.reshape(128, 64 * 128).astype(ml_dtypes.bfloat16)}


def bias_setup(cx, rel_bias, c_onehot, c_anti, tvec, tag=""):
    S, nc = cx.S, cx.nc
    with contextlib.ExitStack() as st:
        rb = sb(cx, st, "rb33" + tag, [33, 16], F32)
        oh = sb(cx, st, "oh33" + tag, [33, 512], F32)
        anti = sb(cx, st, "anti" + tag, [128, 128], F32)
        tv = sb(cx, st, "tv" + tag, [128, 4, 16], F32)
        hk = sb(cx, st, "hankel" + tag, [128, 384 * 16], F32)
        pp = ps(cx, st, "ps_bias" + tag, [128, 8, 512], F32)
        r_rb, r_oh, r_anti, r_tv, r_hk = Res("rb"), Res("oh"), Res("anti" + tag), Res("tv" + tag), Res("hk")
        r_tvec = Res("tvec")
        r_b = [Res(f"bb{i}") for i in range(8)]
        S.op(S.pool, lambda h: h.memset(rb[:, :], NEG), writes=[r_rb])
        S.dma(S.sp, lambda h: h.dma_start(out=rb[0:32, :], in_=rel_bias[:, :]), slot=r_rb, writes=[r_rb])
        S.dma(S.sp, lambda h: h.dma_start(out=oh[:, :], in_=c_onehot[:, :]), slot=r_oh, writes=[r_oh])
        S.dma(S.sp, lambda h: h.dma_start(out=anti[:, :], in_=c_anti[:, :]), slot=r_anti, writes=[r_anti])
        for c in range(4):
            S.op(S.pe, (lambda c: lambda h: h.matmul(pp[:, 0, c * 16:(c + 1) * 16], oh[:, c * 128:(c + 1) * 128],
                                                     rb[:, :], start=True, stop=True))(c),
                 reads=[r_rb, r_oh], writes=[r_b[0]])
        S.op(S.dve, lambda h: h.tensor_copy(tv[:, :, :], pp[:, 0, 0:64].rearrange("p (c h) -> p c h", c=4)),
             reads=[r_b[0]], writes=[r_tv])
        S.dma(S.sp, lambda h: h.dma_start(out=tvec.rearrange("(c p) h -> p c h", p=128), in_=tv[:, :, :]),
              slot=r_tv, reads=[r_tv], writes=[r_tvec])
        hank_src = bass.AP(tvec.tensor, 0, [[16, 128], [1, 384 * 16]])
        S.dma(S.sp, lambda h: h.dma_start(out=hk[:, :], in_=hank_src), slot=r_hk, reads=[r_tvec], writes=[r_hk])
        for m in range(12):
            b = 1 + m % 4
            S.op(S.pe, (lambda m, b: lambda h: h.matmul(pp[:, b, :], anti[:, :], hk[:, m * 512:(m + 1) * 512],
                                                        start=True, stop=True))(m, b),
                 reads=[r_anti, r_hk], writes=[r_b[b]])
            S.op(S.dve, (lambda m, b: lambda h: h.tensor_copy(
                cx.biasH[:, :, m * 32:(m + 1) * 32], pp[:, b, :].rearrange("p (j h) -> p h j", h=16)))(m, b),
                reads=[r_b[b]], writes=[cx.r_biasH])
        S.barrier()


def attn_phase(cx, wqkv, wout, sink, g_pre, g_post, src, dst, src_res, dst_res, seq_list, tag):
    S, nc = cx.S, cx.nc
    with contextlib.ExitStack() as st:
        cx.biasH = sb(cx, st, "biasH" + tag, [128, 16, 384], F32)
        cx.r_biasH = Res("biasH" + tag)
        bias_setup(cx, cx.dram("rel_bias", [N_BUCKETS, N_HEADS]), cx.dram("c_onehot", [33, 512]),
                   cx.dram("c_anti", [128, 128]), cx.dram("tvec", [512, 16], "Internal"), tag)
        Wqkv = sb(cx, st, "Wqkv" + tag, [128, 8, 1280 + 512], BF16)
        Wo = sb(cx, st, "Wo" + tag, [128, 8, D], BF16)
        gpre = sb(cx, st, "agpre" + tag, [128, D], F32)
        gpost = sb(cx, st, "agpost" + tag, [128, D], F32)
        sinkb = sb(cx, st, "sinkb" + tag, [128, 16], F32)
        xa = [sb(cx, st, f"axa{i}" + tag, [128, D], F32) for i in range(2)]
        xb = [sb(cx, st, f"axb{i}" + tag, [128, D], F32) for i in range(2)]
        hb = [sb(cx, st, f"ahb{i}" + tag, [128, D], BF16) for i in range(2)]
        hT = [sb(cx, st, f"ahT{i}" + tag, [128, 8, 128], BF16) for i in range(2)]
        qT = [sb(cx, st, f"qT{i}" + tag, [128, 8, 128], BF16) for i in range(3)]
        klo = sb(cx, st, "klo" + tag, [128, 4, 4, 128], BF16)
        khi = sb(cx, st, "khi" + tag, [128, 4, 4, 128], BF16)
        vv = sb(cx, st, "vv" + tag, [128, 4, 256], BF16)
        Ssb = sb(cx, st, "Ssb" + tag, [128, 16, 384], F32)
        Pb2 = [sb(cx, st, f"Pb{i}" + tag, [128, 16, 384], BF16) for i in range(2)]
        PT = sb(cx, st, "PT" + tag, [128, 16, 3, 128], BF16)
        osb = sb(cx, st, "osb" + tag, [128, D], BF16)
        oT = sb(cx, st, "oT" + tag, [128, 8, 128], BF16)
        junk = sb(cx, st, "ajunk" + tag, [128, D], BF16)
        tt = [sb(cx, st, f"att{i}" + tag, [128, 512], F32) for i in range(2)]
        small = sb(cx, st, "asmall" + tag, [128, 192], F32)
        pp = ps(cx, st, "aps" + tag, [128, 8, 512], F32)

        r_W, r_Wo = Res("Wqkv" + tag), Res("Wo" + tag)
        r_gpre, r_gpost, r_sink = Res("agpre" + tag), Res("agpost" + tag), Res("sinkb" + tag)
        r_xa = [Res(f"axa{i}" + tag) for i in range(2)]
        r_xb = [Res(f"axb{i}" + tag) for i in range(2)]
        r_hb = [Res(f"ahb{i}" + tag) for i in range(2)]
        r_hT = [Res(f"ahT{i}" + tag) for i in range(2)]
        r_qT = [Res(f"qT{i}" + tag) for i in range(3)]
        r_k = [Res(f"kslot{i}" + tag) for i in range(4)]
        r_v = [Res(f"vslot{i}" + tag) for i in range(4)]
        r_S, r_PT, r_osb, r_oT, r_junk = (Res("Ssb" + tag), Res("PT" + tag),
                                          Res("osb" + tag), Res("oT" + tag), Res("ajunk" + tag))
        r_P2 = [[Res(f"Pb{i}_{hp}" + tag) for hp in range(8)] for i in range(2)]
        r_PTh = [Res(f"PTh{i}" + tag) for i in range(8)]
        r_Sh = [Res(f"Sh{i}" + tag) for i in range(16)]
        r_tt = [Res(f"att{i}" + tag) for i in range(2)]
        r_b = [Res(f"abank{i}" + tag) for i in range(8)]
        r_sm = {}

        def rsm(name):
            if name not in r_sm:
                r_sm[name] = Res("asm_" + name + tag)
            return r_sm[name]
        C_SS, C_V, C_RSTD = 0, 2, 4
        C_M, C_NEGM, C_ROW, C_ES, C_DEN, C_RDEN = 16, 32, 48, 64, 80, 96
        C_SS2, C_SSS, C_V2, C_RSTD2 = 112, 116, 118, 120

        wq3 = wqkv.rearrange("(k p) f -> p k f", p=128)
        for k in range(8):
            S.dma(S.pool, (lambda k: lambda h: h.dma_start(out=Wqkv[:, k, 0:1024], in_=wqkv[k * 128:(k + 1) * 128, 0:1024]))(k),
                  slot=r_W, writes=[r_W])
            S.dma(S.pool, (lambda k: lambda h: h.dma_start(out=Wqkv[:, k, 1024:1280], in_=wqkv[k * 128:(k + 1) * 128, 1280:1536]))(k),
                  slot=r_W, writes=[r_W])
            for rep in range(2):
                S.dma(S.pool, (lambda k, rep: lambda h: h.dma_start(
                    out=Wqkv[:, k, 1280:1792].rearrange("p (a r d) -> p a r d", a=4, r=2)[:, :, rep, :],
                    in_=wqkv[k * 128:(k + 1) * 128, 1024:1280].rearrange("p (a d) -> p a d", a=4)))(k, rep),
                    slot=r_W, writes=[r_W])
            S.dma(S.pool, (lambda k: lambda h: h.dma_start(out=Wo[:, k, :], in_=wout[k * 128:(k + 1) * 128, :]))(k),
                  slot=r_Wo, writes=[r_Wo])
        load_bcast_row(cx, S.sp, gpre, r_gpre, g_pre)
        load_bcast_row(cx, S.sp, gpost, r_gpost, g_post)
        load_bcast_row(cx, S.sp, sinkb, r_sink, sink)
        S.op(S.pool, lambda h: h.memset(klo[:, :, :, :], 0.0), writes=r_k)
        S.op(S.pool, lambda h: h.memset(khi[:, :, :, :], 0.0), writes=r_k)

        cnt = {"a": 0, "ep": 0}

        def a_pre(base, n):
            i = n % 2
            gt = base // 128 + n
            rows = slice(gt * 128, (gt + 1) * 128)
            S.dma(S.sp, lambda h: h.dma_start(out=xa[i][:, :], in_=src[rows, :]),
                  slot=r_xa[i], reads=[src_res[gt]], writes=[r_xa[i]])
            S.op(S.act, lambda h: h.activation(junk[:, :], xa[i][:, :], AF.Square,
                                               accum_out=small[:, C_SS + i:C_SS + i + 1]),
                 reads=[r_xa[i]], writes=[r_junk, rsm(f"ss{i}")])
            emit_rstd(cx, small[:, C_SS + i:C_SS + i + 1], small[:, C_V + i:C_V + i + 1],
                      small[:, C_RSTD + i:C_RSTD + i + 1], rsm(f"ss{i}"), rsm(f"v{i}"), rsm(f"rstd{i}"), D)
            S.op(S.dve, lambda h: h.scalar_tensor_tensor(
                hb[i][:, :], xa[i][:, :], small[:, C_RSTD + i:C_RSTD + i + 1], gpre[:, :], ALU.mult, ALU.mult),
                reads=[r_xa[i], rsm(f"rstd{i}"), r_gpre], writes=[r_hb[i]])

        def a_main(base, n):
            i = n % 2
            sl = n % 4
            pst = pp[:, 0, :].bitcast(BF16)
            for k in range(8):
                S.op(S.pe, (lambda k: lambda h: h.transpose(
                    pst[:, k * 128:(k + 1) * 128], hb[i][:, k * 128:(k + 1) * 128], cx.ident[:, :]))(k),
                    reads=[r_hb[i]], writes=[r_b[0]])
            S.op(S.dve, lambda h: h.tensor_copy(hT[i][:, :, :], pst.rearrange("p (k c) -> p k c", k=8)),
                 reads=[r_b[0]], writes=[r_hT[i]])
            for c in range(8):
                b = 1 + c // 4
                for k in range(8):
                    S.op(S.pe, (lambda c, k, b: lambda h: h.matmul(
                        pp[:, b, (c % 4) * 128:(c % 4 + 1) * 128], Wqkv[:, k, c * 128:(c + 1) * 128], hT[i][:, k, :],
                        start=(k == 0), stop=(k == 7)))(c, k, b),
                        reads=[r_W, r_hT[i]], writes=[r_b[b]])
            qi = n % 3
            for b in (1, 2):
                S.op(S.dve, (lambda b: lambda h: h.tensor_copy(
                    qT[qi][:, (b - 1) * 4:b * 4, :], pp[:, b, :].rearrange("p (c t) -> p c t", c=4)))(b),
                    reads=[r_b[b]], writes=[r_qT[qi]])
            for a in range(4):
                for k in range(8):
                    S.op(S.pe, (lambda a, k: lambda h: h.matmul(
                        pp[:, 3, a * 128:(a + 1) * 128], Wqkv[:, k, 1280 + a * 128:1280 + (a + 1) * 128], hT[i][:, k, :],
                        start=(k == 0), stop=(k == 7)))(a, k),
                        reads=[r_W, r_hT[i]], writes=[r_b[3]])
            S.op(S.dve, lambda h: h.tensor_copy(klo[0:64, sl, :, :], pp[0:64, 3, :].rearrange("p (a t) -> p a t", a=4)),
                 reads=[r_b[3]], writes=[r_k[sl]])
            S.op(S.dve, lambda h: h.tensor_copy(khi[64:128, sl, :, :], pp[64:128, 3, :].rearrange("p (a t) -> p a t", a=4)),
                 reads=[r_b[3]], writes=[r_k[sl]])
            for k in range(8):
                S.op(S.pe, (lambda k: lambda h: h.matmul(
                    pp[:, 4, 0:256], hT[i][:, k, :], Wqkv[:, k, 1024:1280], start=(k == 0), stop=(k == 7)))(k),
                    reads=[r_W, r_hT[i]], writes=[r_b[4]])
            S.op(S.dve, lambda h: h.tensor_copy(vv[:, sl, :], pp[:, 4, 0:256]),
                 reads=[r_b[4]], writes=[r_v[sl]])

        def keys_of(n, nb):
            jlist = [j for j in range(3) if 0 <= n - 1 + j < nb]
            return jlist, jlist[0] * 128, (jlist[-1] + 1) * 128

        def pv_part(prev, hp):
            n, nb, par = prev
            Pb, r_P = Pb2[par], r_P2[par]
            jlist, _, _ = keys_of(n, nb)
            b = 3 if hp % 2 == 0 else 0
            pb = pp[:, b, :].bitcast(BF16)
            for u in range(2):
                hd = hp * 2 + u
                for j in jlist:
                    S.op(S.pe, (lambda pb, u, j, hd: lambda h: h.transpose(
                        pb[:, (u * 3 + j) * 128:(u * 3 + j + 1) * 128], Pb[:, hd, j * 128:(j + 1) * 128],
                        cx.ident[:, :]))(pb, u, j, hd),
                        reads=[r_P[hp]], writes=[r_b[b]])
            j0, j1 = jlist[0], jlist[-1] + 1
            S.op(S.act, lambda h: h.activation(
                PT[:, hp * 2:hp * 2 + 2, j0:j1, :],
                pb[:, 0:768].rearrange("p (u j t) -> p u j t", u=2, j=3)[:, :, j0:j1, :], AF.Copy),
                reads=[r_b[b]], writes=[r_PTh[hp]])
            for u in range(2):
                hd = hp * 2 + u
                kvh = hd // 4
                bo = 1 + hd // 8
                for idx, j in enumerate(jlist):
                    sl = (n - 1 + j) % 4
                    S.op(S.pe, (lambda hd, bo, j, sl, kvh, idx: lambda h: h.matmul(
                        pp[:, bo, (hd % 8) * 64:(hd % 8 + 1) * 64], PT[:, hd, j, :], vv[:, sl, kvh * 64:(kvh + 1) * 64],
                        start=(idx == 0), stop=(idx == len(jlist) - 1)))(hd, bo, j, sl, kvh, idx),
                        reads=[r_PTh[hp], r_v[sl]], writes=[r_b[bo]])

        def b_scores(n, nb, prev):
            qi = n % 3
            jlist, c0, c1 = keys_of(n, nb)
            for hd in range(16):
                kvh, half, c = hd // 4, hd % 2, hd // 2
                b = 6 + hd % 2
                kt = klo if half == 0 else khi
                for j in jlist:
                    sl = (n - 1 + j) % 4
                    S.op(S.pe, (lambda b, c, j, sl, kvh, kt: lambda h: h.matmul(
                        pp[:, b, j * 128:(j + 1) * 128], qT[qi][:, c, :], kt[:, sl, kvh, :], start=True, stop=True))(
                        b, c, j, sl, kvh, kt),
                        reads=[r_qT[qi], r_k[sl]], writes=[r_b[b]])
                S.op(S.dve, (lambda b, hd: lambda h: h.scalar_tensor_tensor(
                    Ssb[:, hd, c0:c1], pp[:, b, c0:c1], 0.125, cx.biasH[:, hd, c0:c1], ALU.mult, ALU.add))(b, hd),
                    reads=[r_b[b], cx.r_biasH], writes=[r_Sh[hd]])
                S.op(S.dve, (lambda hd: lambda h: h.tensor_reduce(small[:, C_M + hd:C_M + hd + 1], Ssb[:, hd, c0:c1], AX.X, ALU.max))(hd),
                     reads=[r_Sh[hd]], writes=[rsm(f"m{hd}")])
                if prev is not None and hd % 2 == 1:
                    pv_part(prev, hd // 2)

        def b_soft(n, nb, par):
            Pb, r_P = Pb2[par], r_P2[par]
            C_RDEN_ = C_RDEN + 48 * par
            jlist, c0, c1 = keys_of(n, nb)
            sm = lambda c: small[:, c:c + 16]
            S.op(S.dve, lambda h: h.tensor_tensor(sm(C_M), sm(C_M), sinkb[:, :], ALU.max),
                 reads=[rsm(f"m{hd}") for hd in range(16)] + [r_sink], writes=[rsm("m")])
            S.op(S.dve, lambda h: h.tensor_scalar(sm(C_NEGM), sm(C_M), -1.0, None, ALU.mult),
                 reads=[rsm("m")], writes=[rsm("negm")])
            for hd in range(16):
                S.op(S.act, (lambda hd: lambda h: h.activation(
                    Pb[:, hd, c0:c1], Ssb[:, hd, c0:c1], AF.Exp, bias=small[:, C_NEGM + hd:C_NEGM + hd + 1],
                    accum_out=small[:, C_ROW + hd:C_ROW + hd + 1]))(hd),
                    reads=[r_Sh[hd], rsm("negm")], writes=[r_P[hd // 2], rsm(f"row{hd}")])
            S.op(S.dve, lambda h: h.tensor_tensor(sm(C_ES), sinkb[:, :], sm(C_NEGM), ALU.add),
                 reads=[r_sink, rsm("negm")], writes=[rsm("es")])
            S.op(S.act, lambda h: h.activation(sm(C_ES), sm(C_ES), AF.Exp), reads=[rsm("es")], writes=[rsm("es")])

        def b_tail(base, prev):
            n, nb, par = prev
            gt = base // 128 + n
            rows = slice(gt * 128, (gt + 1) * 128)
            C_RDEN_ = C_RDEN + 48 * par
            sm = lambda c: small[:, c:c + 16]
            S.op(S.dve, lambda h: h.tensor_tensor(sm(C_DEN), sm(C_ROW), sm(C_ES), ALU.add),
                 reads=[rsm("es")] + [rsm(f"row{hd}") for hd in range(16)], writes=[rsm("den")])
            S.op(S.dve, lambda h: h.reciprocal(sm(C_RDEN_), sm(C_DEN)), reads=[rsm("den")], writes=[rsm(f"rden{par}")])
            for b in (1, 2):
                S.op(S.dve, (lambda b: lambda h: h.tensor_tensor(
                    osb[:, (b - 1) * 512:b * 512].rearrange("p (a d) -> p a d", a=8),
                    pp[:, b, :].rearrange("p (a d) -> p a d", a=8),
                    small[:, C_RDEN_ + (b - 1) * 8:C_RDEN_ + b * 8].unsqueeze(2).broadcast_to([128, 8, 64]),
                    ALU.mult))(b),
                    reads=[r_b[b], rsm(f"rden{par}")], writes=[r_osb])
            pst = pp[:, 0, :].bitcast(BF16)
            for k in range(8):
                S.op(S.pe, (lambda k: lambda h: h.transpose(
                    pst[:, k * 128:(k + 1) * 128], osb[:, k * 128:(k + 1) * 128], cx.ident[:, :]))(k),
                    reads=[r_osb], writes=[r_b[0]])
            S.op(S.act, lambda h: h.activation(oT[:, :, :], pst.rearrange("p (k c) -> p k c", k=8), AF.Copy),
                 reads=[r_b[0]], writes=[r_oT])
            jx = cnt["ep"] % 2
            cnt["ep"] += 1
            S.dma(S.sp, lambda h: h.dma_start(out=xb[jx][:, :], in_=src[rows, :]),
                  slot=r_xb[jx], reads=[src_res[gt]], writes=[r_xb[jx]])
            for half in range(2):
                b = 4 + half
                for k in range(8):
                    S.op(S.pe, (lambda b, k, half: lambda h: h.matmul(
                        pp[:, b, :], oT[:, k, :], Wo[:, k, half * 512:(half + 1) * 512],
                        start=(k == 0), stop=(k == 7)))(b, k, half),
                        reads=[r_oT, r_Wo], writes=[r_b[b]])
                S.op(S.act, (lambda b, half: lambda h: h.activation(
                    junk[:, 0:512], pp[:, b, :], AF.Square, accum_out=small[:, C_SS2 + half:C_SS2 + half + 1]))(b, half),
                    reads=[r_b[b]], writes=[r_junk, rsm(f"ss2{half}")])
            S.op(S.pool, lambda h: h.tensor_tensor(small[:, C_SSS:C_SSS + 1], small[:, C_SS2:C_SS2 + 1],
                                                   small[:, C_SS2 + 1:C_SS2 + 2], ALU.add),
                 reads=[rsm("ss20"), rsm("ss21")], writes=[rsm("sss")])
            emit_rstd(cx, small[:, C_SSS:C_SSS + 1], small[:, C_V2:C_V2 + 1], small[:, C_RSTD2:C_RSTD2 + 1],
                      rsm("sss"), rsm("v2"), rsm("rstd2"), D)
            for half in range(2):
                b = 4 + half
                S.op(S.dve, (lambda b, half: lambda h: h.scalar_tensor_tensor(
                    tt[half][:, :], pp[:, b, :], small[:, C_RSTD2:C_RSTD2 + 1], gpost[:, half * 512:(half + 1) * 512],
                    ALU.mult, ALU.mult))(b, half),
                    reads=[r_b[b], rsm("rstd2"), r_gpost], writes=[r_tt[half]])
                S.op(S.pool, (lambda half: lambda h: h.tensor_tensor(
                    xb[jx][:, half * 512:(half + 1) * 512], tt[half][:, :], xb[jx][:, half * 512:(half + 1) * 512],
                    ALU.add))(half),
                    reads=[r_tt[half], r_xb[jx]], writes=[r_xb[jx]])
            S.dma(S.sp, lambda h: h.dma_start(out=dst[rows, :], in_=xb[jx][:, :]),
                  slot=r_xb[jx], reads=[r_xb[jx]], writes=[dst_res[gt]])

        kk = 0
        for (base, slen) in seq_list:
            nb = slen // 128
            a_pre(base, 0)
            a_main(base, 0)
            if nb > 1:
                a_pre(base, 1)
                a_main(base, 1)
            if nb > 2:
                a_pre(base, 2)
            prev = None
            for n in range(nb):
                b_scores(n, nb, prev)
                if prev is not None:
                    b_tail(base, prev)
                b_soft(n, nb, kk % 2)
                if n + 2 < nb:
                    a_main(base, n + 2)
                if n + 3 < nb:
                    a_pre(base, n + 3)
                prev = (n, nb, kk % 2)
                kk += 1
            for hp in range(8):
                pv_part(prev, hp)
            b_tail(base, prev)
        S.barrier()


def bc3(ap2, n):
    return ap2.unsqueeze(2).broadcast_to([ap2.shape[0], ap2.shape[1], n])


def ssd_phase(cx, j, g_pre, g_post, src, dst, src_res, dst_res, seq_list, tag):
    S, nc = cx.S, cx.nc
    dram = cx.dram
    ntok = sum(s for _, s in seq_list)
    nch = ntok // 128
    w_in = dram("ssd_w_in", [2, D, D_IN_PROJ])[j]
    conv_w = dram("ssd_conv_w", [2, 5, CONV_DIM])[j]
    conv_b = dram("ssd_conv_b", [2, CONV_DIM])[j]
    dt_bias = dram("ssd_dt_bias", [2, 2, SSD_HEADS])[j]
    A_log = dram("ssd_A_log", [2, 2, SSD_HEADS])[j]
    Dp = dram("ssd_D", [2, SSD_HEADS])[j]
    ng = dram("ssd_norm_g", [2, D_INNER])[j]
    w_out = dram("ssd_w_out", [2, D_INNER, D])[j]
    c_tri = dram("c_tri", [4, 128, 128])
    c_sel = dram("c_sel2", [128, 64 * 128], dtype=BF16)
    c_ident = dram("c_ident", [128, 128])

    def scr(name, shape, dtype=F32):
        key = name
        if key not in cx.scr:
            cx.scr[key] = nc.dram_tensor(name, list(shape), dtype, kind="Internal").ap()
        return cx.scr[key]
    zs = scr("s_z", [ntok, 2048])
    dtr = scr("s_dtr", [ntok, 64])
    xtok = scr("s_xtok", [ntok, 2048], BF16)
    btok = scr("s_btok", [ntok, 1024], BF16)
    bct = scr("s_bct", [nch, 128, 16 * 128], BF16)
    yacc_d = scr("s_yacc", [ntok, 2048])
    xddb_d = scr("s_xddb", [ntok, 2048], BF16)
    scb_d = scr("s_scb", [ntok, 64])
    R = lambda n: [Res(f"{n}{t}{tag}") for t in range(nch)]
    r_zs, r_dtr, r_xtok, r_btok, r_bct, r_yacc, r_xddb, r_scb = (R("zs"), R("dtr"), R("xtok"), R("btok"), R("bct"),
                                                               R("yacc"), R("xddb"), R("scb"))

    def _sweep_f1():
        with contextlib.ExitStack() as st:
            Win = sb(cx, st, "Win" + tag, [128, 8, D_IN_PROJ], BF16)
            acc2 = [sb(cx, st, f"acc{i}" + tag, [128, 32, 128], F32) for i in range(2)]
            tmpA = sb(cx, st, "tmpA" + tag, [128, 32, 128], BF16)
            pc = [sb(cx, st, f"pc{i}" + tag, [128, 32, 132], BF16) for i in range(3)]
            xbcT = sb(cx, st, "xbcT" + tag, [128, 32, 128], BF16)
            xtk = sb(cx, st, "xtk" + tag, [128, 2048], BF16)
            btk = sb(cx, st, "btk" + tag, [128, 1024], BF16)
            zst = [sb(cx, st, f"zst{i}" + tag, [128, 1024], F32) for i in range(2)]
            dtst = sb(cx, st, "dtst" + tag, [128, 64], F32)
            xa2 = [sb(cx, st, f"sxa{i}" + tag, [128, D], F32) for i in range(2)]
            hb2 = [sb(cx, st, f"shb{i}" + tag, [128, D], BF16) for i in range(2)]
            hT = [sb(cx, st, f"shT{i}" + tag, [128, 8, 128], BF16) for i in range(2)]
            gpre = sb(cx, st, "sgpre" + tag, [128, D], F32)
            idf = sb(cx, st, "idf" + tag, [128, 128], F32)
            cw = [sb(cx, st, f"cw{i}" + tag, [128, 128], F32) for i in range(2)]
            wT = sb(cx, st, "wT" + tag, [128, 192], F32)
            wTb = sb(cx, st, "wTb" + tag, [128, 192], BF16)
            small = sb(cx, st, "f1small" + tag, [128, 8], F32)
            pp = ps(cx, st, "f1ps" + tag, [128, 8, 512], F32)
            r_Win, r_diag, r_gpre, r_cbrow, r_ones, r_idf, r_cw, r_wT = (Res("Win" + tag), Res("diag" + tag), Res("sgpre" + tag),
                                                                          Res("cbrow" + tag), Res("onesr" + tag), Res("idf" + tag),
                                                                          [Res("cw0" + tag), Res("cw1" + tag)], Res("wT" + tag))
            r_pc = [Res(f"pc{i}" + tag) for i in range(3)]
            r_xbcT, r_xtk, r_btk, r_dtst = (Res("xbcT" + tag), Res("xtk" + tag), Res("btk" + tag), Res("dtst" + tag))
            r_xa2 = [Res(f"sxa{i}" + tag) for i in range(2)]
            r_hb2 = [Res(f"shb{i}" + tag) for i in range(2)]
            r_acc2 = [Res(f"acc{i}" + tag) for i in range(2)]
            r_tmpA = Res("tmpA" + tag)
            r_zst = [Res(f"zst{i}" + tag) for i in range(2)]
            r_hT = [Res(f"shT{i}" + tag) for i in range(2)]
            r_b = [Res(f"f1b{i}" + tag) for i in range(8)]
            r_sm = [Res(f"f1sm{i}" + tag) for i in range(8)]

            for k in range(8):
                for q in range(4):
                    c0, c1 = q * 1552, (q + 1) * 1552
                    S.dma(S.pool, (lambda k, c0, c1: lambda h: h.dma_start(out=Win[:, k, c0:c1], in_=w_in[k * 128:(k + 1) * 128, c0:c1]))(k, c0, c1),
                          slot=r_Win, writes=[r_Win])
            load_bcast_row(cx, S.sp, gpre, r_gpre, g_pre)
            S.dma(S.sp, lambda h: h.dma_start(out=idf[:, :], in_=c_ident[:, :]), slot=r_idf, writes=[r_idf])
            cwv = conv_w.rearrange("k (c p) -> (k c) p", p=128)
            S.dma(S.sp, lambda h: h.dma_start(out=cw[0][:, :], in_=cwv[0:128, :]), slot=r_cw[0], writes=[r_cw[0]])
            S.dma(S.sp, lambda h: h.dma_start(out=cw[1][0:32, :], in_=cwv[128:160, :]), slot=r_cw[1], writes=[r_cw[1]])
            S.dma(S.sp, lambda h: h.dma_start(out=cw[1][32:64, :], in_=conv_b.rearrange("(c p) -> c p", p=128)), slot=r_cw[1], writes=[r_cw[1]])
            S.op(S.pe, lambda h: h.transpose(pp[:, 0, 0:128], cw[0][:, :], idf[:, :]), reads=[r_cw[0], r_idf], writes=[r_b[0]])
            S.op(S.pe, lambda h: h.transpose(pp[:, 0, 128:192], cw[1][0:64, :], idf[0:64, 0:64]), reads=[r_cw[1], r_idf], writes=[r_b[0]])
            S.op(S.dve, lambda h: h.tensor_copy(wT[:, :], pp[:, 0, 0:192]), reads=[r_b[0]], writes=[r_wT])
            S.op(S.dve, lambda h: h.tensor_copy(wTb[:, :], wT[:, :]), reads=[r_wT], writes=[r_wT])
            for i in range(3):
                S.op(S.pool, (lambda i: lambda h: h.memset(pc[i][:, :, :], 0.0))(i), writes=[r_pc[i]])

            cnt = {"p": 0, "z": 0}

            def project_pre(base, c):
                i = c % 2
                xa, hb, r_xa, r_hb = xa2[i], hb2[i], r_xa2[i], r_hb2[i]
                gt = base // 128 + c
                rows = slice(gt * 128, (gt + 1) * 128)
                S.dma(S.sp, lambda h: h.dma_start(out=xa[:, :], in_=src[rows, :]), slot=r_xa, reads=[src_res[gt]], writes=[r_xa])
                S.op(S.act, lambda h: h.activation(hb[:, :], xa[:, :], AF.Square, accum_out=small[:, 3 * i:3 * i + 1]),
                     reads=[r_xa], writes=[r_hb, r_sm[3 * i]])
                emit_rstd(cx, small[:, 3 * i:3 * i + 1], small[:, 3 * i + 1:3 * i + 2], small[:, 3 * i + 2:3 * i + 3],
                          r_sm[3 * i], r_sm[3 * i + 1], r_sm[3 * i + 2], D)
                S.op(S.dve, lambda h: h.scalar_tensor_tensor(hb[:, :], xa[:, :], small[:, 3 * i + 2:3 * i + 3], gpre[:, :], ALU.mult, ALU.mult),
                     reads=[r_xa, r_sm[3 * i + 2], r_gpre], writes=[r_hb])

            def project_main(base, c, ncs):
                i = c % 2
                hb, r_hb = hb2[i], r_hb2[i]
                gt = base // 128 + c
                rows = slice(gt * 128, (gt + 1) * 128)
                sl = c % 3
                pst = pp[:, 0, :].bitcast(BF16)
                for k in range(8):
                    S.op(S.pe, (lambda k: lambda h: h.transpose(pst[:, k * 128:(k + 1) * 128], hb[:, k * 128:(k + 1) * 128], cx.ident[:, :]))(k),
                         reads=[r_hb], writes=[r_b[0]])
                S.op(S.act, lambda h: h.activation(hT[i][:, :, :], pst.rearrange("p (k c) -> p k c", k=8), AF.Copy),
                     reads=[r_b[0]], writes=[r_hT[i]])
                for q in range(4):
                    for k in range(8):
                        S.op(S.pe, (lambda q, k: lambda h: h.matmul(pp[:, 1 + q, :], hT[i][:, k, :], Win[:, k, q * 512:(q + 1) * 512],
                                                                    start=(k == 0), stop=(k == 7)))(q, k),
                             reads=[r_hT[i], r_Win], writes=[r_b[1 + q]])
                for hf in range(2):
                    zi = cnt["z"] % 2
                    cnt["z"] += 1
                    S.op(S.act, (lambda hf, zi: lambda h: h.activation(zst[zi][:, :].rearrange("p (a b) -> p a b", a=2),
                                                                       pp[:, 1 + 2 * hf:3 + 2 * hf, :], AF.Copy))(hf, zi),
                         reads=[r_b[1 + 2 * hf], r_b[2 + 2 * hf]], writes=[r_zst[zi]])
                    S.dma(S.sp, (lambda hf, zi: lambda h: h.dma_start(out=zs[rows, hf * 1024:(hf + 1) * 1024], in_=zst[zi][:, :]))(hf, zi),
                          slot=r_zst[zi], reads=[r_zst[zi]], writes=[r_zs[gt]])
                for k in range(8):
                    S.op(S.pe, (lambda k: lambda h: h.matmul(pp[:, 5, 0:64], hT[i][:, k, :], Win[:, k, 6144:6208],
                                                             start=(k == 0), stop=(k == 7)))(k),
                         reads=[r_hT[i], r_Win], writes=[r_b[5]])
                S.op(S.act, lambda h: h.activation(dtst[:, :], pp[:, 5, 0:64], AF.Copy), reads=[r_b[5]], writes=[r_dtst])
                S.dma(S.sp, lambda h: h.dma_start(out=dtr[rows, :], in_=dtst[:, :]), slot=r_dtst, reads=[r_dtst], writes=[r_dtr[gt]])
                for c4 in range(8):
                    b = 6 + c4 % 2
                    for u in range(4):
                        cc = c4 * 4 + u
                        for k in range(8):
                            S.op(S.pe, (lambda b, u, cc, k: lambda h: h.matmul(
                                pp[:, b, u * 128:(u + 1) * 128], Win[:, k, 2048 + cc * 128:2048 + (cc + 1) * 128], hT[i][:, k, :],
                                start=(k == 0), stop=(k == 7)))(b, u, cc, k),
                                reads=[r_hT[i], r_Win], writes=[r_b[b]])
                    S.op(S.act, (lambda b, c4: lambda h: h.activation(pc[sl][:, c4 * 4:(c4 + 1) * 4, 2:130],
                                                                      pp[:, b, :].rearrange("p (u t) -> p u t", u=4), AF.Copy))(b, c4),
                         reads=[r_b[b]], writes=[r_pc[sl]])
                if c > 0:
                    pv_ = (c - 1) % 3
                    S.op(S.pool, lambda h: h.tensor_copy(pc[pv_][:, :, 130:132], pc[sl][:, :, 2:4]), reads=[r_pc[sl]], writes=[r_pc[pv_]])
                    S.op(S.pool, lambda h: h.tensor_copy(pc[sl][:, :, 0:2], pc[pv_][:, :, 128:130]), reads=[r_pc[pv_]], writes=[r_pc[sl]])
                else:
                    S.op(S.pool, lambda h: h.memset(pc[sl][:, :, 0:2], 0.0), writes=[r_pc[sl]])
                if c == ncs - 1:
                    S.op(S.pool, lambda h: h.memset(pc[sl][:, :, 130:132], 0.0), writes=[r_pc[sl]])

            wv = lambda k: wT[:, k * 32:(k + 1) * 32].unsqueeze(2).broadcast_to([128, 32, 128])

            def conv_dve(c):
                sl = c % 3
                P, rP = pc[sl], r_pc[sl]
                A, rA = acc2[c % 2], r_acc2[c % 2]
                S.op(S.dve, lambda h: h.tensor_tensor(A[:, :, :], P[:, :, 0:128], wv(0), ALU.mult), reads=[rP, r_wT], writes=[rA])
                for k in (1, 2, 3, 4):
                    S.op(S.dve, (lambda k: lambda h: h.tensor_tensor(tmpA[:, :, :], P[:, :, k:k + 128], wv(k), ALU.mult))(k),
                         reads=[rP, r_wT], writes=[r_tmpA])
                    S.op(S.dve, lambda h: h.tensor_tensor(A[:, :, :], A[:, :, :], tmpA[:, :, :], ALU.add), reads=[rA, r_tmpA], writes=[rA])

            def conv_post(base, c):
                gt = base // 128 + c
                rows = slice(gt * 128, (gt + 1) * 128)
                A, rA = acc2[c % 2], r_acc2[c % 2]
                for cc in range(32):
                    S.op(S.act, (lambda cc: lambda h: h.activation(xbcT[:, cc, :], A[:, cc, :], AF.Silu, bias=wT[:, 160 + cc:161 + cc]))(cc),
                         reads=[rA, r_wT], writes=[r_xbcT])
                for hf in range(2):
                    pst = pp[:, 3 + hf, :].bitcast(BF16)
                    for u in range(8):
                        cc = hf * 8 + u
                        S.op(S.pe, (lambda pst, u, cc: lambda h: h.transpose(pst[:, u * 128:(u + 1) * 128], xbcT[:, cc, :], cx.ident[:, :]))(pst, u, cc),
                             reads=[r_xbcT], writes=[r_b[3 + hf]])
                    S.op(S.act, (lambda pst, hf: lambda h: h.activation(xtk[:, hf * 1024:(hf + 1) * 1024], pst, AF.Copy))(pst, hf),
                         reads=[r_b[3 + hf]], writes=[r_xtk])
                pst = pp[:, 5, :].bitcast(BF16)
                for u in range(8):
                    S.op(S.pe, (lambda u: lambda h: h.transpose(pst[:, u * 128:(u + 1) * 128], xbcT[:, 16 + u, :], cx.ident[:, :]))(u),
                         reads=[r_xbcT], writes=[r_b[5]])
                S.op(S.act, lambda h: h.activation(btk[:, :], pst, AF.Copy), reads=[r_b[5]], writes=[r_btk])
                S.dma(S.sp, lambda h: h.dma_start(out=xtok[rows, :], in_=xtk[:, :]), slot=r_xtk, reads=[r_xtk], writes=[r_xtok[gt]])
                S.dma(S.sp, lambda h: h.dma_start(out=btok[rows, :], in_=btk[:, :]), slot=r_btk, reads=[r_btk], writes=[r_btok[gt]])
                S.dma(S.sp, lambda h: h.dma_start(out=bct[gt], in_=xbcT[:, 16:32, :].rearrange("p a t -> p (a t)")),
                      slot=r_xbcT, reads=[r_xbcT], writes=[r_bct[gt]])

            for (base, slen) in seq_list:
                ncs = slen // 128
                project_pre(base, 0)
                project_main(base, 0, ncs)
                if ncs > 1:
                    project_pre(base, 1)
                for c in range(1, ncs + 1):
                    if c < ncs:
                        project_main(base, c, ncs)
                    if c + 1 < ncs:
                        project_pre(base, c + 1)
                    conv_dve(c - 1)
                    if c >= 2:
                        conv_post(base, c - 2)
                conv_post(base, ncs - 1)
            S.barrier()
    _sweep_f1()

    def _sweep_f2():
        with contextlib.ExitStack() as st:
            sel2 = sb(cx, st, "sel2" + tag, [128, 64, 128], BF16)
            tri = sb(cx, st, "tri" + tag, [128, 2, 128], F32)
            nmask = sb(cx, st, "nmask" + tag, [128, 2, 128], BF16)
            onesf = sb(cx, st, "onesf" + tag, [128, 128], F32)
            dtb = sb(cx, st, "dtb" + tag, [128, 64], F32)
            Abc = sb(cx, st, "Abc" + tag, [128, 64], F32)
            Dbc = sb(cx, st, "Dbc" + tag, [128, 32], F32)
            xtk = [sb(cx, st, f"f2xtk{i}" + tag, [128, 2048], BF16) for i in range(2)]
            btk = [sb(cx, st, f"f2btk{i}" + tag, [128, 1024], BF16) for i in range(2)]
            bcs = [sb(cx, st, f"f2bct{i}" + tag, [128, 16, 128], BF16) for i in range(2)]
            dts = [sb(cx, st, f"f2dt{i}" + tag, [128, 64], F32) for i in range(2)]
            scs = [sb(cx, st, f"f2sc{i}" + tag, [128, 16, 64], F32) for i in range(2)]
            acss = [sb(cx, st, f"f2acs{i}" + tag, [128, 128], F32) for i in range(2)]
            hl = sb(cx, st, "f2hl" + tag, [128, 128], BF16)
            nhl = sb(cx, st, "f2nhl" + tag, [128, 128], BF16)
            hlTs = [sb(cx, st, f"f2hlT{i}" + tag, [128, 128], BF16) for i in range(2)]
            nhlTs = [sb(cx, st, f"f2nhlT{i}" + tag, [128, 128], BF16) for i in range(2)]
            CBTs = [sb(cx, st, f"f2CBT{i}" + tag, [128, 8, 128], BF16) for i in range(2)]
            xdds = [[sb(cx, st, f"f2xdd{j}_{i}" + tag, [128, 2048], BF16) for i in range(2)] for j in range(2)]
            identD = sb(cx, st, "f2identD" + tag, [128, 32, 128], BF16)
            seg = [sb(cx, st, f"f2seg{i}" + tag, [128, 4, 128], BF16) for i in range(4)]
            MT = [sb(cx, st, f"f2MT{i}" + tag, [128, 4, 128], BF16) for i in range(4)]
            Sf = sb(cx, st, "f2Sf" + tag, [128, 2048], F32)
            Sbf = sb(cx, st, "f2Sbf" + tag, [128, 2048], BF16)
            tmp = sb(cx, st, "f2tmp" + tag, [128, 2048], F32)
            yacc = sb(cx, st, "f2yacc" + tag, [128, 2048], F32)
            scbs = [sb(cx, st, f"f2scb{i}" + tag, [128, 64], F32) for i in range(2)]
            pp = ps(cx, st, "f2ps" + tag, [128, 8, 512], F32)
            rr = {}

            def r(name):
                if name not in rr:
                    rr[name] = Res("f2_" + name + tag)
                return rr[name]
            r_b = [r(f"b{i}") for i in range(8)]
            S.dma(S.sp, lambda h: h.dma_start(out=sel2[:, :, :].rearrange("p a b -> p (a b)"), in_=c_sel[:, :]), slot=r("sel2"), writes=[r("sel2")])
            S.dma(S.sp, lambda h: h.dma_start(out=tri[:, 0, :], in_=c_tri[0]), slot=r("tri"), writes=[r("tri")])
            S.dma(S.sp, lambda h: h.dma_start(out=tri[:, 1, :], in_=c_tri[1]), slot=r("tri"), writes=[r("tri")])
            S.dma(S.pool, lambda h: h.dma_start(out=nmask[:, 0, :], in_=c_tri[2]), slot=r("nmask"), writes=[r("nmask")])
            S.dma(S.pool, lambda h: h.dma_start(out=nmask[:, 1, :], in_=c_tri[3]), slot=r("nmask"), writes=[r("nmask")])
            S.op(S.pool, lambda h: h.memset(onesf[:, :], 1.0), writes=[r("onesf")])
            load_bcast_row(cx, S.sp, dtb, r("dtb"), dt_bias.rearrange("a b -> (a b)"))
            load_bcast_row(cx, S.sp, Abc, r("Abc"), A_log.rearrange("a b -> (a b)"))
            load_bcast_row(cx, S.sp, Dbc, r("Dbc"), Dp)
            S.op(S.act, lambda h: h.activation(Abc[:, :], Abc[:, :], AF.Exp), reads=[r("Abc")], writes=[r("Abc")])
            S.op(S.pool, lambda h: h.tensor_scalar(Abc[:, :], Abc[:, :], -1.0, None, ALU.mult), reads=[r("Abc")], writes=[r("Abc")])
            for hd_ in range(32):
                e_ = S.dve
                S.op(e_, (lambda hd_: lambda h: h.tensor_scalar(identD[:, hd_, :], cx.ident[:, :], Dbc[:, hd_:hd_ + 1], None, ALU.mult))(hd_),
                     reads=[r("Dbc")], writes=[r("identD")])
            cnt = {"c": 0, "dk": 0}

            def load_chunk(gt):
                i = cnt["c"] % 2
                cnt["c"] += 1
                rows = slice(gt * 128, (gt + 1) * 128)
                S.dma(S.sp, lambda h: h.dma_start(out=xtk[i][:, :], in_=xtok[rows, :]), slot=r(f"xtk{i}"), reads=[r_xtok[gt]], writes=[r(f"xtk{i}")])
                S.dma(S.sp, lambda h: h.dma_start(out=btk[i][:, :], in_=btok[rows, :]), slot=r(f"btk{i}"), reads=[r_btok[gt]], writes=[r(f"btk{i}")])
                S.dma(S.sp, lambda h: h.dma_start(out=bcs[i][:, :, :].rearrange("p a t -> p (a t)"), in_=bct[gt]),
                      slot=r(f"bcs{i}"), reads=[r_bct[gt]], writes=[r(f"bcs{i}")])
                S.dma(S.sp, lambda h: h.dma_start(out=dts[i][:, :], in_=dtr[rows, :]), slot=r(f"dts{i}"), reads=[r_dtr[gt]], writes=[r(f"dts{i}")])
                return i

            def prep_stage(gt, i, p, stage):
                rows = slice(gt * 128, (gt + 1) * 128)
                X, BC, DT = xtk[i], bcs[i], dts[i]
                rX, rBC, rDT = r(f"xtk{i}"), r(f"bcs{i}"), r(f"dts{i}")
                sc = scs[p]
                acs = acss[p]
                V = lambda k: sc[:, k, :]
                q = lambda n_: r(f"{n_}_{p}")
                if stage == 1:
                    S.op(S.dve, lambda h: h.tensor_tensor(V(0), DT[:, :], dtb[:, :], ALU.add), reads=[rDT, r("dtb")], writes=[q("u")])
                    S.op(S.act, lambda h: h.activation(V(1), V(0), AF.Abs), reads=[q("u")], writes=[q("au")])
                    S.op(S.act, lambda h: h.activation(V(2), V(1), AF.Exp, scale=-1.0), reads=[q("au")], writes=[q("e")])
                    S.op(S.act, lambda h: h.activation(V(3), V(2), AF.Ln, bias=1.0), reads=[q("e")], writes=[q("l")])
                    S.op(S.dve, lambda h: h.scalar_tensor_tensor(V(4), V(0), 0.0, V(3), ALU.max, ALU.add), reads=[q("u"), q("l")], writes=[q("dt")])
                    S.op(S.dve, lambda h: h.tensor_tensor(V(5), V(4), Abc[:, :], ALU.mult), reads=[q("dt"), r("Abc")], writes=[q("a")])
                elif stage == 2:
                    for d in range(2):
                        S.op(S.pe, (lambda d: lambda h: h.matmul(pp[:, 7, d * 32:(d + 1) * 32], tri[:, d, :], sc[:, 5, d * 32:(d + 1) * 32],
                                                                 start=True, stop=True))(d),
                             reads=[r("tri"), q("a")], writes=[r_b[7]])
                    S.op(S.pe, lambda h: h.matmul(pp[:, 7, 64:128], onesf[:, :], V(5), start=True, stop=True),
                         reads=[r("onesf"), q("a")], writes=[r_b[7]])
                elif stage == 3:
                    S.op(S.act, lambda h: h.activation(acs[:, :], pp[:, 7, 0:128], AF.Copy), reads=[r_b[7]], writes=[q("acs")])
                    S.op(S.act, lambda h: h.activation(V(6), acs[:, 0:64], AF.Exp), reads=[q("acs")], writes=[q("eacs")])
                    S.op(S.dve, lambda h: h.tensor_tensor(V(7), acs[:, 64:128], acs[:, 0:64], ALU.subtract), reads=[q("acs")], writes=[q("dd")])
                    S.op(S.act, lambda h: h.activation(V(8), V(7), AF.Exp), reads=[q("dd")], writes=[q("dte")])
                    S.op(S.dve, lambda h: h.tensor_tensor(V(9), V(4), V(8), ALU.mult), reads=[q("dt"), q("dte")], writes=[q("w")])
                    S.op(S.act, lambda h: h.activation(V(10), acs[:, 64:128], AF.Exp), reads=[q("acs")], writes=[q("cdb")])
                    S.op(S.dve, lambda h: h.tensor_copy(hl[:, 0:64], acs[:, 0:64]), reads=[q("acs")], writes=[r("hl")])
                    S.op(S.dve, lambda h: h.tensor_tensor(hl[:, 64:128], acs[:, 0:64], hl[:, 0:64], ALU.subtract), reads=[q("acs"), r("hl")], writes=[r("hl")])
                    S.op(S.act, lambda h: h.activation(V(11), V(4), AF.Ln), reads=[q("dt")], writes=[q("lnd")])
                    S.op(S.dve, lambda h: h.tensor_tensor(V(12), V(11), acs[:, 0:64], ALU.subtract), reads=[q("lnd"), q("acs")], writes=[q("gg")])
                    S.op(S.dve, lambda h: h.tensor_copy(nhl[:, 0:64], V(12)), reads=[q("gg")], writes=[r("nhl")])
                    S.op(S.dve, lambda h: h.tensor_tensor(nhl[:, 64:128], V(12), nhl[:, 0:64], ALU.subtract), reads=[q("gg"), r("nhl")], writes=[r("nhl")])
                elif stage == 4:
                    pst = pp[:, 7, :].bitcast(BF16)
                    S.op(S.pe, lambda h: h.transpose(pst[:, 256:384], hl[:, :], cx.ident[:, :]), reads=[r("hl")], writes=[r_b[7]])
                    S.op(S.pe, lambda h: h.transpose(pst[:, 384:512], nhl[:, :], cx.ident[:, :]), reads=[r("nhl")], writes=[r_b[7]])
                elif stage == 5:
                    pst = pp[:, 7, :].bitcast(BF16)
                    S.op(S.dve, lambda h: h.tensor_copy(hlTs[p][:, :], pst[:, 256:384]), reads=[r_b[7]], writes=[q("hlT")])
                    S.op(S.dve, lambda h: h.tensor_copy(nhlTs[p][:, :], pst[:, 384:512]), reads=[r_b[7]], writes=[q("nhlT")])
                    S.op(S.pool, lambda h: h.tensor_copy(scbs[p][:, 0:32], sc[:, 6, 32:64]), reads=[q("eacs")], writes=[q("scb")])
                    S.op(S.pool, lambda h: h.tensor_copy(scbs[p][:, 32:64], sc[:, 10, 32:64]), reads=[q("cdb"), q("scb")], writes=[q("scb")])
                    S.dma(S.sp, lambda h: h.dma_start(out=scb_d[rows, :], in_=scbs[p][:, :]), slot=q("scb"), reads=[q("scb")], writes=[r_scb[gt]])
                elif stage in (6, 7):
                    hf = stage - 6
                    for gg in range(4):
                        g = hf * 4 + gg
                        S.op(S.pe, (lambda g, gg: lambda h: h.matmul(pp[:, 7, gg * 128:(gg + 1) * 128], BC[:, g, :], BC[:, 8 + g, :],
                                                                     start=True, stop=True))(g, gg),
                             reads=[rBC], writes=[r_b[7]])
                    S.op(S.act, lambda h: h.activation(CBTs[p][:, hf * 4:(hf + 1) * 4, :],
                                                       pp[:, 7, :].rearrange("p (u t) -> p u t", u=4), AF.Copy),
                         reads=[r_b[7]], writes=[q("CBT")])
                elif stage == 8:
                    X3 = X[:, :].rearrange("p (a d) -> p a d", a=32)
                    for d in range(2):
                        e_ = S.dve if d == 0 else S.pool
                        S.op(e_, (lambda d: lambda h: h.tensor_tensor(xdds[p][d][:, :].rearrange("p (a d) -> p a d", a=32), X3,
                                                                      bc3(sc[:, 9, d * 32:(d + 1) * 32], 64), ALU.mult))(d),
                             reads=[rX, q("w")], writes=[q(f"xdd{d}")])
                    S.dma(S.sp, lambda h: h.dma_start(out=xddb_d[rows, :], in_=xdds[p][1][:, :]), slot=q("xdd1"), reads=[q("xdd1")], writes=[r_xddb[gt]])

            PREP_AT = {0: 1, 2: 2, 3: 3, 6: 4, 7: 5, 9: 6, 11: 7, 12: 8}

            def f2_chunk(gt, i, p, first, nxt):
                rows = slice(gt * 128, (gt + 1) * 128)
                X, Bk, BC = xtk[i], btk[i], bcs[i]
                rX, rB, rBC = r(f"xtk{i}"), r(f"btk{i}"), r(f"bcs{i}")
                sc = scs[p]
                q = lambda n_: r(f"{n_}_{p}")
                hlT, nhlT, CBT, xdd = hlTs[p], nhlTs[p], CBTs[p], xdds[p]
                for hd in range(32):
                    S.op(S.pe, (lambda hd: lambda h: h.matmul(pp[:, 2 + hd // 8, (hd % 8) * 64:(hd % 8 + 1) * 64], identD[:, hd, :],
                                                              X[:, hd * 64:(hd + 1) * 64], start=(hd % 8 == 0), stop=False))(hd),
                         reads=[r("identD"), rX], writes=[r_b[2 + hd // 8]])
                dbanks = [0, 1, 6]
                groups = [(d, g) for d in range(2) for g in range(8)]

                def dec(ix):
                    d, g = groups[ix]
                    k = ix % 3
                    b = dbanks[k]
                    for hh in range(4):
                        col = d * 32 + g * 4 + hh
                        o = pp[:, b, hh * 128:(hh + 1) * 128]
                        S.op(S.pe, (lambda o, col: lambda h: h.matmul(o, sel2[:, col, :], hlT[:, :], start=True, stop=False))(o, col),
                             reads=[r("sel2"), q("hlT")], writes=[r_b[b]])
                        S.op(S.pe, (lambda o, col: lambda h: h.matmul(o, nhlT[:, :], sel2[:, col, :], start=False, stop=False))(o, col),
                             reads=[r("sel2"), q("nhlT")], writes=[r_b[b]])
                        S.op(S.pe, (lambda o, d: lambda h: h.matmul(o, cx.ident[:, :], nmask[:, d, :], start=False, stop=True))(o, d),
                             reads=[r("nmask")], writes=[r_b[b]])
                    S.op(S.act, (lambda b, k: lambda h: h.activation(seg[k][:, :, :], pp[:, b, :].rearrange("p (u t) -> p u t", u=4), AF.Exp))(b, k),
                         reads=[r_b[b]], writes=[r(f"seg{k}")])
                    S.op(S.dve, (lambda k, g: lambda h: h.tensor_tensor(MT[k][:, :, :], seg[k][:, :, :],
                                                                         CBT[:, g, :].unsqueeze(1).broadcast_to([128, 4, 128]), ALU.mult))(k, g),
                         reads=[r(f"seg{k}"), q("CBT")], writes=[r(f"MT{k}")])

                def ymm(ix):
                    d, g = groups[ix]
                    k = ix % 3
                    for hh in range(4):
                        hd = g * 4 + hh
                        S.op(S.pe, (lambda k, hh, hd: lambda h: h.matmul(
                            pp[:, 2 + hd // 8, (hd % 8) * 64:(hd % 8 + 1) * 64], MT[k][:, hh, :], X[:, hd * 64:(hd + 1) * 64],
                            start=False, stop=(ix >= 8)))(k, hh, hd),
                            reads=[r(f"MT{k}"), rX], writes=[r_b[2 + hd // 8]])
                for ix in range(2):
                    dec(ix)
                for ix in range(16):
                    if nxt is not None and ix in PREP_AT:
                        prep_stage(nxt[0], nxt[1], nxt[2], PREP_AT[ix])
                    if ix + 2 < 16:
                        dec(ix + 2)
                    ymm(ix)
                for hf in range(2):
                    if not first:
                        for gg in range(4):
                            g = hf * 4 + gg
                            b = 6 + gg // 2
                            S.op(S.pe, (lambda g, b, gg: lambda h: h.matmul(pp[:, b, (gg % 2) * 256:(gg % 2 + 1) * 256], BC[:, 8 + g, :],
                                                                            Sbf[:, g * 256:(g + 1) * 256], start=True, stop=True))(g, b, gg),
                                 reads=[rBC, r("Sbf")], writes=[r_b[b]])
                        S.op(S.dve, (lambda hf: lambda h: h.tensor_tensor(
                            tmp[:, hf * 1024:(hf + 1) * 1024].rearrange("p (a d) -> p a d", a=16),
                            pp[:, 6:8, :].rearrange("p b (a d) -> p (b a) d", d=64),
                            bc3(sc[:, 6, hf * 16:(hf + 1) * 16], 64), ALU.mult))(hf),
                            reads=[r_b[6], r_b[7], q("eacs")], writes=[r("tmp")])
                        S.op(S.dve, (lambda hf: lambda h: h.tensor_tensor(
                            yacc[:, hf * 1024:(hf + 1) * 1024].rearrange("p (b c) -> p b c", b=2),
                            tmp[:, hf * 1024:(hf + 1) * 1024].rearrange("p (b c) -> p b c", b=2),
                            pp[:, 2 + 2 * hf:4 + 2 * hf, :], ALU.add))(hf),
                            reads=[r("tmp"), r_b[2 + 2 * hf], r_b[3 + 2 * hf]], writes=[r("yacc")])
                    else:
                        S.op(S.act, (lambda hf: lambda h: h.activation(
                            yacc[:, hf * 1024:(hf + 1) * 1024].rearrange("p (b c) -> p b c", b=2),
                            pp[:, 2 + 2 * hf:4 + 2 * hf, :], AF.Copy))(hf),
                            reads=[r_b[2 + 2 * hf], r_b[3 + 2 * hf]], writes=[r("yacc")])
                S.dma(S.sp, lambda h: h.dma_start(out=yacc_d[rows, :], in_=yacc[:, :]), slot=r("yacc"), reads=[r("yacc")], writes=[r_yacc[gt]])
                for hf in range(2):
                    for gg in range(4):
                        g = hf * 4 + gg
                        b = 6 + gg // 2
                        S.op(S.pe, (lambda g, b, gg: lambda h: h.matmul(pp[:, b, (gg % 2) * 256:(gg % 2 + 1) * 256], Bk[:, g * 128:(g + 1) * 128],
                                                                        xdd[0][:, g * 256:(g + 1) * 256], start=True, stop=True))(g, b, gg),
                             reads=[rB, q("xdd0")], writes=[r_b[b]])
                    sl_ = slice(hf * 1024, (hf + 1) * 1024)
                    if first:
                        S.op(S.dve, (lambda sl_: lambda h: h.tensor_copy(Sf[:, sl_].rearrange("p (b c) -> p b c", b=2), pp[:, 6:8, :]))(sl_),
                             reads=[r_b[6], r_b[7]], writes=[r("Sf")])
                    else:
                        S.op(S.dve, (lambda sl_, hf: lambda h: h.tensor_tensor(
                            tmp[:, sl_].rearrange("p (a d) -> p a d", a=16), Sf[:, sl_].rearrange("p (a d) -> p a d", a=16),
                            bc3(sc[:, 10, hf * 16:(hf + 1) * 16], 64), ALU.mult))(sl_, hf),
                            reads=[r("Sf"), q("cdb"), r("tmp")], writes=[r("tmp")])
                        S.op(S.dve, (lambda sl_: lambda h: h.tensor_tensor(Sf[:, sl_].rearrange("p (b c) -> p b c", b=2),
                                                                            tmp[:, sl_].rearrange("p (b c) -> p b c", b=2), pp[:, 6:8, :], ALU.add))(sl_),
                             reads=[r("tmp"), r_b[6], r_b[7]], writes=[r("Sf")])
                S.op(S.act, lambda h: h.activation(Sbf[:, :], Sf[:, :], AF.Copy), reads=[r("Sf")], writes=[r("Sbf")])

            kk = 0
            for (base, slen) in seq_list:
                ncs = slen // 128
                g0 = base // 128
                i = load_chunk(g0)
                for stg in range(1, 9):
                    prep_stage(g0, i, kk % 2, stg)
                for c in range(ncs):
                    inext = load_chunk(g0 + c + 1) if c + 1 < ncs else None
                    nxt = (g0 + c + 1, inext, (kk + 1) % 2) if inext is not None else None
                    f2_chunk(g0 + c, i, kk % 2, first=(c == 0), nxt=nxt)
                    i = inext
                    kk += 1
            S.barrier()

    _sweep_f2()

    def _sweep_b():
        with contextlib.ExitStack() as st:
            Wout = sb(cx, st, "Wout" + tag, [128, 16, D], BF16)
            gpost = sb(cx, st, "sgpost" + tag, [128, D], F32)
            ngb = sb(cx, st, "ngb" + tag, [128, 2048], F32)
            yin = [sb(cx, st, f"byacc{i}" + tag, [128, 2048], F32) for i in range(3)]
            zin = [sb(cx, st, f"bz{i}" + tag, [128, 2048], F32) for i in range(3)]
            xdb = [sb(cx, st, f"bxdd{i}" + tag, [128, 2048], BF16) for i in range(2)]
            btk = [sb(cx, st, f"bbtk{i}" + tag, [128, 1024], BF16) for i in range(2)]
            bcs = [sb(cx, st, f"bbct{i}" + tag, [128, 16, 128], BF16) for i in range(2)]
            scb = [sb(cx, st, f"bscb{i}" + tag, [128, 64], F32) for i in range(2)]
            xb3 = [sb(cx, st, f"bxb{i}" + tag, [128, D], F32) for i in range(4)]
            tmpS = sb(cx, st, "btmpS" + tag, [128, 2048], F32)
            Sb = sb(cx, st, "bSb" + tag, [128, 2048], F32)
            Sbb = sb(cx, st, "bSbb" + tag, [128, 2048], BF16)
            tmp = sb(cx, st, "btmp" + tag, [128, 2048], F32)
            yn2 = [sb(cx, st, f"byn{i}" + tag, [128, 2048], BF16) for i in range(2)]
            ynT = sb(cx, st, "bynT" + tag, [128, 16, 128], BF16)
            junk = sb(cx, st, "bjunk" + tag, [128, 512], BF16)
            tt = [sb(cx, st, f"btt{i}" + tag, [128, 512], F32) for i in range(2)]
            small = sb(cx, st, "bsmall" + tag, [128, 64], F32)
            pp = ps(cx, st, "bps" + tag, [128, 8, 512], F32)
            rr = {}

            def r(name):
                if name not in rr:
                    rr[name] = Res("b_" + name + tag)
                return rr[name]
            r_b = [r(f"b{i}") for i in range(8)]
            for k in range(16):
                S.dma(S.pool, (lambda k: lambda h: h.dma_start(out=Wout[:, k, :], in_=w_out[k * 128:(k + 1) * 128, :]))(k),
                      slot=r("Wout"), writes=[r("Wout")])
            load_bcast_row(cx, S.sp, gpost, r("gpost"), g_post)
            load_bcast_row(cx, S.sp, ngb, r("ngb"), ng)
            cnt = {"c": 0}

            def load_chunk(gt, kk):
                i = cnt["c"] % 2
                cnt["c"] += 1
                rows = slice(gt * 128, (gt + 1) * 128)
                y3 = kk % 3
                for (t, dsrc, rs, nm, ii) in ((yin, yacc_d, r_yacc, "yin", y3), (zin, zs, r_zs, "zin", y3), (xdb, xddb_d, r_xddb, "xdb", i),
                                              (btk, btok, r_btok, "btk", i), (scb, scb_d, r_scb, "scb", i)):
                    S.dma(S.sp, (lambda t, dsrc, ii: lambda h: h.dma_start(out=t[ii][:, :], in_=dsrc[rows, :]))(t, dsrc, ii),
                          slot=r(f"{nm}{ii}"), reads=[rs[gt]], writes=[r(f"{nm}{ii}")])
                S.dma(S.sp, lambda h: h.dma_start(out=bcs[i][:, :, :].rearrange("p a t -> p (a t)"), in_=bct[gt]),
                      slot=r(f"bcs{i}"), reads=[r_bct[gt]], writes=[r(f"bcs{i}")])
                x3 = kk % 4
                S.dma(S.sp, lambda h: h.dma_start(out=xb3[x3][:, :], in_=src[rows, :]),
                      slot=r(f"xb{x3}"), reads=[src_res[gt]], writes=[r(f"xb{x3}")])
                return i

            def b_stage12(gt, i, first, kk):
                y3 = kk % 3
                Y, Z, XD, Bk, BC, SC = yin[y3], zin[y3], xdb[i], btk[i], bcs[i], scb[i]
                rY, rZ, rXD, rB, rBC, rSC = (r(f"yin{y3}"), r(f"zin{y3}"), r(f"xdb{i}"), r(f"btk{i}"), r(f"bcs{i}"), r(f"scb{i}"))
                YN, rYN = yn2[kk % 2], r(f"yn{kk % 2}")
                if not first:
                    for hf in range(2):
                        for gg in range(4):
                            g = hf * 4 + gg
                            b = 2 * hf + gg // 2
                            S.op(S.pe, (lambda g, b, gg: lambda h: h.matmul(pp[:, b, (gg % 2) * 256:(gg % 2 + 1) * 256], BC[:, 8 + g, :],
                                                                            Sbb[:, g * 256:(g + 1) * 256], start=True, stop=True))(g, b, gg),
                                 reads=[rBC, r("Sbb")], writes=[r_b[b]])
                for g in range(8):
                    b = 4 + g // 2
                    S.op(S.pe, (lambda g, b: lambda h: h.matmul(pp[:, b, (g % 2) * 256:(g % 2 + 1) * 256], Bk[:, g * 128:(g + 1) * 128],
                                                                XD[:, g * 256:(g + 1) * 256], start=True, stop=True))(g, b),
                         reads=[rB, rXD], writes=[r_b[b]])
                if first:
                    S.op(S.dve, lambda h: h.tensor_copy(Sb[:, :].rearrange("p (b c) -> p b c", b=4), pp[:, 4:8, :]),
                         reads=[r_b[4], r_b[5], r_b[6], r_b[7]], writes=[r("Sb")])
                else:
                    S.op(S.dve, lambda h: h.tensor_tensor(tmpS[:, :].rearrange("p (a d) -> p a d", a=32), Sb[:, :].rearrange("p (a d) -> p a d", a=32),
                                                          bc3(SC[:, 32:64], 64), ALU.mult),
                         reads=[r("Sb"), rSC], writes=[r("tmpS")])
                    S.op(S.dve, lambda h: h.tensor_tensor(Sb[:, :].rearrange("p (b c) -> p b c", b=4), tmpS[:, :].rearrange("p (b c) -> p b c", b=4),
                                                          pp[:, 4:8, :], ALU.add),
                         reads=[r("tmpS"), r_b[4], r_b[5], r_b[6], r_b[7]], writes=[r("Sb")])
                S.op(S.act, lambda h: h.activation(Sbb[:, :], Sb[:, :], AF.Copy), reads=[r("Sb")], writes=[r("Sbb")])
                if not first:
                    S.op(S.dve, lambda h: h.tensor_tensor(tmp[:, :].rearrange("p (a d) -> p a d", a=32),
                                                          pp[:, 0:4, :].rearrange("p b (a d) -> p (b a) d", d=64),
                                                          bc3(SC[:, 0:32], 64), ALU.mult),
                         reads=[r_b[0], r_b[1], r_b[2], r_b[3], rSC], writes=[r("tmp")])
                    S.op(S.pool, lambda h: h.tensor_tensor(Y[:, :], Y[:, :], tmp[:, :], ALU.add), reads=[rY, r("tmp")], writes=[rY])

            def b_gate(gt, i, first, kk):
                y3 = kk % 3
                Y, Z = yin[y3], zin[y3]
                rY, rZ = r(f"yin{y3}"), r(f"zin{y3}")
                YN, rYN = yn2[kk % 2], r(f"yn{kk % 2}")
                S.op(S.act, lambda h: h.activation(Z[:, :], Z[:, :], AF.Silu), reads=[rZ], writes=[rZ])
                S.op(S.dve, lambda h: h.tensor_tensor(Y[:, :], Y[:, :], Z[:, :], ALU.mult), reads=[rY, rZ], writes=[rY])
                for g in range(8):
                    S.op(S.act, (lambda g: lambda h: h.activation(Z[:, g * 256:(g + 1) * 256], Y[:, g * 256:(g + 1) * 256], AF.Square,
                                                                  accum_out=small[:, g:g + 1]))(g),
                         reads=[rY, rZ], writes=[rZ, r(f"gss{g}")])
                S.op(S.pool, lambda h: h.tensor_scalar(small[:, 8:16], small[:, 0:8], 1.0 / 256, EPS, ALU.mult, ALU.add),
                     reads=[r(f"gss{g}") for g in range(8)], writes=[r("gv")])
                S.op(S.pool, lambda h: h.tensor_tensor(small[:, 16:24], small[:, 8:16], cx.mhalf[:, 0:1].broadcast_to([128, 8]), ALU.pow),
                     reads=[r("gv")], writes=[r("grstd")])
                for g in range(8):
                    S.op(S.dve, (lambda g: lambda h: h.scalar_tensor_tensor(YN[:, g * 256:(g + 1) * 256], Y[:, g * 256:(g + 1) * 256],
                                                                            small[:, 16 + g:17 + g], ngb[:, g * 256:(g + 1) * 256],
                                                                            ALU.mult, ALU.mult))(g),
                         reads=[rY, r("grstd"), r("ngb")], writes=[rYN])

            def b_stage3(gt, kk):
                rows = slice(gt * 128, (gt + 1) * 128)
                YN, rYN = yn2[kk % 2], r(f"yn{kk % 2}")
                XB, rXB = xb3[kk % 4], r(f"xb{kk % 4}")
                for hf in range(2):
                    pst = pp[:, hf, :].bitcast(BF16)
                    for u in range(8):
                        k = hf * 8 + u
                        S.op(S.pe, (lambda pst, u, k: lambda h: h.transpose(pst[:, u * 128:(u + 1) * 128], YN[:, k * 128:(k + 1) * 128], cx.ident[:, :]))(pst, u, k),
                             reads=[rYN], writes=[r_b[hf]])
                    S.op(S.act, (lambda pst, hf: lambda h: h.activation(ynT[:, hf * 8:(hf + 1) * 8, :], pst.rearrange("p (k c) -> p k c", k=8), AF.Copy))(pst, hf),
                         reads=[r_b[hf]], writes=[r("ynT")])

            def b_stage3b(gt, kk):
                rows = slice(gt * 128, (gt + 1) * 128)
                XB, rXB = xb3[kk % 4], r(f"xb{kk % 4}")
                for half in range(2):
                    b = 2 + half
                    for k in range(16):
                        S.op(S.pe, (lambda b, k, half: lambda h: h.matmul(pp[:, b, :], ynT[:, k, :], Wout[:, k, half * 512:(half + 1) * 512],
                                                                          start=(k == 0), stop=(k == 15)))(b, k, half),
                             reads=[r("ynT"), r("Wout")], writes=[r_b[b]])
                    S.op(S.act, (lambda b, half: lambda h: h.activation(junk[:, :], pp[:, b, :], AF.Square, accum_out=small[:, 32 + half:33 + half]))(b, half),
                         reads=[r_b[b]], writes=[r("junk"), r(f"ss2{half}")])
                S.op(S.pool, lambda h: h.tensor_tensor(small[:, 34:35], small[:, 32:33], small[:, 33:34], ALU.add),
                     reads=[r("ss20"), r("ss21")], writes=[r("sss")])
                emit_rstd(cx, small[:, 34:35], small[:, 35:36], small[:, 36:37], r("sss"), r("v2"), r("rstd2"), D)
                for half in range(2):
                    b = 2 + half
                    S.op(S.dve, (lambda b, half: lambda h: h.scalar_tensor_tensor(tt[half][:, :], pp[:, b, :], small[:, 36:37],
                                                                                  gpost[:, half * 512:(half + 1) * 512], ALU.mult, ALU.mult))(b, half),
                         reads=[r_b[b], r("rstd2"), r("gpost")], writes=[r(f"tt{half}")])
                    S.op(S.pool, (lambda half: lambda h: h.tensor_tensor(XB[:, half * 512:(half + 1) * 512], tt[half][:, :],
                                                                         XB[:, half * 512:(half + 1) * 512], ALU.add))(half),
                         reads=[r(f"tt{half}"), rXB], writes=[rXB])
                S.dma(S.sp, lambda h: h.dma_start(out=dst[rows, :], in_=XB[:, :]), slot=rXB, reads=[rXB], writes=[dst_res[gt]])

            chunks = []
            for (base, slen) in seq_list:
                ncs = slen // 128
                g0 = base // 128
                for c in range(ncs - 1, -1, -1):
                    chunks.append((g0 + c, c == ncs - 1))
            nchunks = len(chunks)
            slots = {}
            slots[0] = load_chunk(chunks[0][0], 0)
            for kk in range(nchunks + 2):
                if kk + 1 < nchunks:
                    slots[kk + 1] = load_chunk(chunks[kk + 1][0], kk + 1)
                if kk < nchunks:
                    b_stage12(chunks[kk][0], slots[kk], first=chunks[kk][1], kk=kk)
                if 0 <= kk - 2 < nchunks:
                    b_stage3(chunks[kk - 2][0], kk - 2)
                if 0 <= kk - 1 < nchunks:
                    b_gate(chunks[kk - 1][0], slots[kk - 1], first=chunks[kk - 1][1], kk=kk - 1)
                if 0 <= kk - 2 < nchunks:
                    b_stage3b(chunks[kk - 2][0], kk - 2)
            S.barrier()
    _sweep_b()


_NC_CACHE = {}


def run_encoder(xs_per_core, weights, seqs):
    key = tuple(seqs)
    if key not in _NC_CACHE:
        _NC_CACHE[key] = build_program(list(seqs))
    nc = _NC_CACHE[key]
    cst = consts()
    in_maps = []
    for x in xs_per_core:
        m = {"x": np.ascontiguousarray(x, dtype=np.float32)}
        m.update(weights)
        m.update(cst)
        in_maps.append(m)
    res = run_bass_kernel_spmd(nc, in_maps, core_ids=list(range(len(in_maps))))
    return [r["y"] for r in res.results]


_WNAMES = ("norm_g", "ffn_w_gate", "ffn_w_up", "ffn_w_down", "ssd_w_in", "ssd_conv_w", "ssd_conv_b", "ssd_dt_bias",
           "ssd_A_log", "ssd_D", "ssd_norm_g", "ssd_w_out", "attn_w_qkv", "attn_sink", "attn_w_out", "rel_bias")


def kernel(x_prompt, x_sample, **w):
    x_prompt = np.asarray(x_prompt, dtype=np.float32)
    x_sample = np.asarray(x_sample, dtype=np.float32)
    weights = {k: np.ascontiguousarray(np.asarray(w[k], dtype=np.float32)) for k in _WNAMES}
    nb, sp, _ = x_prompt.shape
    _, ss, _ = x_sample.shape
    assert nb == 8 and x_sample.shape[0] == 8
    xs = [np.concatenate([x_prompt[c], x_sample[c]], axis=0) for c in range(nb)]
    ys = run_encoder(xs, weights, (sp, ss))
    y_prompt = np.stack([ys[c][:sp] for c in range(nb)], axis=0)
    y_sample = np.stack([ys[c][sp:] for c in range(nb)], axis=0)
    return (y_prompt, y_sample)
```
